# Optimizing a Trainium2 kernel written in Bass

```python
import jax, jax.numpy as jnp
from jax import lax
import numpy as np

D_MODEL = 1024
BATCH = 2
SEQ = 8192
DEPTH = 1

GLA_HEADS = 4
GLA_DK = 128
GLA_DV = 256
GLA_RANK = 16
GLA_TAU = 16.0
GLA_CHUNK = 64
GLA_QK = GLA_HEADS * GLA_DK
GLA_V = GLA_HEADS * GLA_DV
SGU_GROUPS = 8
SGU_CH = D_MODEL // SGU_GROUPS
SGU_W = SGU_GROUPS * SGU_CH
SGU_CHUNK = 128
PEER_HEADS = 8
PEER_NKEYS = 128
PEER_NEXP = PEER_NKEYS * PEER_NKEYS
PEER_DQ = 256
PEER_TOPK = 16
PEER_BLOCK = 128
EPS = 1e-6

IN_SIZES = (GLA_QK, GLA_QK, GLA_V, GLA_V, GLA_RANK, SGU_W, SGU_W, D_MODEL, D_MODEL)
IN_WIDTH = sum(IN_SIZES)
IN_SPLITS = [int(s) for s in np.cumsum(IN_SIZES)[:-1]]

kernel_name = 'hybrid_gla_gmlp_peer_adaln'


def rms_norm(x, w):
    xf = x.astype(jnp.float32)
    y = xf * lax.rsqrt(jnp.mean(xf * xf, axis=-1, keepdims=True) + EPS)
    return (y * w.astype(jnp.float32)).astype(x.dtype)


def layer_norm(x, w, b):
    xf = x.astype(jnp.float32)
    mu = jnp.mean(xf, axis=-1, keepdims=True)
    xc = xf - mu
    y = xc * lax.rsqrt(jnp.mean(xc * xc, axis=-1, keepdims=True) + EPS)
    return (y * w.astype(jnp.float32) + b.astype(jnp.float32)).astype(x.dtype)


def modulate(h, shift, scale):
    return h * (1 + scale[:, None, :]) + shift[:, None, :]


def gla_chunked(q, k, v, log_a):
    B, T, H, dk = q.shape
    dv = v.shape[-1]
    C = GLA_CHUNK
    N = T // C
    f32 = jnp.float32

    def to_chunks(t):
        return t.astype(f32).reshape(B, N, C, H, t.shape[-1]).transpose(1, 0, 3, 2, 4)

    qc, kc, vc, lac = to_chunks(q) * dk ** -0.5, to_chunks(k), to_chunks(v), to_chunks(log_a)
    G = jnp.cumsum(lac, axis=-2)
    G_last = G[..., -1:, :]
    q_dec = qc * jnp.exp(G)
    k_inv = kc * jnp.exp(-G)
    k_tail = kc * jnp.exp(G_last - G)
    mask = jnp.tril(jnp.ones((C, C), dtype=bool))
    A = jnp.where(mask, jnp.einsum('nbhik,nbhjk->nbhij', q_dec, k_inv), 0.0)
    o_intra = jnp.einsum('nbhij,nbhjv->nbhiv', A, vc)

    def step(S, inp):
        qd, kt, vv, gl = inp
        o = jnp.einsum('bhik,bhkv->bhiv', qd, S)
        S = S * jnp.exp(gl)[..., 0, :, None] + jnp.einsum('bhjk,bhjv->bhkv', kt, vv)
        return S, o

    S0 = jnp.zeros((B, H, dk, dv), f32)
    _, o_inter = lax.scan(step, S0, (q_dec, k_tail, vc, G_last))
    o = o_intra + o_inter
    return o.transpose(1, 0, 3, 2, 4).reshape(B, T, H, dv).astype(q.dtype)


def spatial_gating(u, v, w_s, b_s, ln_w, ln_b):
    B, T, _ = v.shape
    N = T // SGU_CHUNK
    v = layer_norm(v, ln_w, ln_b)
    vg = v.reshape(B, N, SGU_CHUNK, SGU_GROUPS, SGU_CH)
    mask = jnp.tril(jnp.ones((SGU_CHUNK, SGU_CHUNK), dtype=bool))
    w = jnp.where(mask[None], w_s, 0)
    s = jnp.einsum('gij,bnjgc->bnigc', w, vg) + b_s.T[None, None, :, :, None]
    return u * s.reshape(B, T, SGU_W)


def peer(h, w_q, keys1, keys2, expert_down, expert_up):
    B, T, D = h.shape
    H, K = PEER_HEADS, PEER_TOPK
    q = jnp.einsum('btd,dq->btq', h, w_q).reshape(B, T, H, 2, PEER_DQ // 2)
    s1 = jnp.einsum('bthc,kc->bthk', q[..., 0, :], keys1).astype(jnp.float32)
    s2 = jnp.einsum('bthc,kc->bthk', q[..., 1, :], keys2).astype(jnp.float32)
    v1, i1 = lax.top_k(s1, K)
    v2, i2 = lax.top_k(s2, K)
    cand = (v1[..., :, None] + v2[..., None, :]).reshape(B, T, H, K * K)
    cidx = (i1[..., :, None] * PEER_NKEYS + i2[..., None, :]).reshape(B, T, H, K * K)
    top_s, pos = lax.top_k(cand, K)
    idx = jnp.take_along_axis(cidx, pos, axis=-1)
    g = jax.nn.softmax(top_s, axis=-1).astype(h.dtype)
    nb = (B * T) // PEER_BLOCK
    xb = h.reshape(nb, PEER_BLOCK, D)
    ib = idx.reshape(nb, PEER_BLOCK, H * K)
    gb = g.reshape(nb, PEER_BLOCK, H * K)

    def block(args):
        xx, ii, gg = args
        u = expert_down[ii]
        a = jax.nn.gelu(jnp.einsum('pd,ped->pe', xx, u), approximate=False)
        vv = expert_up[ii]
        return jnp.einsum('pe,ped->pd', gg * a, vv)

    return lax.map(block, (xb, ib, gb)).reshape(B, T, D)


def setup_inputs(seed: int = 0) -> dict:
    key = jax.random.key(seed)
    ks = jax.random.split(key, 24)
    f32 = jnp.float32
    L, D = DEPTH, D_MODEL

    def nrm(k, shape, scale):
        return jax.random.normal(k, shape, f32) * scale

    return {
        'x': nrm(ks[0], (BATCH, SEQ, D), 1.0),
        'c': nrm(ks[1], (BATCH, D), 1.0),
        'w_ada': nrm(ks[2], (L, D, 6 * D), 0.5 * D ** -0.5),
        'b_ada': nrm(ks[3], (L, 6 * D), 0.02),
        'norm1_w': 1.0 + nrm(ks[4], (L, D), 0.02),
        'w_in': nrm(ks[5], (L, D, IN_WIDTH), D ** -0.5),
        'w_alpha_up': nrm(ks[6], (L, GLA_RANK, GLA_QK), GLA_RANK ** -0.5),
        'b_alpha': nrm(ks[7], (L, GLA_QK), 0.1),
        'gla_norm_w': 1.0 + nrm(ks[8], (L, GLA_V), 0.02),
        'sgu_ln_w': 1.0 + nrm(ks[9], (L, SGU_W), 0.02),
        'sgu_ln_b': nrm(ks[10], (L, SGU_W), 0.02),
        'sgu_w': nrm(ks[11], (L, SGU_GROUPS, SGU_CHUNK, SGU_CHUNK), 0.5 * SGU_CHUNK ** -0.5),
        'sgu_b': 1.0 + nrm(ks[12], (L, SGU_GROUPS, SGU_CHUNK), 0.02),
        'w_out': nrm(ks[13], (L, D, D), D ** -0.5),
        'norm2_w': 1.0 + nrm(ks[14], (L, D), 0.02),
        'peer_w_q': nrm(ks[15], (L, D, PEER_HEADS * PEER_DQ), D ** -0.5),
        'peer_keys1': nrm(ks[16], (L, PEER_NKEYS, PEER_DQ // 2), (PEER_DQ // 2) ** -0.5),
        'peer_keys2': nrm(ks[17], (L, PEER_NKEYS, PEER_DQ // 2), (PEER_DQ // 2) ** -0.5),
        'peer_down': nrm(ks[18], (L, PEER_NEXP, D), D ** -0.5),
        'peer_up': nrm(ks[19], (L, PEER_NEXP, D), D ** -0.5),
        'final_norm_w': 1.0 + nrm(ks[20], (D,), 0.02),
    }


def reference(x, c, w_ada, b_ada, norm1_w, w_in, w_alpha_up, b_alpha, gla_norm_w,
              sgu_ln_w, sgu_ln_b, sgu_w, sgu_b, w_out, norm2_w, peer_w_q, peer_keys1,
              peer_keys2, peer_down, peer_up, final_norm_w):
    B, T, D = x.shape
    silu_c = jax.nn.silu(c)
    for l in range(DEPTH):
        mod = jnp.einsum('bd,de->be', silu_c, w_ada[l]) + b_ada[l]
        shift1, scale1, gate1, shift2, scale2, gate2 = jnp.split(mod, 6, axis=-1)

        h = modulate(rms_norm(x, norm1_w[l]), shift1, scale1)
        proj = jnp.einsum('btd,de->bte', h, w_in[l])
        q, k, v, r, a_lr, su, sv, g_a, g_b = jnp.split(proj, IN_SPLITS, axis=-1)

        z = jnp.einsum('btr,rk->btk', a_lr, w_alpha_up[l]) + b_alpha[l]
        log_a = jax.nn.log_sigmoid(z.astype(jnp.float32)) / GLA_TAU
        o = gla_chunked(q.reshape(B, T, GLA_HEADS, GLA_DK), k.reshape(B, T, GLA_HEADS, GLA_DK),
                        v.reshape(B, T, GLA_HEADS, GLA_DV), log_a.reshape(B, T, GLA_HEADS, GLA_DK))
        o = rms_norm(o, gla_norm_w[l].reshape(GLA_HEADS, GLA_DV)).reshape(B, T, GLA_V)
        o_gla = o * jax.nn.silu(r)

        o_sgu = spatial_gating(jax.nn.gelu(su, approximate=False), jax.nn.gelu(sv, approximate=False),
                               sgu_w[l], sgu_b[l], sgu_ln_w[l], sgu_ln_b[l])

        merged = jax.nn.sigmoid(g_a) * o_gla + jax.nn.sigmoid(g_b) * o_sgu
        y = jnp.einsum('bte,ed->btd', merged, w_out[l])
        x = x + gate1[:, None, :] * y

        h2 = modulate(rms_norm(x, norm2_w[l]), shift2, scale2)
        y2 = peer(h2, peer_w_q[l], peer_keys1[l], peer_keys2[l], peer_down[l], peer_up[l])
        x = x + gate2[:, None, :] * y2
    return rms_norm(x, final_norm_w)
```

```python
import contextlib
import numpy as np
import concourse.bass as bass
import concourse.mybir as mybir
from concourse.bass_utils import run_bass_kernel_spmd

F32 = mybir.dt.float32
BF16 = mybir.dt.bfloat16
U32 = mybir.dt.uint32
AF = mybir.ActivationFunctionType
ALU = mybir.AluOpType
AX = mybir.AxisListType

ENGS = ("pe", "act", "dve", "pool", "sp")
EPS = 1e-6
NT = 16
import os
NDBG = int(os.environ.get('NDBG', '1'))
D = 1024
IN_SIZES = (512, 512, 1024, 1024, 16, 1024, 1024, 1024, 1024)
IN_OFF = [int(v) for v in np.cumsum((0,) + IN_SIZES)]
OQ, OK_, OV, OR, OALR, OSU, OSV, OGA, OGB = IN_OFF[:9]


class _Op:
    __slots__ = ("eng", "fn", "waits", "sig", "is_dma", "chan")


class Sched:
    def __init__(self, nc):
        self.nc = nc
        self.q = {e: [] for e in ENGS}
        self.chan_count = {}
        self.last_w = {}
        self.readers = {}
        self.cap = None

    def _add_wait(self, op, tok):
        if tok is None:
            return
        if tok[0] == "e":
            if tok[1] == op.eng and tok[1] == "pe":
                return
            if tok[2].sig is None:
                tok[2].sig = 1
        op.waits.append(tok)

    def op(self, eng, fn, reads=(), writes=(), dma_chan=None):
        if self.cap is not None:
            self.cap.append((eng, fn, tuple(reads), tuple(writes), dma_chan))
            return None
        o = _Op()
        o.eng = eng
        o.fn = fn
        o.waits = []
        o.sig = None
        o.is_dma = dma_chan is not None
        o.chan = dma_chan
        for k in reads:
            self._add_wait(o, self.last_w.get(k))
        for k in writes:
            self._add_wait(o, self.last_w.get(k))
            lastr = {}
            for t in self.readers.get(k, ()):
                if t[0] == "e":
                    lastr[t[1]] = t
                else:
                    self._add_wait(o, t)
            for t in lastr.values():
                self._add_wait(o, t)
        if o.is_dma:
            self.chan_count[dma_chan] = self.chan_count.get(dma_chan, 0) + 1
            tok = ("d", dma_chan, 16 * self.chan_count[dma_chan])
        else:
            tok = ("e", eng, o)
        for k in writes:
            self.last_w[k] = tok
            self.readers[k] = []
        for k in reads:
            self.readers.setdefault(k, []).append(tok)
        self.q[eng].append(o)
        return o

    def barrier(self):
        toks = []
        for e in ENGS:
            last = None
            for o in reversed(self.q[e]):
                if not o.is_dma:
                    last = o
                    break
            if last is not None:
                toks.append(("e", e, last))
        for c, n in self.chan_count.items():
            toks.append(("d", c, 16 * n))
        for e in ENGS:
            o = _Op()
            o.eng = e
            o.fn = lambda eng: eng.nop()
            o.waits = []
            o.sig = None
            o.is_dma = False
            o.chan = None
            for t in toks:
                if t[0] == "e":
                    if t[2].sig is None:
                        t[2].sig = 1
                o.waits.append(t)
            self.q[e].append(o)
        self.last_w = {}
        self.readers = {}

    def emit(self, final_waits=()):
        nc = self.nc
        for e in ENGS:
            n = 0
            for o in self.q[e]:
                if o.sig is not None and not o.is_dma:
                    n += 1
                    o.sig = n
        chans = sorted(self.chan_count.keys(), key=str)
        print("ops:", {e: len(self.q[e]) for e in ENGS}, "chan max:", max(self.chan_count.values()) * 16)
        with contextlib.ExitStack() as st:
            esem = {e: st.enter_context(nc.semaphore("s_" + e)) for e in ENGS}
            csem = {c: st.enter_context(nc.semaphore("c_%d" % i)) for i, c in enumerate(chans)}
            block = st.enter_context(nc.Block())

            def run(ename, eng):
                seen = {}
                for o in self.q[ename]:
                    for w in o.waits:
                        if w[0] == "e":
                            key = ("e", w[1]); val = w[2].sig; sem = esem[w[1]]
                        else:
                            key = ("d", w[1]); val = w[2]; sem = csem[w[1]]
                        if seen.get(key, 0) >= val:
                            continue
                        seen[key] = val
                        eng.wait_ge(sem, val)
                    ins = o.fn(eng)
                    if o.is_dma:
                        ins.then_inc(csem[o.chan], 16)
                    elif o.sig is not None:
                        ins.then_inc(esem[ename], 1)
                if ename == "sp":
                    for c in final_waits:
                        eng.wait_ge(csem[c], 16 * self.chan_count[c])

            @block.sync
            def _(eng):
                run("sp", eng)

            @block.scalar
            def _(eng):
                run("act", eng)

            @block.vector
            def _(eng):
                run("dve", eng)

            @block.gpsimd
            def _(eng):
                run("pool", eng)

            @block.tensor
            def _(eng):
                run("pe", eng)


class Mem:
    def __init__(self, nc, base=20608, limit=229376):
        self.nc = nc
        self.off = base
        self.limit = limit
        self.n = 0

    def alloc(self, name, shape, dt):
        size = 1
        for s in shape[1:]:
            size *= s
        size *= {F32: 4, BF16: 2, U32: 4}[dt]
        size = (size + 63) // 64 * 64
        assert self.off + size <= self.limit, (name, self.off, size)
        self.n += 1
        t = self.nc.alloc_sbuf_tensor_at("%s_%d" % (name, self.n), list(shape), dt, offset=self.off)
        self.off += size
        return t


def _dtsize(dt):
    return {F32: 4, BF16: 2, U32: 4}[dt]


def build(stage=9):
    nc = bass.Bass("TRN2", target_bir_lowering=False)
    S = Sched(nc)
    M = Mem(nc)
    M_alloc = M.alloc

    def dram(name, shape, kind="ExternalInput", dt=F32):
        return nc.dram_tensor(name, list(shape), dt, kind=kind).ap()

    xpre = dram("xpre", [3, 2048, D])
    xown = dram("xown", [2048, D])
    flags_d = dram("flags", [128, 4])
    cT_d = dram("cT", [128, 8])
    colv_d = dram("colv", [128, 2, 8])
    rowv_d = dram("rowv", [4, D])
    wada_d = dram("w_ada", [D, 6 * D])
    bada_d = dram("b_ada", [1, 6 * D])
    win_d = dram("w_in", [D, IN_OFF[9]])
    wup_d = dram("w_alpha_up", [16, 512])
    balpha_d = dram("b_alpha", [1, 512])
    sguw_d = dram("sgu_wT", [8, 128, 128])
    sgub_d = dram("sgu_bT", [128, 8])
    wout_d = dram("w_out", [D, D])
    wq_d = dram("peer_w_q", [D, 2048])
    k1T_d = dram("keys1T", [128, 128])
    k2T_d = dram("keys2T", [128, 128])
    downT_d = dram("downT", [128, 128, 8, 128])
    up_d = dram("up", [128, 128, D])
    out_d = dram("out", [2048, D], kind="ExternalOutput")
    scr_down = nc.dram_tensor("scr_down", [128, 128, 8, 128], BF16).ap()
    scr_up = nc.dram_tensor("scr_up", [128, 128, D], BF16).ap()
    cvt_jobs = []
    for g in range(32):
        cvt_jobs.append((scr_down[:, g * 4:(g + 1) * 4, :, :], downT_d[:, g * 4:(g + 1) * 4, :, :]))
        cvt_jobs.append((scr_up[:, g * 4:(g + 1) * 4, :], up_d[:, g * 4:(g + 1) * 4, :]))

    def issue_cvt(n):
        for _ in range(n):
            if cvt_jobs:
                o_, i_ = cvt_jobs.pop(0)
                S.op("pool", lambda e, o_=o_, i_=i_: e.dma_start(out=o_, in_=i_), writes=["scr"], dma_chan="c_cvt")

    P01 = nc.alloc_psum_tensor("P01", [128, 1024], F32)
    P23 = nc.alloc_psum_tensor("P23", [128, 1024], F32)
    P45 = nc.alloc_psum_tensor("P45", [128, 1024], F32)
    P67 = nc.alloc_psum_tensor("P67", [128, 1024], F32)
    W0 = P01[:, :]; W1 = P23[:, :]; W2 = P67[:, :]
    W0a = P01[:, 0:512]; W0b = P01[:, 512:1024]
    W2a = P67[:, 0:512]; W2b = P67[:, 512:1024]
    PA = P45[:, 0:512]; PB = P45[:, 512:1024]
    kW2 = ["W2a", "W2b"]

    def alloc(name, shape, dt):
        return M_alloc(name, shape, dt)

    ident_f = alloc("ident_f", [128, 128], F32)
    ident_b = alloc("ident_b", [128, 128], BF16)
    iota_f = alloc("iota_f", [128, 128], F32)
    iota_b = alloc("iota_b", [128, 128], BF16)
    Lm = alloc("Lm", [128, 128], F32)
    Rm = alloc("Rm", [128, 128], F32)
    maskU = alloc("maskU", [128, 128], F32)
    ones_f = alloc("ones_f", [128, 128], F32)
    neghalf = alloc("neghalf", [128, 8], F32)
    flags = alloc("flagsb", [128, 4], F32)
    cols = alloc("cols", [128, 6, 8], F32)
    mark_p3 = M.off
    Bsgu = alloc("Bsgu", [128, 8], F32)
    WsT = alloc("WsT", [128, 8, 128], BF16)
    wup_f = alloc("wup_f", [16, 512], F32)
    balpha_f = alloc("balpha_f", [1, 512], F32)
    lnwB = alloc("lnwB", [128, D], F32)
    lnbB = alloc("lnbB", [128, D], F32)
    gnwB = alloc("gnwB", [128, D], F32)
    woutg = alloc("woutg", [128, 8, D], BF16)
    wk = alloc("wk", [128, 8, 512], BF16)
    wv = alloc("wv", [128, 8, 1024], BF16)
    walr = alloc("walr", [128, 8, 16], BF16)
    S_f = alloc("S_f", [128, 4, 256], F32)
    Sb0 = alloc("Sb0", [128, 4, 256], BF16)
    Sb1 = alloc("Sb1", [128, 4, 256], BF16)
    mark_late = M.off

    pid = alloc("pid", [128, 1], F32)
    tmpA = alloc("tmpA", [128, 128], F32)
    tmpB = alloc("tmpB", [128, 128], F32)
    cj = alloc("cj", [128, 1], F32)
    scB = alloc("scB", [128, 8, 128], F32)
    cT = alloc("cTs", [128, 8], F32)
    sigc = alloc("sigc", [128, 8], F32)
    colv = alloc("colvs", [128, 2, 8], F32)
    modB = alloc("modB", [128, 6 * D], F32)
    wbuf = [alloc("wbuf%d" % i, [128, 8, 512], F32) for i in range(2)]
    bb = [alloc("bb%d" % i, [128, 512], F32) for i in range(2)]
    wo_tmp = alloc("wo_tmp", [128, 4, D], F32)
    sgu_tmp = alloc("sgu_tmp", [128, 8, 128], F32)

    def iota(e, out, pattern, base, cm):
        return e.iota(out, pattern=pattern, base=base, channel_multiplier=cm,
                      allow_small_or_imprecise_dtypes=True)

    S.op("pool", lambda e: iota(e, iota_f[:], [[1, 128]], 0, 0), writes=["iota_f"])
    S.op("dve", lambda e: e.tensor_copy(out=iota_b[:], in_=iota_f[:]), reads=["iota_f"], writes=["iota_b"])
    S.op("pool", lambda e: iota(e, pid[:], [[0, 1]], 0, 1), writes=["pid"])
    S.op("pool", lambda e: iota(e, tmpA[:], [[1, 128]], 0, -1), writes=["tmpA"])
    S.op("pool", lambda e: e.memset(ones_f[:], 1.0), writes=["ones_f"])
    S.op("pool", lambda e: e.memset(neghalf[:], -0.5), writes=["neghalf"])
    S.op("pool", lambda e: e.memset(S_f[:], 0.0), writes=["S_f"])
    S.op("dve", lambda e: e.tensor_single_scalar(out=ident_f[:], in_=tmpA[:], scalar=0.0, op=ALU.is_equal),
         reads=["tmpA"], writes=["ident_f"])
    S.op("dve", lambda e: e.tensor_copy(out=ident_b[:], in_=ident_f[:]), reads=["ident_f"], writes=["ident_b"])
    S.op("dve", lambda e: e.tensor_single_scalar(out=cj[:], in_=pid[:], scalar=64.0, op=ALU.is_ge),
         reads=["pid"], writes=["cj"])
    S.op("dve", lambda e: e.tensor_scalar(out=tmpB[:], in0=iota_f[:], scalar1=64.0, scalar2=cj[:, 0:1],
                                          op0=ALU.is_ge, op1=ALU.is_equal),
         reads=["iota_f", "cj"], writes=["tmpB"])
    maskS = alloc("maskS", [128, 128], F32)
    S.op("dve", lambda e: e.tensor_single_scalar(out=maskS[:], in_=tmpA[:], scalar=0.0, op=ALU.is_ge),
         reads=["tmpA"], writes=["maskS"])
    S.op("dve", lambda e: e.tensor_tensor(out=maskU[:], in0=maskS[:], in1=tmpB[:], op=ALU.mult),
         reads=["maskS", "tmpB"], writes=["maskU"])
    S.op("dve", lambda e: e.tensor_scalar(out=Lm[:], in0=maskU[:], scalar1=-1.0 / 16.0, scalar2=None, op0=ALU.mult),
         reads=["maskU"], writes=["Lm"])
    S.op("dve", lambda e: e.tensor_tensor(out=Rm[:], in0=tmpB[:], in1=maskU[:], op=ALU.subtract),
         reads=["tmpB", "maskU"], writes=["Rm"])
    S.op("dve", lambda e: e.tensor_scalar(out=Rm[:], in0=Rm[:], scalar1=-1.0 / 16.0, scalar2=None, op0=ALU.mult),
         reads=["Rm"], writes=["Rm"])

    S.op("sp", lambda e: e.dma_start(out=flags[:], in_=flags_d[:, :]), writes=["flags"], dma_chan="c_flags")
    S.op("sp", lambda e: e.dma_start(out=cT[:], in_=cT_d[:, :]), writes=["cT"], dma_chan="c_cT")
    S.op("sp", lambda e: e.dma_start(out=colv[:], in_=colv_d[:, :, :]), writes=["colv"], dma_chan="c_colv")
    S.op("sp", lambda e: e.dma_start(out=Bsgu[:], in_=sgub_d[:, :]), writes=["Bsgu"], dma_chan="c_bsgu")
    S.op("sp", lambda e: e.dma_start(out=wup_f[:], in_=wup_d[:, :]), writes=["wup_f"], dma_chan="c_wup")
    S.op("sp", lambda e: e.dma_start(out=balpha_f[:], in_=balpha_d[:, :]), writes=["balpha_f"], dma_chan="c_balpha")
    S.op("act", lambda e: e.dma_start(out=gnwB[:], in_=rowv_d[1:2, :].to_broadcast([128, D])), writes=["gnwB"], dma_chan="c_gnw")
    S.op("act", lambda e: e.dma_start(out=lnwB[:], in_=rowv_d[2:3, :].to_broadcast([128, D])), writes=["lnwB"], dma_chan="c_lnw")
    S.op("act", lambda e: e.dma_start(out=lnbB[:], in_=rowv_d[3:4, :].to_broadcast([128, D])), writes=["lnbB"], dma_chan="c_lnb")
    S.op("sp", lambda e: e.dma_start(out=sgu_tmp[:], in_=sguw_d.rearrange("g j i -> j g i")), writes=["sgu_tmp"], dma_chan="c_sguw")
    win_v = win_d.rearrange("(c p) e -> p c e", p=128)
    for ch in range(8):
        S.op("pool", lambda e, ch=ch: e.dma_start(out=wk[:, ch, :], in_=win_v[:, ch, OK_:OK_ + 512]), writes=["wk"], dma_chan="c_wk")
        S.op("pool", lambda e, ch=ch: e.dma_start(out=wv[:, ch, :], in_=win_v[:, ch, OV:OV + 1024]), writes=["wv"], dma_chan="c_wv")
        S.op("pool", lambda e, ch=ch: e.dma_start(out=walr[:, ch, :], in_=win_v[:, ch, OALR:OALR + 16]), writes=["walr"], dma_chan="c_walr")

    S.op("dve", lambda e: e.tensor_tensor(out=WsT[:], in0=sgu_tmp[:], in1=maskS[:].unsqueeze(1).to_broadcast([128, 8, 128]), op=ALU.mult),
         reads=["sgu_tmp", "maskS"], writes=["WsT"])

    S.op("act", lambda e: e.activation(out=sigc[:], in_=cT[:], func=AF.Sigmoid), reads=["cT"], writes=["sigc"])
    S.op("dve", lambda e: e.tensor_tensor(out=sigc[:], in0=sigc[:], in1=cT[:], op=ALU.mult), reads=["sigc", "cT"], writes=["sigc"])
    S.op("dve", lambda e: e.tensor_copy(out=scB[:], in_=sigc[:].unsqueeze(2).to_broadcast([128, 8, 128])),
         reads=["sigc"], writes=["scB"])

    wada_v = wada_d.rearrange("(c p) e -> p c e", p=128)
    for g in range(12):
        b = g % 2
        wb_, bb_ = wbuf[b], bb[b]
        kq = "sp" if b == 0 else "act"
        S.op(kq, lambda e, g=g, wb_=wb_: e.dma_start(out=wb_[:], in_=wada_v[:, :, g * 512:(g + 1) * 512]),
             writes=["wbuf%d" % b], dma_chan="c_wbuf%d" % b)
        S.op(kq, lambda e, g=g, bb_=bb_: e.dma_start(out=bb_[:], in_=bada_d[0:1, g * 512:(g + 1) * 512].to_broadcast([128, 512])),
             writes=["bb%d" % b], dma_chan="c_bb%d" % b)
        for ch in range(8):
            S.op("pe", lambda e, ch=ch, wb_=wb_: e.matmul(PA, lhsT=scB[:, ch, :], rhs=wb_[:, ch, :], start=(ch == 0), stop=(ch == 7)),
                 reads=["scB", "wbuf%d" % b], writes=["PA"])
        S.op("dve", lambda e, g=g, bb_=bb_: e.tensor_tensor(out=modB[:, g * 512:(g + 1) * 512], in0=PA, in1=bb_[:], op=ALU.add),
             reads=["PA", "bb%d" % b], writes=["modB"])

    def col_from_mod(dst_idx, mod_off):
        for ch in range(8):
            S.op("pe", lambda e, ch=ch: e.transpose(PB[:, 0:128], modB[:, mod_off + ch * 128: mod_off + (ch + 1) * 128], ident_f[:]),
                 reads=["modB", "ident_f"], writes=["PB"])
            S.op("dve", lambda e, ch=ch: e.tensor_copy(out=cols[:, dst_idx, ch:ch + 1], in_=PB[:, 0:1]),
                 reads=["PB"], writes=["cols"])
    col_from_mod(1, 0 * D)
    col_from_mod(0, 1 * D)
    col_from_mod(3, 3 * D)
    col_from_mod(2, 4 * D)
    col_from_mod(4, 5 * D)
    for di, ci in ((0, 0), (2, 1)):
        S.op("dve", lambda e, di=di, ci=ci: e.scalar_tensor_tensor(out=cols[:, di, :], in0=cols[:, di, :], scalar=1.0,
                                                                   in1=colv[:, ci, :], op0=ALU.add, op1=ALU.mult),
             reads=["cols", "colv"], writes=["cols"])

    wout_v = wout_d.rearrange("(c p) e -> p c e", p=128)
    for hf in range(2):
        S.op("sp", lambda e, hf=hf: e.dma_start(out=wo_tmp[:], in_=wout_v[:, hf * 4:(hf + 1) * 4, :]), writes=["wo_tmp"], dma_chan="c_wo")
        S.op("dve", lambda e, hf=hf: e.tensor_tensor(out=woutg[:, hf * 4:(hf + 1) * 4, :], in0=wo_tmp[:],
                                                     in1=modB[:, 2 * D:3 * D].unsqueeze(1).to_broadcast([128, 4, D]), op=ALU.mult),
             reads=["wo_tmp", "modB"], writes=["woutg"])

    if stage == 0:
        dbg = dram("dbg", [128, 2048], kind="ExternalOutput")
        S.op("sp", lambda e: e.dma_start(out=dbg[:, 0:48], in_=cols[:].rearrange("p a b -> p (a b)")), reads=["cols"], writes=["dbg"], dma_chan="c_out")
        S.op("sp", lambda e: e.dma_start(out=dbg[:, 128:256], in_=Lm[:]), reads=["Lm"], writes=["dbg"], dma_chan="c_out")
        S.op("sp", lambda e: e.dma_start(out=dbg[:, 256:384], in_=Rm[:]), reads=["Rm"], writes=["dbg"], dma_chan="c_out")
        S.op("sp", lambda e: e.dma_start(out=dbg[:, 1024:2048], in_=modB[:, 2048:3072]), reads=["modB"], writes=["dbg"], dma_chan="c_out")
        S.emit(final_waits=["c_out0", "c_out1"])
        return nc
    S.barrier()
    M.off = mark_late

    wq = alloc("wq", [128, 8, 512], BF16)
    wr = alloc("wr", [128, 8, 1024], BF16)
    wsu = alloc("wsu", [128, 8, 1024], BF16)
    wsv = alloc("wsv", [128, 8, 1024], BF16)
    wga = alloc("wga", [128, 8, 1024], BF16)
    wgb = alloc("wgb", [128, 8, 1024], BF16)
    mark_work = M.off

    xt = alloc("xt", [128, D], F32)
    xtB = alloc("xtB", [128, D], F32)
    mTb = alloc("mTb", [128, 8, 128], BF16)
    Fs = [alloc("F%d" % i, [128, D], F32) for i in range(4)]
    xn, t2_, t1_, u_, gv_ = Fs[0], Fs[0], Fs[1], Fs[2], Fs[3]
    hT = alloc("hT", [128, 8, 128], BF16)
    v_bf = alloc("v_bf", [128, D], BF16)
    alrT = alloc("alrT", [16, 128], F32)
    bufE = alloc("bufE", [128, 512], F32)
    lbuf = alloc("lbuf", [128, 512], F32)
    expnG = alloc("expnG", [128, 512], F32)
    ktail = alloc("ktail", [128, 512], BF16)
    dec = alloc("dec", [128, 4, 2], F32)
    ss = alloc("ss", [128, 8], F32)
    rstd = alloc("rstd", [128, 8], F32)
    qd = alloc("qd", [128, 4, 128], BF16)
    q0 = alloc("q0", [128, 4, 128], BF16)
    q1 = alloc("q1", [128, 4, 128], BF16)
    ki = alloc("ki", [128, 4, 128], BF16)
    ATm = alloc("ATm", [128, 4, 128], BF16)
    vln = alloc("vln", [128, D], BF16)
    bnst = alloc("bnst", [128, 2, 6], F32)
    mv = alloc("mv", [128, 2], F32)
    print("SBUF used (P2):", M.off)

    S.op("pool", lambda e: e.memset(q0[:], 0.0), writes=["q0"])
    S.op("pool", lambda e: e.memset(q1[:], 0.0), writes=["q1"])

    def load_late():
        for ch in range(8):
            for (wt, off, n, nm) in ((wq, OQ, 512, "wq"), (wr, OR, 1024, "wr"), (wga, OGA, 1024, "wga"),
                                     (wsu, OSU, 1024, "wsu"), (wsv, OSV, 1024, "wsv"), (wgb, OGB, 1024, "wgb")):
                S.op("pool", lambda e, ch=ch, wt=wt, off=off, n=n: e.dma_start(out=wt[:, ch, :], in_=win_v[:, ch, off:off + n]),
                     writes=[nm], dma_chan="c_" + nm)

    def proj_tm(dst, dkeys, wt, wkey, c0, n):
        for ch in range(8):
            S.op("pe", lambda e, ch=ch: e.matmul(dst, lhsT=hT[:, ch, :], rhs=wt[:, ch, c0:c0 + n], start=(ch == 0), stop=(ch == 7)),
                 reads=[("hT", ch), wkey], writes=dkeys)

    xts = [xt, xtB]

    def p2_front(k):
        xk = xts[k % 2]
        kx = "xt%d" % (k % 2)
        x_ap = xown[k * 128:(k + 1) * 128, :]
        S.op("sp", lambda e: e.dma_start(out=xk[:], in_=x_ap), writes=[kx], dma_chan="c_xt%d" % (k % 2))
        S.op("act", lambda e: e.activation(out=xn[:], in_=xk[:], func=AF.Square, accum_out=ss[:, 0:1]),
             reads=[kx], writes=["F0", "ss_f"])
        S.op("dve", lambda e: e.tensor_scalar(out=ss[:, 1:2], in0=ss[:, 0:1], scalar1=1.0 / D, scalar2=EPS, op0=ALU.mult, op1=ALU.add),
             reads=["ss_f"], writes=["ss_f2"])
        S.op("pool", lambda e: e.tensor_tensor(out=rstd[:, 0:1], in0=ss[:, 1:2], in1=neghalf[:, 0:1], op=ALU.pow),
             reads=["ss_f2", "neghalf"], writes=["rstd_f"])
        S.op("dve", lambda e: e.tensor_scalar(out=xn[:], in0=xk[:], scalar1=rstd[:, 0:1], scalar2=None, op0=ALU.mult),
             reads=[kx, "rstd_f"], writes=["F0"])
        for ch in range(8):
            S.op("pe", lambda e, ch=ch: e.transpose(W0[:, ch * 128:(ch + 1) * 128], xn[:, ch * 128:(ch + 1) * 128], ident_f[:]),
                 reads=["F0", "ident_f"], writes=["W0a", "W0b"])
        for ch in range(8):
            if ch < 4:
                S.op("dve", lambda e, ch=ch: e.tensor_scalar(out=hT[:, ch, :], in0=W0[:, ch * 128:(ch + 1) * 128],
                                                             scalar1=cols[:, 0, ch:ch + 1], scalar2=cols[:, 1, ch:ch + 1],
                                                             op0=ALU.mult, op1=ALU.add),
                     reads=["W0a" if ch < 4 else "W0b", "cols"], writes=[("hT", ch)])
            else:
                S.op("act", lambda e, ch=ch: e.activation(out=hT[:, ch, :], in_=W0[:, ch * 128:(ch + 1) * 128], func=AF.Identity,
                                                          scale=cols[:, 0, ch:ch + 1], bias=cols[:, 1, ch:ch + 1]),
                     reads=["W0a" if ch < 4 else "W0b", "cols"], writes=[("hT", ch)])

    def p2_body(k):
        seg, full, row0 = 3, True, k * 128
        fcol = flags[:, seg:seg + 1]
        capA = []
        S.cap = capA
        proj_tm(PA, ["PA"], wk, "wk", 0, 512)
        proj_tm(W1[:, 0:512], ["W1"], wv, "wv", 0, 512)
        proj_tm(W1[:, 512:1024], ["W1"], wv, "wv", 512, 512)
        for ch in range(8):
            S.op("pe", lambda e, ch=ch: e.matmul(PB[0:16, 0:128], lhsT=walr[:, ch, :], rhs=hT[:, ch, :], start=(ch == 0), stop=(ch == 7)),
                 reads=[("hT", ch), "walr"], writes=["PB"])
        S.op("dve", lambda e: e.tensor_scalar(out=v_bf[:], in0=W1, scalar1=fcol, scalar2=None, op0=ALU.mult),
             reads=["W1", "flags"], writes=["v_bf"])
        S.op("act", lambda e: e.activation(out=alrT[:], in_=PB[0:16, 0:128], func=AF.Copy), reads=["PB"], writes=["alrT"])
        S.op("pe", lambda e: e.matmul(W0a, lhsT=alrT[:], rhs=wup_f[:], start=True, stop=False),
             reads=["alrT", "wup_f"], writes=["W0a"])
        S.op("pe", lambda e: e.matmul(W0a, lhsT=ones_f[0:1, :], rhs=balpha_f[:], start=False, stop=True),
             reads=["ones_f", "balpha_f"], writes=["W0a"])
        S.op("act", lambda e: e.activation(out=bufE[:], in_=W0a, func=AF.Exp, scale=-1.0), reads=["W0a"], writes=["bufE"])
        S.op("act", lambda e: e.activation(out=lbuf[:], in_=bufE[:], func=AF.Ln, bias=1.0, scale=1.0), reads=["bufE"], writes=["lbuf"])
        S.op("pe", lambda e: e.matmul(W0b, lhsT=Rm[:], rhs=lbuf[:], start=True, stop=True), reads=["Rm", "lbuf"], writes=["W0b"])
        for h in range(4):
            S.op("pe", lambda e, h=h: e.matmul(PB[:, h * 128:(h + 1) * 128], lhsT=lbuf[:, h * 128:(h + 1) * 128], rhs=Lm[:], start=True, stop=True),
                 reads=["lbuf", "Lm"], writes=["PB"])
        S.op("act", lambda e: e.activation(out=bufE[:], in_=W0b, func=AF.Exp), reads=["W0b"], writes=["bufE"])
        S.op("dve", lambda e: e.tensor_tensor(out=ktail[:], in0=PA, in1=bufE[:], op=ALU.mult), reads=["PA", "bufE"], writes=["ktail"])
        PBv = PB.rearrange("p (h t) -> p h t", h=4)
        S.op("act", lambda e: e.activation(out=dec[:], in_=PBv[:, :, 63:128:64], func=AF.Exp), reads=["PB"], writes=["dec"])
        if full:
            for h in range(4):
                for ch in range(8):
                    S.op("pe", lambda e, h=h, ch=ch: e.matmul(W0a[:, h * 128:(h + 1) * 128], lhsT=wq[:, ch, h * 128:(h + 1) * 128], rhs=hT[:, ch, :],
                                                             start=(ch == 0), stop=(ch == 7)),
                         reads=[("hT", ch), "wq"], writes=["W0a"])
            for h in range(4):
                for ch in range(8):
                    S.op("pe", lambda e, h=h, ch=ch: e.matmul(W0b[:, h * 128:(h + 1) * 128], lhsT=wk[:, ch, h * 128:(h + 1) * 128], rhs=hT[:, ch, :],
                                                             start=(ch == 0), stop=(ch == 7)),
                         reads=[("hT", ch), "wk"], writes=["W0b"])
            S.op("act", lambda e: e.activation(out=lbuf[:], in_=PB, func=AF.Exp), reads=["PB"], writes=["lbuf"])
            S.op("act", lambda e: e.activation(out=expnG[:], in_=PB, func=AF.Exp, scale=-1.0), reads=["PB"], writes=["expnG"])
            qdf = qd[:].rearrange("p h t -> p (h t)")
            kif = ki[:].rearrange("p h t -> p (h t)")
            S.op("dve", lambda e: e.scalar_tensor_tensor(out=qdf, in0=W0a, scalar=128.0 ** -0.5, in1=lbuf[:], op0=ALU.mult, op1=ALU.mult),
                 reads=["W0a", "lbuf"], writes=["qd"])
            S.op("dve", lambda e: e.tensor_tensor(out=kif, in0=W0b, in1=expnG[:], op=ALU.mult), reads=["W0b", "expnG"], writes=["ki"])
            S.op("pool", lambda e: e.tensor_copy(out=q0[:, :, 0:64], in_=qd[:, :, 0:64]), reads=["qd"], writes=["q0"])
            S.op("pool", lambda e: e.tensor_copy(out=q1[:, :, 64:128], in_=qd[:, :, 64:128]), reads=["qd"], writes=["q1"])
            for h in range(4):
                S.op("pe", lambda e, h=h: e.matmul(PA[:, h * 128:(h + 1) * 128], lhsT=ki[:, h, :], rhs=qd[:, h, :], start=True, stop=True),
                     reads=["ki", "qd"], writes=["PA"])
            S.op("dve", lambda e: e.tensor_tensor(out=ATm[:], in0=PA.rearrange("p (h t) -> p h t", h=4),
                                                  in1=maskU[:].unsqueeze(1).to_broadcast([128, 4, 128]), op=ALU.mult),
                 reads=["PA", "maskU"], writes=["ATm"])
        W1v = W1.rearrange("p (h v) -> p h v", h=4)

        def kv_update(c):
            for h in range(4):
                S.op("pe", lambda e, h=h: e.matmul(W1v[:, h, :], lhsT=ktail[c * 64:(c + 1) * 64, h * 128:(h + 1) * 128],
                                                   rhs=v_bf[c * 64:(c + 1) * 64, h * 256:(h + 1) * 256], start=True, stop=True),
                     reads=["ktail", "v_bf"], writes=["W1"])
            for h in range(4):
                S.op("dve", lambda e, h=h: e.scalar_tensor_tensor(out=S_f[:, h, :], in0=S_f[:, h, :], scalar=dec[:, h, c:c + 1],
                                                                  in1=W1v[:, h, :], op0=ALU.mult, op1=ALU.add),
                     reads=["S_f", "dec", "W1"], writes=["S_f"])

        kv_update(0)
        if full:
            S.op("act", lambda e: e.activation(out=Sb1[:], in_=S_f[:], func=AF.Copy), reads=["S_f"], writes=["Sb1"])
            W0v = W0.rearrange("p (h v) -> p h v", h=4)
            for h in range(4):
                S.op("pe", lambda e, h=h: e.matmul(W0v[:, h, :], lhsT=ATm[:, h, :], rhs=v_bf[:, h * 256:(h + 1) * 256], start=True, stop=False),
                     reads=["ATm", "v_bf"], writes=["W0a" if h < 2 else "W0b"])
                S.op("pe", lambda e, h=h: e.matmul(W0v[:, h, :], lhsT=q0[:, h, :], rhs=Sb0[:, h, :], start=False, stop=False),
                     reads=["q0", "Sb0"], writes=["W0a" if h < 2 else "W0b"])
                S.op("pe", lambda e, h=h: e.matmul(W0v[:, h, :], lhsT=q1[:, h, :], rhs=Sb1[:, h, :], start=False, stop=True),
                     reads=["q1", "Sb1"], writes=["W0a" if h < 2 else "W0b"])
        kv_update(1)
        if not full:
            return
        S.op("act", lambda e: e.activation(out=Sb0[:], in_=S_f[:], func=AF.Copy), reads=["S_f"], writes=["Sb0"])
        t1v = t1_[:].rearrange("p (h v) -> p h v", h=4)
        t2v = t2_[:].rearrange("p (h v) -> p h v", h=4)
        for h in range(4):
            S.op("act", lambda e, h=h: e.activation(out=t1v[:, h, :], in_=W0v[:, h, :], func=AF.Square, accum_out=ss[:, 2 + h:3 + h]),
                 reads=["W0a" if h < 2 else "W0b"], writes=["F1", "ss"])
        S.op("dve", lambda e: e.tensor_scalar(out=ss[:, 2:6], in0=ss[:, 2:6], scalar1=1.0 / 256.0, scalar2=EPS, op0=ALU.mult, op1=ALU.add),
             reads=["ss"], writes=["ss"])
        S.op("pool", lambda e: e.tensor_tensor(out=rstd[:, 2:6], in0=ss[:, 2:6], in1=neghalf[:, 0:4], op=ALU.pow),
             reads=["ss", "neghalf"], writes=["rstd"])
        proj_tm(W1[:, 0:512], ["W1"], wr, "wr", 0, 512)
        proj_tm(W1[:, 512:1024], ["W1"], wr, "wr", 512, 512)
        S.op("act", lambda e: e.activation(out=t2_[:], in_=W1, func=AF.Sigmoid), reads=["W1"], writes=["F0"])
        S.op("dve", lambda e: e.tensor_tensor(out=t2_[:], in0=W1, in1=t2_[:], op=ALU.mult), reads=["W1", "F0"], writes=["F0"])
        S.op("pool", lambda e: e.tensor_tensor(out=t2_[:], in0=t2_[:], in1=gnwB[:], op=ALU.mult), reads=["F0", "gnwB"], writes=["F0"])
        for h in range(4):
            S.op("dve", lambda e, h=h: e.scalar_tensor_tensor(out=t1v[:, h, :], in0=W0v[:, h, :], scalar=rstd[:, 2 + h:3 + h],
                                                              in1=t2v[:, h, :], op0=ALU.mult, op1=ALU.mult),
                 reads=["W0a" if h < 2 else "W0b", "rstd", "F0"], writes=["F1"])
        proj_tm(W1[:, 0:512], ["W1"], wga, "wga", 0, 512)
        proj_tm(W1[:, 512:1024], ["W1"], wga, "wga", 512, 512)
        S.op("act", lambda e: e.activation(out=t2_[:], in_=W1, func=AF.Sigmoid), reads=["W1"], writes=["F0"])
        S.op("pool", lambda e: e.tensor_tensor(out=t1_[:], in0=t1_[:], in1=t2_[:], op=ALU.mult), reads=["F1", "F0"], writes=["F1"])
        capB = []
        S.cap = capB
        proj_tm(W2a, ["W2a"], wsu, "wsu", 0, 512)
        proj_tm(W2b, ["W2b"], wsu, "wsu", 512, 512)
        S.op("act", lambda e: e.activation(out=u_[:], in_=W2, func=AF.Gelu), reads=kW2, writes=["F2"])
        proj_tm(W2a, ["W2a"], wsv, "wsv", 0, 512)
        proj_tm(W2b, ["W2b"], wsv, "wsv", 512, 512)
        S.op("act", lambda e: e.activation(out=gv_[:], in_=W2, func=AF.Gelu), reads=kW2, writes=["F3"])
        for hf in range(2):
            S.op("dve", lambda e, hf=hf: e.bn_stats(out=bnst[:, hf, :], in_=gv_[:, hf * 512:(hf + 1) * 512]), reads=["F3"], writes=["bnst"])
        S.op("dve", lambda e: e.bn_aggr(out=mv[:], in_=bnst[:].rearrange("p a s -> p (a s)")), reads=["bnst"], writes=["mv"])
        S.op("dve", lambda e: e.tensor_scalar(out=ss[:, 6:7], in0=mv[:, 1:2], scalar1=EPS, scalar2=None, op0=ALU.add),
             reads=["mv"], writes=["ss_s"])
        S.op("pool", lambda e: e.tensor_tensor(out=rstd[:, 6:7], in0=ss[:, 6:7], in1=neghalf[:, 0:1], op=ALU.pow),
             reads=["ss_s", "neghalf"], writes=["rstd_s"])
        S.op("dve", lambda e: e.tensor_scalar(out=gv_[:], in0=gv_[:], scalar1=mv[:, 0:1], scalar2=rstd[:, 6:7], op0=ALU.subtract, op1=ALU.mult),
             reads=["F3", "mv", "rstd_s"], writes=["F3"])
        S.op("pool", lambda e: e.tensor_tensor(out=gv_[:], in0=gv_[:], in1=lnwB[:], op=ALU.mult), reads=["F3", "lnwB"], writes=["F3"])
        S.op("pool", lambda e: e.tensor_tensor(out=vln[:], in0=gv_[:], in1=lnbB[:], op=ALU.add), reads=["F3", "lnbB"], writes=["vln"])
        proj_tm(W2a, ["W2a"], wgb, "wgb", 0, 512)
        proj_tm(W2b, ["W2b"], wgb, "wgb", 512, 512)
        S.op("act", lambda e: e.activation(out=gv_[:], in_=W2, func=AF.Sigmoid), reads=kW2, writes=["F3"])
        for g in range(8):
            S.op("pe", lambda e, g=g: e.matmul(W2[:, g * 128:(g + 1) * 128], lhsT=WsT[:, g, :], rhs=vln[:, g * 128:(g + 1) * 128], start=True, stop=True),
                 reads=["WsT", "vln"], writes=kW2)
        for g in range(8):
            S.op("dve", lambda e, g=g: e.scalar_tensor_tensor(out=u_[:, g * 128:(g + 1) * 128], in0=W2[:, g * 128:(g + 1) * 128],
                                                              scalar=Bsgu[:, g:g + 1], in1=u_[:, g * 128:(g + 1) * 128], op0=ALU.add, op1=ALU.mult),
                 reads=kW2 + ["Bsgu", "F2"], writes=["F2"])
        S.op("pool", lambda e: e.tensor_tensor(out=u_[:], in0=u_[:], in1=gv_[:], op=ALU.mult), reads=["F2", "F3"], writes=["F2"])
        S.cap = None
        ia = ib = 0
        na, nb = len(capA), len(capB)
        while ia < na or ib < nb:
            if ib >= nb or (ia < na and ia * nb <= ib * na):
                S.op(*capA[ia]); ia += 1
            else:
                S.op(*capB[ib]); ib += 1
        S.op("dve", lambda e: e.tensor_tensor(out=vln[:], in0=t1_[:], in1=u_[:], op=ALU.add), reads=["F1", "F2"], writes=["vln"])

    def p2_tail(k):
        xk = xts[k % 2]
        kx = "xt%d" % (k % 2)
        row0 = k * 128
        PAb = PA.bitcast(BF16)
        for ch in range(8):
            S.op("pe", lambda e, ch=ch: e.transpose(PAb[:, ch * 128:(ch + 1) * 128], vln[:, ch * 128:(ch + 1) * 128], ident_b[:]),
                 reads=["vln", "ident_b"], writes=["PA"])
        S.op("act", lambda e: e.activation(out=mTb[:].rearrange("p c t -> p (c t)"), in_=PAb[:, 0:1024], func=AF.Copy), reads=["PA"], writes=["mT"])
        for hf in range(2):
            for ch in range(8):
                S.op("pe", lambda e, hf=hf, ch=ch: e.matmul(W1[:, hf * 512:(hf + 1) * 512], lhsT=mTb[:, ch, :], rhs=woutg[:, ch, hf * 512:(hf + 1) * 512],
                                                            start=(ch == 0), stop=(ch == 7)),
                     reads=["mT", "woutg"], writes=["W1"])
        S.op("dve", lambda e: e.tensor_tensor(out=xk[:], in0=W1, in1=xk[:], op=ALU.add), reads=["W1", kx], writes=[kx])
        S.op("sp", lambda e: e.dma_start(out=out_d[row0:row0 + 128, :], in_=xk[:]), reads=[kx], writes=[("out_d", row0 // 128)], dma_chan="c_out%d" % (k % 2))

    load_late()
    alrT1 = alloc("alrT1", [16, 128], F32)
    dec1 = alloc("dec1", [128, 4, 2], F32)
    ss1 = alloc("ss1", [128, 8], F32)
    rstd1 = alloc("rstd1", [128, 8], F32)
    F3h = Fs[3]
    p1buf = [
        dict(xt=xt[:], xn=Fs[0][:], hT=hT, v=v_bf[:], alrT=alrT, bufE=bufE[:], lbuf=lbuf[:], ktail=ktail[:], dec=dec, ss=ss, rstd=rstd,
             Ww=P01[:, :], Wx=P23[:, :], Bc=P23[:, 0:512], Bd=P23[:, 512:1024]),
        dict(xt=Fs[1][:], xn=Fs[2][:], hT=vln[:].rearrange("p (c t) -> p c t", c=8), v=F3h[:, 0:512].bitcast(BF16), alrT=alrT1, bufE=F3h[:, 512:1024],
             lbuf=expnG[:], ktail=qd[:].rearrange("p h t -> p (h t)"), dec=dec1, ss=ss1, rstd=rstd1,
             Ww=P45[:, :], Wx=P67[:, :], Bc=P67[:, 0:512], Bd=P67[:, 512:1024]),
    ]

    def p1_stages(x_ap, seg, sl):
        B = p1buf[sl]
        K = lambda n: "%s_%d" % (n, sl)
        fcol = flags[:, seg:seg + 1]
        xt_, xn_, hT_, v_, alrT_, bufE_, lbuf_, ktail_, dec_, ss_, rstd_ = (B[k] for k in ("xt", "xn", "hT", "v", "alrT", "bufE", "lbuf", "ktail", "dec", "ss", "rstd"))
        Ww, Wx, Bc, Bd = B["Ww"], B["Wx"], B["Bc"], B["Bd"]
        st = []

        def s0():
            S.op("sp", lambda e: e.dma_start(out=xt_, in_=x_ap), writes=[K("xt")], dma_chan="c_p1xt%d" % sl)
            S.op("act", lambda e: e.activation(out=xn_, in_=xt_, func=AF.Square, accum_out=ss_[:, 0:1]), reads=[K("xt")], writes=[K("xn"), K("ss")])
            S.op("pool", lambda e: e.tensor_scalar(out=ss_[:, 1:2], in0=ss_[:, 0:1], scalar1=1.0 / D, scalar2=EPS, op0=ALU.mult, op1=ALU.add),
                 reads=[K("ss")], writes=[K("ssb")])
            S.op("pool", lambda e: e.tensor_tensor(out=rstd_[:, 0:1], in0=ss_[:, 1:2], in1=neghalf[:, 0:1], op=ALU.pow), reads=[K("ssb"), "neghalf"], writes=[K("rstd")])
        st.append(s0)

        def s1():
            S.op("act", lambda e: e.activation(out=xn_, in_=xt_, func=AF.Copy, scale=rstd_[:, 0:1]), reads=[K("xt"), K("rstd")], writes=[K("xn")])
            for ch in range(8):
                S.op("pe", lambda e, ch=ch: e.transpose(Ww[:, ch * 128:(ch + 1) * 128], xn_[:, ch * 128:(ch + 1) * 128], ident_f[:]),
                     reads=[K("xn"), "ident_f"], writes=[K("Ww")])
        st.append(s1)

        def s2():
            for ch in range(8):
                eng = "dve" if ch < 4 else "act"
                if eng == "dve":
                    S.op("dve", lambda e, ch=ch: e.tensor_scalar(out=hT_[:, ch, :], in0=Ww[:, ch * 128:(ch + 1) * 128],
                                                                 scalar1=cols[:, 0, ch:ch + 1], scalar2=cols[:, 1, ch:ch + 1], op0=ALU.mult, op1=ALU.add),
                         reads=[K("Ww"), "cols"], writes=[(K("hT"), ch)])
                else:
                    S.op("act", lambda e, ch=ch: e.activation(out=hT_[:, ch, :], in_=Ww[:, ch * 128:(ch + 1) * 128], func=AF.Identity,
                                                              scale=cols[:, 0, ch:ch + 1], bias=cols[:, 1, ch:ch + 1]),
                         reads=[K("Ww"), "cols"], writes=[(K("hT"), ch)])
        st.append(s2)

        def s3():
            for ch in range(8):
                S.op("pe", lambda e, ch=ch: e.matmul(Bc, lhsT=hT_[:, ch, :], rhs=wk[:, ch, :], start=(ch == 0), stop=(ch == 7)), reads=[(K("hT"), ch), "wk"], writes=[K("Bc")])
            for ch in range(8):
                S.op("pe", lambda e, ch=ch: e.matmul(Bd[0:16, 0:128], lhsT=walr[:, ch, :], rhs=hT_[:, ch, :], start=(ch == 0), stop=(ch == 7)),
                     reads=[(K("hT"), ch), "walr"], writes=[K("Bd")])
            for hf in range(2):
                for ch in range(8):
                    S.op("pe", lambda e, ch=ch, hf=hf: e.matmul(Ww[:, hf * 512:(hf + 1) * 512], lhsT=hT_[:, ch, :], rhs=wv[:, ch, hf * 512:(hf + 1) * 512],
                                                               start=(ch == 0), stop=(ch == 7)),
                         reads=[(K("hT"), ch), "wv"], writes=[K("Ww")])
            S.op("act", lambda e: e.activation(out=alrT_[:], in_=Bd[0:16, 0:128], func=AF.Copy), reads=[K("Bd")], writes=[K("alrT")])
        st.append(s3)

        def s4():
            S.op("pe", lambda e: e.matmul(Bd, lhsT=alrT_[:], rhs=wup_f[:], start=True, stop=False), reads=[K("alrT"), "wup_f"], writes=[K("Bd")])
            S.op("pe", lambda e: e.matmul(Bd, lhsT=ones_f[0:1, :], rhs=balpha_f[:], start=False, stop=True), reads=["ones_f", "balpha_f"], writes=[K("Bd")])
            S.op("dve", lambda e: e.tensor_scalar(out=v_, in0=Ww, scalar1=fcol, scalar2=None, op0=ALU.mult), reads=[K("Ww"), "flags"], writes=[K("v")])
            S.op("act", lambda e: e.activation(out=bufE_, in_=Bd, func=AF.Exp, scale=-1.0), reads=[K("Bd")], writes=[K("bufE")])
        st.append(s4)

        def s5():
            S.op("act", lambda e: e.activation(out=lbuf_, in_=bufE_, func=AF.Ln, bias=1.0, scale=1.0), reads=[K("bufE")], writes=[K("lbuf")])
            S.op("pe", lambda e: e.matmul(Bd, lhsT=Rm[:], rhs=lbuf_, start=True, stop=True), reads=["Rm", K("lbuf")], writes=[K("Bd")])
            for h in range(4):
                S.op("pe", lambda e, h=h: e.matmul(Ww[:, h * 128:(h + 1) * 128], lhsT=lbuf_[:, h * 128:(h + 1) * 128], rhs=Lm[:], start=True, stop=True),
                     reads=[K("lbuf"), "Lm", K("v")], writes=[K("Ww")])
        st.append(s5)

        def s6():
            S.op("act", lambda e: e.activation(out=bufE_, in_=Bd, func=AF.Exp), reads=[K("Bd")], writes=[K("bufE")])
            Wv4 = Ww[:, 0:512].rearrange("p (h t) -> p h t", h=4)
            S.op("act", lambda e: e.activation(out=dec_[:], in_=Wv4[:, :, 63:128:64], func=AF.Exp), reads=[K("Ww")], writes=[K("dec")])
            S.op("dve", lambda e: e.tensor_tensor(out=ktail_, in0=Bc, in1=bufE_, op=ALU.mult), reads=[K("Bc"), K("bufE")], writes=[K("ktail")])
        st.append(s6)

        def kv(c, Wdst, wkey):
            Wd = Wdst.rearrange("p (h v) -> p h v", h=4)
            for h in range(4):
                S.op("pe", lambda e, h=h: e.matmul(Wd[:, h, :], lhsT=ktail_[c * 64:(c + 1) * 64, h * 128:(h + 1) * 128],
                                                   rhs=v_[c * 64:(c + 1) * 64, h * 256:(h + 1) * 256], start=True, stop=True),
                     reads=[K("ktail"), K("v")] + ([K("dec")] if wkey == "Ww" else []), writes=[K(wkey)] if wkey != "Wx" else [K("Bc"), K("Bd")])
            for h in range(4):
                S.op("dve", lambda e, h=h: e.scalar_tensor_tensor(out=S_f[:, h, :], in0=S_f[:, h, :], scalar=dec_[:, h, c:c + 1],
                                                                  in1=Wd[:, h, :], op0=ALU.mult, op1=ALU.add),
                     reads=[("S_f", h), K("dec")] + ([K(wkey)] if wkey != "Wx" else [K("Bc"), K("Bd")]), writes=[("S_f", h)])
        st.append(lambda: kv(0, Wx, "Wx"))
        st.append(lambda: kv(1, Ww, "Ww"))
        return st

    tiles = []
    for seg in range(3):
        for ti in range(NT if stage not in (1, 3, 4) else NDBG):
            tiles.append((xpre[seg, ti * 128:(ti + 1) * 128, :], seg))
    SKW = 4
    stg = [p1_stages(x_ap, seg, k % 2) for k, (x_ap, seg) in enumerate(tiles)]
    nst = len(stg[0])
    for step in range(len(tiles) * SKW + nst):
        for k in range(len(tiles)):
            sidx = step - k * SKW
            if 0 <= sidx < nst:
                stg[k][sidx]()
        if step % 2 == 0:
            issue_cvt(1)
    S.barrier()
    S.op("act", lambda e: e.activation(out=Sb0[:], in_=S_f[:], func=AF.Copy), writes=["Sb0"])
    issue_cvt(1000)
    NT2 = NT if stage not in (1, 3, 4) else NDBG
    p2_front(0)
    for ti in range(NT2):
        p2_body(ti)
        if ti + 1 < NT2:
            p2_front(ti + 1)
        p2_tail(ti)

    if stage <= 2:
        S.emit(final_waits=["c_out0", "c_out1"])
        return nc

    S.barrier()
    M.off = mark_p3
    h2T = alloc("h2T", [128, 8, 2048], BF16)
    idx1T = alloc("idx1T", [128, 2048], BF16)
    idx2T = alloc("idx2T", [128, 2048], BF16)
    gT = alloc("gT", [128, 2048], BF16)
    gate2B = alloc("gate2B", [128, D], F32)
    finwB = alloc("finwB", [128, D], F32)
    mark_p3t = M.off
    wqb = alloc("wqb", [128, 8, 2048], BF16)
    k1T = alloc("k1T", [128, 128], BF16)
    k2T = alloc("k2T", [128, 128], BF16)
    xt2 = alloc("xt2", [128, D], F32)
    xn2 = alloc("xn2", [128, D], F32)
    qT = alloc("qT", [128, 16, 128], BF16)
    sc = alloc("sc", [128, 16, 128], F32)
    work = alloc("work", [128, 256], F32)
    vtop = alloc("vtop", [128, 16, 16], F32)
    iu = alloc("iu", [128, 16, 16], U32)
    itf = alloc("itf", [128, 16, 16], F32)
    cand = alloc("cand", [128, 8, 256], F32)
    ts = alloc("ts", [128, 8, 16], F32)
    posu = alloc("posu", [128, 8, 16], U32)
    k1u = alloc("k1u", [128, 8, 16], U32)
    k2u = alloc("k2u", [128, 8, 16], U32)
    k1f = alloc("k1f", [128, 8, 16], F32)
    k2f = alloc("k2f", [128, 8, 16], F32)
    ee = alloc("ee", [128, 8, 16], F32)
    zz = alloc("zz", [128, 8], F32)
    oh = alloc("oh", [128, 128, 16], F32)
    idx_tm = alloc("idx_tm", [128, 3, 128], F32)
    iota16 = alloc("iota16", [128, 16], F32)
    diag = alloc("diag", [128, 128], F32)
    ss2 = alloc("ss2", [128, 4], F32)
    rstd2 = alloc("rstd2", [128, 4], F32)

    S.op("act", lambda e: e.dma_start(out=finwB[:], in_=rowv_d[0:1, :].to_broadcast([128, D])), writes=["finwB"], dma_chan="c_finw")
    wq_v = wq_d.rearrange("(c p) e -> p c e", p=128)
    for ch in range(8):
        S.op("pool", lambda e, ch=ch: e.dma_start(out=wqb[:, ch, :], in_=wq_v[:, ch, :]), writes=["wqb"], dma_chan="c_wqb")
    S.op("pool", lambda e: e.dma_start(out=k1T[:], in_=k1T_d[:, :]), writes=["k1T"], dma_chan="c_k1T")
    S.op("pool", lambda e: e.dma_start(out=k2T[:], in_=k2T_d[:, :]), writes=["k2T"], dma_chan="c_k2T")
    S.op("dve", lambda e: e.tensor_copy(out=iota16[:], in_=iota_f[:, 0:16]), reads=["iota_f"], writes=["iota16"])
    for ch in range(8):
        S.op("dve", lambda e, ch=ch: e.tensor_scalar(out=diag[:], in0=ident_f[:], scalar1=cols[:, 4, ch:ch + 1], scalar2=None, op0=ALU.mult),
             reads=["ident_f", "cols"], writes=["diag"])
        S.op("pe", lambda e: e.matmul(PA[:, 0:128], lhsT=ones_f[:], rhs=diag[:], start=True, stop=True), reads=["ones_f", "diag"], writes=["PA"])
        S.op("act", lambda e, ch=ch: e.activation(out=gate2B[:, ch * 128:(ch + 1) * 128], in_=PA[:, 0:128], func=AF.Copy), reads=["PA"], writes=["gate2B"])

    W01 = [W0, W1]
    xt2b = [xt2, alloc("xt2b", [128, D], F32)]
    xn2b = [xn2, alloc("xn2b", [128, D], F32)]
    qTb = [qT, alloc("qTb", [128, 16, 128], BF16)]
    scb = [sc, alloc("scb", [128, 16, 128], F32)]
    work16 = alloc("work16", [128, 16, 128], F32)
    oh2 = alloc("oh2", [128, 128, 16], F32)
    ohs = [oh, oh2]
    Ireps = [alloc("Irep%d" % i, [128, 16, 128], F32) for i in range(2)]
    NT25 = NT if stage != 3 else NDBG

    def front(ti):
        p = ti % 2
        xt2_, xn2_, qT_, sc_ = xt2b[p], xn2b[p], qTb[p], scb[p]
        kx, kn, kq, ks = "xt2_%d" % p, "xn2_%d" % p, "qT_%d" % p, "sc_%d" % p
        S.op("sp", lambda e: e.dma_start(out=xt2_[:], in_=out_d[ti * 128:(ti + 1) * 128, :]), reads=[("out_d", ti)], writes=[kx], dma_chan="c_xt2_%d" % p)
        S.op("act", lambda e: e.activation(out=xn2_[:], in_=xt2_[:], func=AF.Square, accum_out=ss2[:, p:p + 1]), reads=[kx], writes=[kn, ("ss2", p)])
        S.op("pool", lambda e: e.tensor_scalar(out=ss2[:, 2 + p:3 + p], in0=ss2[:, p:p + 1], scalar1=1.0 / D, scalar2=EPS, op0=ALU.mult, op1=ALU.add),
             reads=[("ss2", p)], writes=[("ss2b", p)])
        S.op("pool", lambda e: e.tensor_tensor(out=rstd2[:, p:p + 1], in0=ss2[:, 2 + p:3 + p], in1=neghalf[:, 0:1], op=ALU.pow), reads=[("ss2b", p), "neghalf"], writes=[("rstd2", p)])
        S.op("act", lambda e: e.activation(out=xn2_[:], in_=xt2_[:], func=AF.Copy, scale=rstd2[:, p:p + 1]), reads=[kx, ("rstd2", p)], writes=[kn])
        for ch in range(8):
            S.op("pe", lambda e, ch=ch: e.transpose(W2[:, ch * 128:(ch + 1) * 128], xn2_[:, ch * 128:(ch + 1) * 128], ident_f[:]),
                 reads=[kn, "ident_f"], writes=kW2)
        for ch in range(8):
            S.op("act", lambda e, ch=ch: e.activation(out=h2T[:, ch, ti * 128:(ti + 1) * 128], in_=W2[:, ch * 128:(ch + 1) * 128], func=AF.Identity,
                                                      scale=cols[:, 2, ch:ch + 1], bias=cols[:, 3, ch:ch + 1]),
                 reads=[kW2[ch // 4], "cols"], writes=[("h2T", ti, ch)])
        for blk in range(16):
            dst = W01[blk // 8][:, (blk % 8) * 128:(blk % 8 + 1) * 128]
            for ch in range(8):
                S.op("pe", lambda e, blk=blk, ch=ch, dst=dst: e.matmul(dst, lhsT=wqb[:, ch, blk * 128:(blk + 1) * 128], rhs=h2T[:, ch, ti * 128:(ti + 1) * 128],
                                                                      start=(ch == 0), stop=(ch == 7)),
                     reads=["wqb", ("h2T", ti, ch)], writes=["W%d" % (blk // 8)])
        qTf = qT_[:].rearrange("p b t -> p (b t)")
        S.op("act", lambda e: e.activation(out=qTf[:, 0:1024], in_=W0, func=AF.Copy), reads=["W0"], writes=[kq])
        S.op("act", lambda e: e.activation(out=qTf[:, 1024:2048], in_=W1, func=AF.Copy), reads=["W1"], writes=[kq])
        for blk in range(16):
            dst = W01[blk // 8][:, (blk % 8) * 128:(blk % 8 + 1) * 128]
            kT_ = k1T if blk % 2 == 0 else k2T
            S.op("pe", lambda e, blk=blk, dst=dst, kT_=kT_: e.matmul(dst, lhsT=qT_[:, blk, :], rhs=kT_[:], start=True, stop=True),
                 reads=[kq, "k1T", "k2T"], writes=["W%d" % (blk // 8)])
        scf = sc_[:].rearrange("p b k -> p (b k)")
        S.op("act", lambda e: e.activation(out=scf[:, 0:1024], in_=W0, func=AF.Copy), reads=["W0"], writes=[ks])
        S.op("act", lambda e: e.activation(out=scf[:, 1024:2048], in_=W1, func=AF.Copy), reads=["W1"], writes=[ks])

    def top16_multi(items):
        for (src, sk, n, vd, idd, wk_, tg) in items:
            S.op("dve", lambda e, src=src, vd=vd: e.max(out=vd[:, 0:8], in_=src), reads=[sk], writes=[("vt", tg)])
        for (src, sk, n, vd, idd, wk_, tg) in items:
            S.op("dve", lambda e, src=src, vd=vd, idd=idd: e.max_index(out=idd[:, 0:8], in_max=vd[:, 0:8], in_values=src), reads=[sk, ("vt", tg)], writes=[("it", tg)])
        for (src, sk, n, vd, idd, wk_, tg) in items:
            S.op("dve", lambda e, src=src, vd=vd, wk_=wk_: e.match_replace(out=wk_, in_to_replace=vd[:, 0:8], in_values=src, imm_value=-1e30),
                 reads=[sk, ("vt", tg)], writes=[("work", tg)])
        for (src, sk, n, vd, idd, wk_, tg) in items:
            S.op("dve", lambda e, vd=vd, wk_=wk_: e.max(out=vd[:, 8:16], in_=wk_), reads=[("work", tg)], writes=[("vt", tg)])
        for (src, sk, n, vd, idd, wk_, tg) in items:
            S.op("dve", lambda e, vd=vd, idd=idd, wk_=wk_: e.max_index(out=idd[:, 8:16], in_max=vd[:, 8:16], in_values=wk_), reads=[("work", tg), ("vt", tg)], writes=[("it", tg)])

    def back(ti, part):
        p = ti % 2
        sc_ = scb[p]
        ks = "sc_%d" % p
        vkeys = [("vt", t) for t in range(16)]
        ikeys_ = [("it", t) for t in range(16)]
        if part == 0:
            top16_multi([(sc_[:, blk, :], ks, 128, vtop[:, blk, :], iu[:, blk, :], work16[:, blk, :], blk) for blk in range(16)])
            S.op("dve", lambda e: e.tensor_copy(out=itf[:], in_=iu[:]), reads=ikeys_, writes=["itf"])
            for which in (0, 1):
                Irep = Ireps[which]
                for j in range(16):
                    S.op("act", lambda e, j=j, which=which, Irep=Irep: e.activation(out=Irep[:, j, :].rearrange("p (h k) -> p h k", h=8),
                                                                                    in_=itf[:, which::2, j:j + 1].to_broadcast([128, 8, 16]), func=AF.Copy),
                         reads=["itf"], writes=[("Irep", which, j)])
            return
        for h in range(8):
            cv = cand[:, h, :].rearrange("p (a b) -> p a b", a=16)
            S.op("dve", lambda e, h=h, cv=cv: e.tensor_tensor(out=cv, in0=vtop[:, 2 * h, :].unsqueeze(2).to_broadcast([128, 16, 16]),
                                                              in1=vtop[:, 2 * h + 1, :].unsqueeze(1).to_broadcast([128, 16, 16]), op=ALU.add),
                 reads=[("vt", 2 * h), ("vt", 2 * h + 1)], writes=[("cand", h)])
        w8 = work16[:].rearrange("p (h a) k -> p h (a k)", h=8)
        top16_multi([(cand[:, h, :], ("cand", h), 256, ts[:, h, :], posu[:, h, :], w8[:, h, :], 100 + h) for h in range(8)])
        tkeys = [("vt", 100 + h) for h in range(8)]
        pkeys = [("it", 100 + h) for h in range(8)]
        S.op("dve", lambda e: e.tensor_tensor(out=ee[:], in0=ts[:], in1=ts[:, :, 0:1].to_broadcast([128, 8, 16]), op=ALU.subtract),
             reads=tkeys, writes=["ee"])
        S.op("act", lambda e: e.activation(out=ee[:], in_=ee[:], func=AF.Exp), reads=["ee"], writes=["ee"])
        S.op("dve", lambda e: e.tensor_single_scalar(out=k1u[:], in_=posu[:], scalar=4, op=ALU.logical_shift_right), reads=pkeys, writes=["k1u"])
        S.op("dve", lambda e: e.tensor_single_scalar(out=k2u[:], in_=posu[:], scalar=15, op=ALU.bitwise_and), reads=pkeys, writes=["k2u"])
        S.op("dve", lambda e: e.tensor_copy(out=k1f[:], in_=k1u[:]), reads=["k1u"], writes=["k1f"])
        S.op("dve", lambda e: e.tensor_copy(out=k2f[:], in_=k2u[:]), reads=["k2u"], writes=["k2f"])
        for which, kf, kfk in ((0, k1f, "k1f"), (1, k2f, "k2f")):
            Irep = Ireps[which]
            pr = ohs[which][:].rearrange("p a b -> p (a b)").rearrange("p (j m) -> p j m", j=16)
            kff = kf[:].rearrange("p h k -> p (h k)")
            for j in range(16):
                S.op("dve", lambda e, j=j, pr=pr, kff=kff, Irep=Irep: e.scalar_tensor_tensor(out=pr[:, j, :], in0=kff, scalar=float(j), in1=Irep[:, j, :],
                                                                                             op0=ALU.is_equal, op1=ALU.mult),
                     reads=[kfk, ("Irep", which, j)], writes=[("pr", which, j)])
            S.op("dve", lambda e, pr=pr: e.tensor_tensor(out=pr[:, 0:8, :], in0=pr[:, 0:8, :], in1=pr[:, 8:16, :], op=ALU.add),
                 reads=[("pr", which, j) for j in range(16)], writes=[("prs", which)])
            S.op("dve", lambda e, pr=pr: e.tensor_tensor(out=pr[:, 0:4, :], in0=pr[:, 0:4, :], in1=pr[:, 4:8, :], op=ALU.add),
                 reads=[("prs", which)], writes=[("prs", which)])
            S.op("dve", lambda e, pr=pr: e.tensor_tensor(out=pr[:, 0:2, :], in0=pr[:, 0:2, :], in1=pr[:, 2:4, :], op=ALU.add),
                 reads=[("prs", which)], writes=[("prs", which)])
            S.op("dve", lambda e, pr=pr, which=which: e.tensor_tensor(out=idx_tm[:, which, :], in0=pr[:, 0, :], in1=pr[:, 1, :], op=ALU.add),
                 reads=[("prs", which)], writes=[("idx_tm", which)])
        S.op("dve", lambda e: e.tensor_reduce(out=zz[:], in_=ee[:], axis=AX.X, op=ALU.add), reads=["ee"], writes=["zz"])
        S.op("dve", lambda e: e.reciprocal(out=zz[:], in_=zz[:]), reads=["zz"], writes=["zz"])
        S.op("dve", lambda e: e.tensor_tensor(out=idx_tm[:, 2, :].rearrange("p (h k) -> p h k", h=8), in0=ee[:],
                                              in1=zz[:].unsqueeze(2).to_broadcast([128, 8, 16]), op=ALU.mult),
             reads=["ee", "zz"], writes=[("idx_tm", 2)])
        for a, dstT in ((0, idx1T), (1, idx2T), (2, gT)):
            S.op("pe", lambda e, a=a: e.transpose(PA[:, a * 128:(a + 1) * 128], idx_tm[:, a, :], ident_f[:]), reads=[("idx_tm", a), "ident_f"], writes=["PA"])
        for a, dstT in ((0, idx1T), (1, idx2T), (2, gT)):
            S.op("act", lambda e, a=a, dstT=dstT: e.activation(out=dstT[:, ti * 128:(ti + 1) * 128], in_=PA[:, a * 128:(a + 1) * 128], func=AF.Copy),
                 reads=["PA"], writes=[("idxT", ti)])

    front(0)
    for ti in range(NT25):
        back(ti, 0)
        if ti + 1 < NT25:
            front(ti + 1)
        back(ti, 1)

    if stage == 3:
        dbg = dram("dbg", [128, 3, 128 * NDBG], kind="ExternalOutput")
        for a, dstT in ((0, idx1T), (1, idx2T), (2, gT)):
            S.op("sp", lambda e, a=a, dstT=dstT: e.dma_start(out=dbg[:, a, :], in_=dstT[:, 0:128 * NDBG]), reads=[("idxT", t) for t in range(NDBG)], writes=["dbg"], dma_chan="c_dbg")
        S.emit(final_waits=["c_out0", "c_out1", "c_dbg"])
        return nc

    S.barrier()
    M.off = mark_p3t
    TT = 256
    NB = 4
    G = alloc("G", [128, 128, TT], BF16)
    NBUF = 3
    dbuf = [alloc("dbuf%d" % i, [128, NB, 8, 128], BF16) for i in range(NBUF)]
    ubuf = [alloc("ubuf%d" % i, [128, NB, D], BF16) for i in range(NBUF)]
    SBT = 8
    p2oh = [alloc("p2oh%d" % i, [128, SBT, 128], BF16) for i in range(2)]
    p1t = [alloc("p1t%d" % i, [128, SBT, 128], BF16) for i in range(2)]
    p1w = [alloc("p1w%d" % i, [128, SBT, 128], BF16) for i in range(2)]
    Ab = [alloc("Ab%d" % i, [128, TT], BF16) for i in range(4)]
    Wb = [alloc("Wb%d" % i, [128, TT], BF16) for i in range(4)]
    x3 = [alloc("x3_%d" % i, [128, D], F32) for i in range(2)]
    y3 = [alloc("y3_%d" % i, [128, D], F32) for i in range(2)]
    ss3 = alloc("ss3", [128, 4], F32)
    rstd3 = alloc("rstd3", [128, 4], F32)
    print("SBUF used (P3):", M.off)
    PAB = [PA, PB]
    nwd = 0
    for T in range(2048 // TT if stage != 4 else 1):
        t0 = T * TT
        ikeys = [("idxT", t) for t in range(2 * T, 2 * T + 2)]
        W2h = [W2a, W2b]
        for sbi in range(TT // SBT):
            ts0 = t0 + sbi * SBT
            q = sbi % 2
            p2o, p1t_, p1w_ = p2oh[q], p1t[q], p1w[q]
            for tl in range(SBT):
                tk = ts0 + tl
                S.op("dve", lambda e, tk=tk, tl=tl, p2o=p2o: e.tensor_scalar(out=p2o[:, tl, :], in0=iota_b[:], scalar1=idx2T[:, tk:tk + 1], scalar2=None, op0=ALU.is_equal),
                     reads=ikeys + ["iota_b"], writes=[("p2oh", q, tl)])
                S.op("dve", lambda e, tk=tk, tl=tl, p1w_=p1w_: e.tensor_scalar(out=p1w_[:, tl, :], in0=iota_b[:], scalar1=idx1T[:, tk:tk + 1], scalar2=gT[:, tk:tk + 1],
                                                                             op0=ALU.is_equal, op1=ALU.mult),
                     reads=ikeys + ["iota_b"], writes=[("p1w", q, tl)])
            for grp in range(SBT // 4):
                hb = (sbi * (SBT // 4) + grp) % 2
                for tl in range(4):
                    tloc = grp * 4 + tl
                    S.op("pe", lambda e, tl=tl, tloc=tloc, hb=hb, p1w_=p1w_, p2o=p2o: e.matmul(W2h[hb][:, tl * 128:(tl + 1) * 128], lhsT=p1w_[:, tloc, :], rhs=p2o[:, tloc, :], start=True, stop=True),
                         reads=[("p1w", q, tloc), ("p2oh", q, tloc)], writes=[kW2[hb]])
                tg = sbi * SBT + grp * 4
                S.op("act", lambda e, tg=tg, hb=hb: e.activation(out=G[:, :, tg:tg + 4],
                                                                 in_=W2h[hb].rearrange("p (t i) -> p i t", t=4), func=AF.Copy),
                     reads=[kW2[hb]], writes=["G"])
        SK = 2
        NSL = 4
        Aps = [PA[:, 0:TT], PB[:, 0:TT]]
        binfo = {}
        for step in range(128 + SK):
            if step < 128:
                i2 = step
                j = i2 % NB
                if j == 0:
                    b = nwd % NBUF
                    nwd += 1
                    db_, ub_ = dbuf[b], ubuf[b]
                    S.op("sp", lambda e, i2=i2, db_=db_: e.dma_start(out=db_[:], in_=scr_down[:, i2:i2 + NB, :, :]), writes=["dbuf%d" % b], dma_chan="c_dbuf%d" % b)
                    S.op("sp", lambda e, i2=i2, ub_=ub_: e.dma_start(out=ub_[:], in_=scr_up[:, i2:i2 + NB, :]), writes=["ubuf%d" % b], dma_chan="c_ubuf%d" % b)
                binfo[i2] = (b, ub_, j)
                pp = i2 % NSL
                pq = i2 % 2
                pa = Aps[pq]
                for ch in range(8):
                    S.op("pe", lambda e, ch=ch, j=j, db_=db_, pa=pa, t0=t0: e.matmul(pa, lhsT=db_[:, j, ch, :], rhs=h2T[:, ch, t0:t0 + TT], start=(ch == 0), stop=(ch == 7)),
                         reads=["dbuf%d" % b, ("h2T", 2 * T), ("h2T", 2 * T + 1)], writes=["PAB%d" % pq])
                ab, wb_ = Ab[pp], Wb[pp]
                S.op("act", lambda e, ab=ab, pa=pa: e.activation(out=ab[:], in_=pa, func=AF.Gelu), reads=["PAB%d" % pq], writes=["Ab%d" % pp])
                S.op("pool", lambda e, ab=ab, wb_=wb_, i2=i2: e.tensor_tensor(out=wb_[:], in0=ab[:], in1=G[:, i2, :], op=ALU.mult),
                     reads=["Ab%d" % pp, "G"], writes=["Wb%d" % pp])
            if step >= SK:
                i2 = step - SK
                b2, ub2, j2 = binfo[i2]
                pp = i2 % NSL
                wb_ = Wb[pp]
                for tt in range(2):
                    for hf in range(2):
                        S.op("pe", lambda e, tt=tt, hf=hf, wb_=wb_, ub2=ub2, j2=j2, i2=i2: e.matmul(W01[tt][:, hf * 512:(hf + 1) * 512], lhsT=wb_[:, tt * 128:(tt + 1) * 128],
                                                                                                  rhs=ub2[:, j2, hf * 512:(hf + 1) * 512], start=(i2 == 0), stop=(i2 == 127)),
                             reads=["Wb%d" % pp, "ubuf%d" % b2], writes=["W%d" % tt])
        for tt in range(2):
            r0 = t0 + tt * 128
            okey = ("out_d", r0 // 128)
            xx, yy = x3[tt], y3[tt]
            S.op("sp", lambda e, r0=r0, xx=xx: e.dma_start(out=xx[:], in_=out_d[r0:r0 + 128, :]), reads=[okey], writes=["x3_%d" % tt], dma_chan="c_x3_%d" % tt)
            S.op("dve", lambda e, tt=tt, yy=yy: e.tensor_tensor(out=yy[:], in0=W01[tt], in1=gate2B[:], op=ALU.mult), reads=["W%d" % tt, "gate2B"], writes=["y3_%d" % tt])
            S.op("pool", lambda e, xx=xx, yy=yy: e.tensor_tensor(out=xx[:], in0=xx[:], in1=yy[:], op=ALU.add), reads=["x3_%d" % tt, "y3_%d" % tt], writes=["x3_%d" % tt])
            S.op("act", lambda e, xx=xx, yy=yy, tt=tt: e.activation(out=yy[:], in_=xx[:], func=AF.Square, accum_out=ss3[:, tt:tt + 1]),
                 reads=["x3_%d" % tt], writes=["y3_%d" % tt, "ss3"])
            S.op("dve", lambda e, tt=tt: e.tensor_scalar(out=ss3[:, 2 + tt:3 + tt], in0=ss3[:, tt:tt + 1], scalar1=1.0 / D, scalar2=EPS, op0=ALU.mult, op1=ALU.add),
                 reads=["ss3"], writes=["ss3"])
            S.op("pool", lambda e, tt=tt: e.tensor_tensor(out=rstd3[:, tt:tt + 1], in0=ss3[:, 2 + tt:3 + tt], in1=neghalf[:, 0:1], op=ALU.pow),
                 reads=["ss3", "neghalf"], writes=["rstd3"])
            S.op("dve", lambda e, xx=xx, yy=yy, tt=tt: e.scalar_tensor_tensor(out=yy[:], in0=xx[:], scalar=rstd3[:, tt:tt + 1], in1=finwB[:], op0=ALU.mult, op1=ALU.mult),
                 reads=["x3_%d" % tt, "rstd3", "finwB"], writes=["y3_%d" % tt])
            S.op("sp", lambda e, r0=r0, yy=yy: e.dma_start(out=out_d[r0:r0 + 128, :], in_=yy[:]), reads=["y3_%d" % tt], writes=[okey], dma_chan="c_fin%d" % tt)
    S.emit(final_waits=["c_out0", "c_out1", "c_fin0", "c_fin1"])
    return nc


def host_inputs(inputs):
    x = np.asarray(inputs["x"], np.float32)
    f = lambda k: np.asarray(inputs[k], np.float32)
    c = f("c")
    shared = {
        "w_ada": np.ascontiguousarray(f("w_ada")[0]),
        "b_ada": np.ascontiguousarray(f("b_ada")[0][None, :]),
        "w_in": np.ascontiguousarray(f("w_in")[0]),
        "w_alpha_up": np.ascontiguousarray(f("w_alpha_up")[0]),
        "b_alpha": np.ascontiguousarray(f("b_alpha")[0][None, :]),
        "sgu_wT": np.ascontiguousarray(f("sgu_w")[0].transpose(0, 2, 1)),
        "sgu_bT": np.ascontiguousarray(f("sgu_b")[0].T),
        "w_out": np.ascontiguousarray(f("w_out")[0]),
        "peer_w_q": np.ascontiguousarray(f("peer_w_q")[0]),
        "keys1T": np.ascontiguousarray(f("peer_keys1")[0].T),
        "keys2T": np.ascontiguousarray(f("peer_keys2")[0].T),
        "downT": np.ascontiguousarray(f("peer_down")[0].reshape(128, 128, 8, 128).transpose(3, 1, 2, 0)),
        "up": np.ascontiguousarray(f("peer_up")[0].reshape(128, 128, D)),
        "colv": np.ascontiguousarray(np.stack([f("norm1_w")[0].reshape(8, 128).T, f("norm2_w")[0].reshape(8, 128).T], axis=1)),
        "rowv": np.ascontiguousarray(np.stack([f("final_norm_w"), f("gla_norm_w")[0], f("sgu_ln_w")[0], f("sgu_ln_b")[0]], axis=0)),
    }
    maps = []
    for i in range(8):
        b, j = i // 4, i % 4
        xpre = np.zeros((3, 2048, D), np.float32)
        flags = np.zeros((128, 4), np.float32)
        flags[:, 3] = 1.0
        for s in range(3):
            qidx = s - (3 - j)
            if qidx >= 0:
                xpre[s] = x[b, qidx * 2048:(qidx + 1) * 2048]
                flags[:, s] = 1.0
        m = dict(shared)
        m["xpre"] = xpre
        m["xown"] = np.ascontiguousarray(x[b, j * 2048:(j + 1) * 2048])
        m["flags"] = flags
        m["cT"] = np.ascontiguousarray(c[b].reshape(8, 128).T)
        maps.append(m)
    return maps


_NC_CACHE = {}


def kernel(**inputs):
    maps = host_inputs(inputs)
    if "nc" not in _NC_CACHE:
        _NC_CACHE["nc"] = build()
    nc = _NC_CACHE["nc"]
    res = run_bass_kernel_spmd(nc, maps, core_ids=list(range(8)))
    out = np.zeros((2, 8192, D), np.float32)
    for i in range(8):
        b, j = i // 4, i % 4
        out[b, j * 2048:(j + 1) * 2048] = res.results[i]["out"]
    return out
```

```python
import contextlib
import numpy as np
import concourse.bass as bass
import concourse.mybir as mybir
from concourse.bass_utils import run_bass_kernel_spmd

F32 = mybir.dt.float32
BF16 = mybir.dt.bfloat16
U32 = mybir.dt.uint32
AF = mybir.ActivationFunctionType
ALU = mybir.AluOpType
AX = mybir.AxisListType

ENGS = ("pe", "act", "dve", "pool", "sp")
EPS = 1e-6
NT = 16
import os
NDBG = int(os.environ.get('NDBG', '1'))
D = 1024
IN_SIZES = (512, 512, 1024, 1024, 16, 1024, 1024, 1024, 1024)
IN_OFF = [int(v) for v in np.cumsum((0,) + IN_SIZES)]
OQ, OK_, OV, OR, OALR, OSU, OSV, OGA, OGB = IN_OFF[:9]


class _Op:
    __slots__ = ("eng", "fn", "waits", "sig", "is_dma", "chan")


class Sched:
    def __init__(self, nc):
        self.nc = nc
        self.q = {e: [] for e in ENGS}
        self.chan_count = {}
        self.last_w = {}
        self.readers = {}
        self.cap = None

    def _add_wait(self, op, tok):
        if tok is None:
            return
        if tok[0] == "e":
            if tok[1] == op.eng and tok[1] == "pe":
                return
            if tok[2].sig is None:
                tok[2].sig = 1
        op.waits.append(tok)

    def op(self, eng, fn, reads=(), writes=(), dma_chan=None):
        if self.cap is not None:
            self.cap.append((eng, fn, tuple(reads), tuple(writes), dma_chan))
            return None
        o = _Op()
        o.eng = eng
        o.fn = fn
        o.waits = []
        o.sig = None
        o.is_dma = dma_chan is not None
        o.chan = dma_chan
        for k in reads:
            self._add_wait(o, self.last_w.get(k))
        for k in writes:
            self._add_wait(o, self.last_w.get(k))
            lastr = {}
            for t in self.readers.get(k, ()):
                if t[0] == "e":
                    lastr[t[1]] = t
                else:
                    self._add_wait(o, t)
            for t in lastr.values():
                self._add_wait(o, t)
        if o.is_dma:
            self.chan_count[dma_chan] = self.chan_count.get(dma_chan, 0) + 1
            tok = ("d", dma_chan, 16 * self.chan_count[dma_chan])
        else:
            tok = ("e", eng, o)
        for k in writes:
            self.last_w[k] = tok
            self.readers[k] = []
        for k in reads:
            self.readers.setdefault(k, []).append(tok)
        self.q[eng].append(o)
        return o

    def barrier(self):
        toks = []
        for e in ENGS:
            last = None
            for o in reversed(self.q[e]):
                if not o.is_dma:
                    last = o
                    break
            if last is not None:
                toks.append(("e", e, last))
        for c, n in self.chan_count.items():
            toks.append(("d", c, 16 * n))
        for e in ENGS:
            o = _Op()
            o.eng = e
            o.fn = lambda eng: eng.nop()
            o.waits = []
            o.sig = None
            o.is_dma = False
            o.chan = None
            for t in toks:
                if t[0] == "e":
                    if t[2].sig is None:
                        t[2].sig = 1
                o.waits.append(t)
            self.q[e].append(o)
        self.last_w = {}
        self.readers = {}

    def emit(self, final_waits=()):
        nc = self.nc
        for e in ENGS:
            n = 0
            for o in self.q[e]:
                if o.sig is not None and not o.is_dma:
                    n += 1
                    o.sig = n
        chans = sorted(self.chan_count.keys(), key=str)
        print("ops:", {e: len(self.q[e]) for e in ENGS}, "chan max:", max(self.chan_count.values()) * 16)
        with contextlib.ExitStack() as st:
            esem = {e: st.enter_context(nc.semaphore("s_" + e)) for e in ENGS}
            csem = {c: st.enter_context(nc.semaphore("c_%d" % i)) for i, c in enumerate(chans)}
            block = st.enter_context(nc.Block())

            def run(ename, eng):
                seen = {}
                for o in self.q[ename]:
                    for w in o.waits:
                        if w[0] == "e":
                            key = ("e", w[1]); val = w[2].sig; sem = esem[w[1]]
                        else:
                            key = ("d", w[1]); val = w[2]; sem = csem[w[1]]
                        if seen.get(key, 0) >= val:
                            continue
                        seen[key] = val
                        eng.wait_ge(sem, val)
                    ins = o.fn(eng)
                    if o.is_dma:
                        ins.then_inc(csem[o.chan], 16)
                    elif o.sig is not None:
                        ins.then_inc(esem[ename], 1)
                if ename == "sp":
                    for c in final_waits:
                        eng.wait_ge(csem[c], 16 * self.chan_count[c])

            @block.sync
            def _(eng):
                run("sp", eng)

            @block.scalar
            def _(eng):
                run("act", eng)

            @block.vector
            def _(eng):
                run("dve", eng)

            @block.gpsimd
            def _(eng):
                run("pool", eng)

            @block.tensor
            def _(eng):
                run("pe", eng)


class Mem:
    def __init__(self, nc, base=20608, limit=229376):
        self.nc = nc
        self.off = base
        self.limit = limit
        self.n = 0

    def alloc(self, name, shape, dt):
        size = 1
        for s in shape[1:]:
            size *= s
        size *= {F32: 4, BF16: 2, U32: 4}[dt]
        size = (size + 63) // 64 * 64
        assert self.off + size <= self.limit, (name, self.off, size)
        self.n += 1
        t = self.nc.alloc_sbuf_tensor_at("%s_%d" % (name, self.n), list(shape), dt, offset=self.off)
        self.off += size
        return t


def _dtsize(dt):
    return {F32: 4, BF16: 2, U32: 4}[dt]


def build(stage=9):
    nc = bass.Bass("TRN2", target_bir_lowering=False)
    S = Sched(nc)
    M = Mem(nc)
    M_alloc = M.alloc

    def dram(name, shape, kind="ExternalInput", dt=F32):
        return nc.dram_tensor(name, list(shape), dt, kind=kind).ap()

    xpre = dram("xpre", [3, 2048, D])
    xown = dram("xown", [2048, D])
    flags_d = dram("flags", [128, 4])
    cT_d = dram("cT", [128, 8])
    colv_d = dram("colv", [128, 2, 8])
    rowv_d = dram("rowv", [4, D])
    wada_d = dram("w_ada", [D, 6 * D])
    bada_d = dram("b_ada", [1, 6 * D])
    win_d = dram("w_in", [D, IN_OFF[9]])
    wup_d = dram("w_alpha_up", [16, 512])
    balpha_d = dram("b_alpha", [1, 512])
    sguw_d = dram("sgu_wT", [8, 128, 128])
    sgub_d = dram("sgu_bT", [128, 8])
    wout_d = dram("w_out", [D, D])
    wq_d = dram("peer_w_q", [D, 2048])
    k1T_d = dram("keys1T", [128, 128])
    k2T_d = dram("keys2T", [128, 128])
    downT_d = dram("downT", [128, 128, 8, 128])
    up_d = dram("up", [128, 128, D])
    out_d = dram("out", [2048, D], kind="ExternalOutput")
    scr_down = nc.dram_tensor("scr_down", [128, 128, 8, 128], BF16).ap()
    scr_up = nc.dram_tensor("scr_up", [128, 128, D], BF16).ap()
    cvt_jobs = []
    for g in range(32):
        cvt_jobs.append((scr_down[:, g * 4:(g + 1) * 4, :, :], downT_d[:, g * 4:(g + 1) * 4, :, :]))
        cvt_jobs.append((scr_up[:, g * 4:(g + 1) * 4, :], up_d[:, g * 4:(g + 1) * 4, :]))

    def issue_cvt(n):
        for _ in range(n):
            if cvt_jobs:
                o_, i_ = cvt_jobs.pop(0)
                S.op("pool", lambda e, o_=o_, i_=i_: e.dma_start(out=o_, in_=i_), writes=["scr"], dma_chan="c_cvt")

    P01 = nc.alloc_psum_tensor("P01", [128, 1024], F32)
    P23 = nc.alloc_psum_tensor("P23", [128, 1024], F32)
    P45 = nc.alloc_psum_tensor("P45", [128, 1024], F32)
    P67 = nc.alloc_psum_tensor("P67", [128, 1024], F32)
    W0 = P01[:, :]; W1 = P23[:, :]; W2 = P67[:, :]
    W0a = P01[:, 0:512]; W0b = P01[:, 512:1024]
    W2a = P67[:, 0:512]; W2b = P67[:, 512:1024]
    PA = P45[:, 0:512]; PB = P45[:, 512:1024]
    kW2 = ["W2a", "W2b"]

    def alloc(name, shape, dt):
        return M_alloc(name, shape, dt)

    ident_f = alloc("ident_f", [128, 128], F32)
    ident_b = alloc("ident_b", [128, 128], BF16)
    iota_f = alloc("iota_f", [128, 128], F32)
    iota_b = alloc("iota_b", [128, 128], BF16)
    Lm = alloc("Lm", [128, 128], F32)
    Rm = alloc("Rm", [128, 128], F32)
    maskU = alloc("maskU", [128, 128], F32)
    ones_f = alloc("ones_f", [128, 128], F32)
    neghalf = alloc("neghalf", [128, 8], F32)
    flags = alloc("flagsb", [128, 4], F32)
    cols = alloc("cols", [128, 6, 8], F32)
    mark_p3 = M.off
    Bsgu = alloc("Bsgu", [128, 8], F32)
    WsT = alloc("WsT", [128, 8, 128], BF16)
    wup_f = alloc("wup_f", [16, 512], F32)
    balpha_f = alloc("balpha_f", [1, 512], F32)
    lnwB = alloc("lnwB", [128, D], F32)
    lnbB = alloc("lnbB", [128, D], F32)
    gnwB = alloc("gnwB", [128, D], F32)
    woutg = alloc("woutg", [128, 8, D], BF16)
    wk = alloc("wk", [128, 8, 512], BF16)
    wv = alloc("wv", [128, 8, 1024], BF16)
    walr = alloc("walr", [128, 8, 16], BF16)
    S_f = alloc("S_f", [128, 4, 256], F32)
    Sb0 = alloc("Sb0", [128, 4, 256], BF16)
    Sb1 = alloc("Sb1", [128, 4, 256], BF16)
    mark_late = M.off

    pid = alloc("pid", [128, 1], F32)
    tmpA = alloc("tmpA", [128, 128], F32)
    tmpB = alloc("tmpB", [128, 128], F32)
    cj = alloc("cj", [128, 1], F32)
    scB = alloc("scB", [128, 8, 128], F32)
    cT = alloc("cTs", [128, 8], F32)
    sigc = alloc("sigc", [128, 8], F32)
    colv = alloc("colvs", [128, 2, 8], F32)
    modB = alloc("modB", [128, 6 * D], F32)
    wbuf = [alloc("wbuf%d" % i, [128, 8, 512], F32) for i in range(2)]
    bb = [alloc("bb%d" % i, [128, 512], F32) for i in range(2)]
    wo_tmp = alloc("wo_tmp", [128, 4, D], F32)
    sgu_tmp = alloc("sgu_tmp", [128, 8, 128], F32)

    def iota(e, out, pattern, base, cm):
        return e.iota(out, pattern=pattern, base=base, channel_multiplier=cm,
                      allow_small_or_imprecise_dtypes=True)

    S.op("pool", lambda e: iota(e, iota_f[:], [[1, 128]], 0, 0), writes=["iota_f"])
    S.op("dve", lambda e: e.tensor_copy(out=iota_b[:], in_=iota_f[:]), reads=["iota_f"], writes=["iota_b"])
    S.op("pool", lambda e: iota(e, pid[:], [[0, 1]], 0, 1), writes=["pid"])
    S.op("pool", lambda e: iota(e, tmpA[:], [[1, 128]], 0, -1), writes=["tmpA"])
    S.op("pool", lambda e: e.memset(ones_f[:], 1.0), writes=["ones_f"])
    S.op("pool", lambda e: e.memset(neghalf[:], -0.5), writes=["neghalf"])
    S.op("pool", lambda e: e.memset(S_f[:], 0.0), writes=["S_f"])
    S.op("dve", lambda e: e.tensor_single_scalar(out=ident_f[:], in_=tmpA[:], scalar=0.0, op=ALU.is_equal),
         reads=["tmpA"], writes=["ident_f"])
    S.op("dve", lambda e: e.tensor_copy(out=ident_b[:], in_=ident_f[:]), reads=["ident_f"], writes=["ident_b"])
    S.op("dve", lambda e: e.tensor_single_scalar(out=cj[:], in_=pid[:], scalar=64.0, op=ALU.is_ge),
         reads=["pid"], writes=["cj"])
    S.op("dve", lambda e: e.tensor_scalar(out=tmpB[:], in0=iota_f[:], scalar1=64.0, scalar2=cj[:, 0:1],
                                          op0=ALU.is_ge, op1=ALU.is_equal),
         reads=["iota_f", "cj"], writes=["tmpB"])
    maskS = alloc("maskS", [128, 128], F32)
    S.op("dve", lambda e: e.tensor_single_scalar(out=maskS[:], in_=tmpA[:], scalar=0.0, op=ALU.is_ge),
         reads=["tmpA"], writes=["maskS"])
    S.op("dve", lambda e: e.tensor_tensor(out=maskU[:], in0=maskS[:], in1=tmpB[:], op=ALU.mult),
         reads=["maskS", "tmpB"], writes=["maskU"])
    S.op("dve", lambda e: e.tensor_scalar(out=Lm[:], in0=maskU[:], scalar1=-1.0 / 16.0, scalar2=None, op0=ALU.mult),
         reads=["maskU"], writes=["Lm"])
    S.op("dve", lambda e: e.tensor_tensor(out=Rm[:], in0=tmpB[:], in1=maskU[:], op=ALU.subtract),
         reads=["tmpB", "maskU"], writes=["Rm"])
    S.op("dve", lambda e: e.tensor_scalar(out=Rm[:], in0=Rm[:], scalar1=-1.0 / 16.0, scalar2=None, op0=ALU.mult),
         reads=["Rm"], writes=["Rm"])

    S.op("sp", lambda e: e.dma_start(out=flags[:], in_=flags_d[:, :]), writes=["flags"], dma_chan="c_flags")
    S.op("sp", lambda e: e.dma_start(out=cT[:], in_=cT_d[:, :]), writes=["cT"], dma_chan="c_cT")
    S.op("sp", lambda e: e.dma_start(out=colv[:], in_=colv_d[:, :, :]), writes=["colv"], dma_chan="c_colv")
    S.op("sp", lambda e: e.dma_start(out=Bsgu[:], in_=sgub_d[:, :]), writes=["Bsgu"], dma_chan="c_bsgu")
    S.op("sp", lambda e: e.dma_start(out=wup_f[:], in_=wup_d[:, :]), writes=["wup_f"], dma_chan="c_wup")
    S.op("sp", lambda e: e.dma_start(out=balpha_f[:], in_=balpha_d[:, :]), writes=["balpha_f"], dma_chan="c_balpha")
    S.op("act", lambda e: e.dma_start(out=gnwB[:], in_=rowv_d[1:2, :].to_broadcast([128, D])), writes=["gnwB"], dma_chan="c_gnw")
    S.op("act", lambda e: e.dma_start(out=lnwB[:], in_=rowv_d[2:3, :].to_broadcast([128, D])), writes=["lnwB"], dma_chan="c_lnw")
    S.op("act", lambda e: e.dma_start(out=lnbB[:], in_=rowv_d[3:4, :].to_broadcast([128, D])), writes=["lnbB"], dma_chan="c_lnb")
    S.op("sp", lambda e: e.dma_start(out=sgu_tmp[:], in_=sguw_d.rearrange("g j i -> j g i")), writes=["sgu_tmp"], dma_chan="c_sguw")
    win_v = win_d.rearrange("(c p) e -> p c e", p=128)
    for ch in range(8):
        S.op("pool", lambda e, ch=ch: e.dma_start(out=wk[:, ch, :], in_=win_v[:, ch, OK_:OK_ + 512]), writes=["wk"], dma_chan="c_wk")
        S.op("pool", lambda e, ch=ch: e.dma_start(out=wv[:, ch, :], in_=win_v[:, ch, OV:OV + 1024]), writes=["wv"], dma_chan="c_wv")
        S.op("pool", lambda e, ch=ch: e.dma_start(out=walr[:, ch, :], in_=win_v[:, ch, OALR:OALR + 16]), writes=["walr"], dma_chan="c_walr")

    S.op("dve", lambda e: e.tensor_tensor(out=WsT[:], in0=sgu_tmp[:], in1=maskS[:].unsqueeze(1).to_broadcast([128, 8, 128]), op=ALU.mult),
         reads=["sgu_tmp", "maskS"], writes=["WsT"])

    S.op("act", lambda e: e.activation(out=sigc[:], in_=cT[:], func=AF.Sigmoid), reads=["cT"], writes=["sigc"])
    S.op("dve", lambda e: e.tensor_tensor(out=sigc[:], in0=sigc[:], in1=cT[:], op=ALU.mult), reads=["sigc", "cT"], writes=["sigc"])
    S.op("dve", lambda e: e.tensor_copy(out=scB[:], in_=sigc[:].unsqueeze(2).to_broadcast([128, 8, 128])),
         reads=["sigc"], writes=["scB"])

    wada_v = wada_d.rearrange("(c p) e -> p c e", p=128)
    for g in range(12):
        b = g % 2
        wb_, bb_ = wbuf[b], bb[b]
        kq = "sp" if b == 0 else "act"
        S.op(kq, lambda e, g=g, wb_=wb_: e.dma_start(out=wb_[:], in_=wada_v[:, :, g * 512:(g + 1) * 512]),
             writes=["wbuf%d" % b], dma_chan="c_wbuf%d" % b)
        S.op(kq, lambda e, g=g, bb_=bb_: e.dma_start(out=bb_[:], in_=bada_d[0:1, g * 512:(g + 1) * 512].to_broadcast([128, 512])),
             writes=["bb%d" % b], dma_chan="c_bb%d" % b)
        for ch in range(8):
            S.op("pe", lambda e, ch=ch, wb_=wb_: e.matmul(PA, lhsT=scB[:, ch, :], rhs=wb_[:, ch, :], start=(ch == 0), stop=(ch == 7)),
                 reads=["scB", "wbuf%d" % b], writes=["PA"])
        S.op("dve", lambda e, g=g, bb_=bb_: e.tensor_tensor(out=modB[:, g * 512:(g + 1) * 512], in0=PA, in1=bb_[:], op=ALU.add),
             reads=["PA", "bb%d" % b], writes=["modB"])

    def col_from_mod(dst_idx, mod_off):
        for ch in range(8):
            S.op("pe", lambda e, ch=ch: e.transpose(PB[:, 0:128], modB[:, mod_off + ch * 128: mod_off + (ch + 1) * 128], ident_f[:]),
                 reads=["modB", "ident_f"], writes=["PB"])
            S.op("dve", lambda e, ch=ch: e.tensor_copy(out=cols[:, dst_idx, ch:ch + 1], in_=PB[:, 0:1]),
                 reads=["PB"], writes=["cols"])
    col_from_mod(1, 0 * D)
    col_from_mod(0, 1 * D)
    col_from_mod(3, 3 * D)
    col_from_mod(2, 4 * D)
    col_from_mod(4, 5 * D)
    for di, ci in ((0, 0), (2, 1)):
        S.op("dve", lambda e, di=di, ci=ci: e.scalar_tensor_tensor(out=cols[:, di, :], in0=cols[:, di, :], scalar=1.0,
                                                                   in1=colv[:, ci, :], op0=ALU.add, op1=ALU.mult),
             reads=["cols", "colv"], writes=["cols"])

    wout_v = wout_d.rearrange("(c p) e -> p c e", p=128)
    for hf in range(2):
        S.op("sp", lambda e, hf=hf: e.dma_start(out=wo_tmp[:], in_=wout_v[:, hf * 4:(hf + 1) * 4, :]), writes=["wo_tmp"], dma_chan="c_wo")
        S.op("dve", lambda e, hf=hf: e.tensor_tensor(out=woutg[:, hf * 4:(hf + 1) * 4, :], in0=wo_tmp[:],
                                                     in1=modB[:, 2 * D:3 * D].unsqueeze(1).to_broadcast([128, 4, D]), op=ALU.mult),
             reads=["wo_tmp", "modB"], writes=["woutg"])

    if stage == 0:
        dbg = dram("dbg", [128, 2048], kind="ExternalOutput")
        S.op("sp", lambda e: e.dma_start(out=dbg[:, 0:48], in_=cols[:].rearrange("p a b -> p (a b)")), reads=["cols"], writes=["dbg"], dma_chan="c_out")
        S.op("sp", lambda e: e.dma_start(out=dbg[:, 128:256], in_=Lm[:]), reads=["Lm"], writes=["dbg"], dma_chan="c_out")
        S.op("sp", lambda e: e.dma_start(out=dbg[:, 256:384], in_=Rm[:]), reads=["Rm"], writes=["dbg"], dma_chan="c_out")
        S.op("sp", lambda e: e.dma_start(out=dbg[:, 1024:2048], in_=modB[:, 2048:3072]), reads=["modB"], writes=["dbg"], dma_chan="c_out")
        S.emit(final_waits=["c_out0", "c_out1"])
        return nc
    S.barrier()
    M.off = mark_late

    wq = alloc("wq", [128, 8, 512], BF16)
    wr = alloc("wr", [128, 8, 1024], BF16)
    wsu = alloc("wsu", [128, 8, 1024], BF16)
    wsv = alloc("wsv", [128, 8, 1024], BF16)
    wga = alloc("wga", [128, 8, 1024], BF16)
    wgb = alloc("wgb", [128, 8, 1024], BF16)
    mark_work = M.off

    xt = alloc("xt", [128, D], F32)
    xtB = alloc("xtB", [128, D], F32)
    mTb = alloc("mTb", [128, 8, 128], BF16)
    Fs = [alloc("F%d" % i, [128, D], F32) for i in range(4)]
    xn, t2_, t1_, u_, gv_ = Fs[0], Fs[0], Fs[1], Fs[2], Fs[3]
    hT = alloc("hT", [128, 8, 128], BF16)
    v_bf = alloc("v_bf", [128, D], BF16)
    alrT = alloc("alrT", [16, 128], F32)
    bufE = alloc("bufE", [128, 512], F32)
    lbuf = alloc("lbuf", [128, 512], F32)
    expnG = alloc("expnG", [128, 512], F32)
    ktail = alloc("ktail", [128, 512], BF16)
    dec = alloc("dec", [128, 4, 2], F32)
    ss = alloc("ss", [128, 8], F32)
    rstd = alloc("rstd", [128, 8], F32)
    qd = alloc("qd", [128, 4, 128], BF16)
    q0 = alloc("q0", [128, 4, 128], BF16)
    q1 = alloc("q1", [128, 4, 128], BF16)
    ki = alloc("ki", [128, 4, 128], BF16)
    ATm = alloc("ATm", [128, 4, 128], BF16)
    vln = alloc("vln", [128, D], BF16)
    bnst = alloc("bnst", [128, 2, 6], F32)
    mv = alloc("mv", [128, 2], F32)
    print("SBUF used (P2):", M.off)

    S.op("pool", lambda e: e.memset(q0[:], 0.0), writes=["q0"])
    S.op("pool", lambda e: e.memset(q1[:], 0.0), writes=["q1"])

    def load_late():
        for ch in range(8):
            for (wt, off, n, nm) in ((wq, OQ, 512, "wq"), (wr, OR, 1024, "wr"), (wga, OGA, 1024, "wga"),
                                     (wsu, OSU, 1024, "wsu"), (wsv, OSV, 1024, "wsv"), (wgb, OGB, 1024, "wgb")):
                S.op("pool", lambda e, ch=ch, wt=wt, off=off, n=n: e.dma_start(out=wt[:, ch, :], in_=win_v[:, ch, off:off + n]),
                     writes=[nm], dma_chan="c_" + nm)

    def proj_tm(dst, dkeys, wt, wkey, c0, n):
        for ch in range(8):
            S.op("pe", lambda e, ch=ch: e.matmul(dst, lhsT=hT[:, ch, :], rhs=wt[:, ch, c0:c0 + n], start=(ch == 0), stop=(ch == 7)),
                 reads=[("hT", ch), wkey], writes=dkeys)

    xts = [xt, xtB]

    def p2_front(k):
        xk = xts[k % 2]
        kx = "xt%d" % (k % 2)
        x_ap = xown[k * 128:(k + 1) * 128, :]
        S.op("sp", lambda e: e.dma_start(out=xk[:], in_=x_ap), writes=[kx], dma_chan="c_xt%d" % (k % 2))
        S.op("act", lambda e: e.activation(out=xn[:], in_=xk[:], func=AF.Square, accum_out=ss[:, 0:1]),
             reads=[kx], writes=["F0", "ss_f"])
        S.op("dve", lambda e: e.tensor_scalar(out=ss[:, 1:2], in0=ss[:, 0:1], scalar1=1.0 / D, scalar2=EPS, op0=ALU.mult, op1=ALU.add),
             reads=["ss_f"], writes=["ss_f2"])
        S.op("pool", lambda e: e.tensor_tensor(out=rstd[:, 0:1], in0=ss[:, 1:2], in1=neghalf[:, 0:1], op=ALU.pow),
             reads=["ss_f2", "neghalf"], writes=["rstd_f"])
        S.op("dve", lambda e: e.tensor_scalar(out=xn[:], in0=xk[:], scalar1=rstd[:, 0:1], scalar2=None, op0=ALU.mult),
             reads=[kx, "rstd_f"], writes=["F0"])
        for ch in range(8):
            S.op("pe", lambda e, ch=ch: e.transpose(W0[:, ch * 128:(ch + 1) * 128], xn[:, ch * 128:(ch + 1) * 128], ident_f[:]),
                 reads=["F0", "ident_f"], writes=["W0a", "W0b"])
        for ch in range(8):
            if ch < 4:
                S.op("dve", lambda e, ch=ch: e.tensor_scalar(out=hT[:, ch, :], in0=W0[:, ch * 128:(ch + 1) * 128],
                                                             scalar1=cols[:, 0, ch:ch + 1], scalar2=cols[:, 1, ch:ch + 1],
                                                             op0=ALU.mult, op1=ALU.add),
                     reads=["W0a" if ch < 4 else "W0b", "cols"], writes=[("hT", ch)])
            else:
                S.op("act", lambda e, ch=ch: e.activation(out=hT[:, ch, :], in_=W0[:, ch * 128:(ch + 1) * 128], func=AF.Identity,
                                                          scale=cols[:, 0, ch:ch + 1], bias=cols[:, 1, ch:ch + 1]),
                     reads=["W0a" if ch < 4 else "W0b", "cols"], writes=[("hT", ch)])

    def p2_body(k):
        seg, full, row0 = 3, True, k * 128
        fcol = flags[:, seg:seg + 1]
        capA = []
        S.cap = capA
        proj_tm(PA, ["PA"], wk, "wk", 0, 512)
        proj_tm(W1[:, 0:512], ["W1"], wv, "wv", 0, 512)
        proj_tm(W1[:, 512:1024], ["W1"], wv, "wv", 512, 512)
        for ch in range(8):
            S.op("pe", lambda e, ch=ch: e.matmul(PB[0:16, 0:128], lhsT=walr[:, ch, :], rhs=hT[:, ch, :], start=(ch == 0), stop=(ch == 7)),
                 reads=[("hT", ch), "walr"], writes=["PB"])
        S.op("dve", lambda e: e.tensor_scalar(out=v_bf[:], in0=W1, scalar1=fcol, scalar2=None, op0=ALU.mult),
             reads=["W1", "flags"], writes=["v_bf"])
        S.op("act", lambda e: e.activation(out=alrT[:], in_=PB[0:16, 0:128], func=AF.Copy), reads=["PB"], writes=["alrT"])
        S.op("pe", lambda e: e.matmul(W0a, lhsT=alrT[:], rhs=wup_f[:], start=True, stop=False),
             reads=["alrT", "wup_f"], writes=["W0a"])
        S.op("pe", lambda e: e.matmul(W0a, lhsT=ones_f[0:1, :], rhs=balpha_f[:], start=False, stop=True),
             reads=["ones_f", "balpha_f"], writes=["W0a"])
        S.op("act", lambda e: e.activation(out=bufE[:], in_=W0a, func=AF.Exp, scale=-1.0), reads=["W0a"], writes=["bufE"])
        S.op("act", lambda e: e.activation(out=lbuf[:], in_=bufE[:], func=AF.Ln, bias=1.0, scale=1.0), reads=["bufE"], writes=["lbuf"])
        S.op("pe", lambda e: e.matmul(W0b, lhsT=Rm[:], rhs=lbuf[:], start=True, stop=True), reads=["Rm", "lbuf"], writes=["W0b"])
        for h in range(4):
            S.op("pe", lambda e, h=h: e.matmul(PB[:, h * 128:(h + 1) * 128], lhsT=lbuf[:, h * 128:(h + 1) * 128], rhs=Lm[:], start=True, stop=True),
                 reads=["lbuf", "Lm"], writes=["PB"])
        S.op("act", lambda e: e.activation(out=bufE[:], in_=W0b, func=AF.Exp), reads=["W0b"], writes=["bufE"])
        S.op("dve", lambda e: e.tensor_tensor(out=ktail[:], in0=PA, in1=bufE[:], op=ALU.mult), reads=["PA", "bufE"], writes=["ktail"])
        PBv = PB.rearrange("p (h t) -> p h t", h=4)
        S.op("act", lambda e: e.activation(out=dec[:], in_=PBv[:, :, 63:128:64], func=AF.Exp), reads=["PB"], writes=["dec"])
        if full:
            for h in range(4):
                for ch in range(8):
                    S.op("pe", lambda e, h=h, ch=ch: e.matmul(W0a[:, h * 128:(h + 1) * 128], lhsT=wq[:, ch, h * 128:(h + 1) * 128], rhs=hT[:, ch, :],
                                                             start=(ch == 0), stop=(ch == 7)),
                         reads=[("hT", ch), "wq"], writes=["W0a"])
            for h in range(4):
                for ch in range(8):
                    S.op("pe", lambda e, h=h, ch=ch: e.matmul(W0b[:, h * 128:(h + 1) * 128], lhsT=wk[:, ch, h * 128:(h + 1) * 128], rhs=hT[:, ch, :],
                                                             start=(ch == 0), stop=(ch == 7)),
                         reads=[("hT", ch), "wk"], writes=["W0b"])
            S.op("act", lambda e: e.activation(out=lbuf[:], in_=PB, func=AF.Exp), reads=["PB"], writes=["lbuf"])
            S.op("act", lambda e: e.activation(out=expnG[:], in_=PB, func=AF.Exp, scale=-1.0), reads=["PB"], writes=["expnG"])
            qdf = qd[:].rearrange("p h t -> p (h t)")
            kif = ki[:].rearrange("p h t -> p (h t)")
            S.op("dve", lambda e: e.scalar_tensor_tensor(out=qdf, in0=W0a, scalar=128.0 ** -0.5, in1=lbuf[:], op0=ALU.mult, op1=ALU.mult),
                 reads=["W0a", "lbuf"], writes=["qd"])
            S.op("dve", lambda e: e.tensor_tensor(out=kif, in0=W0b, in1=expnG[:], op=ALU.mult), reads=["W0b", "expnG"], writes=["ki"])
            S.op("pool", lambda e: e.tensor_copy(out=q0[:, :, 0:64], in_=qd[:, :, 0:64]), reads=["qd"], writes=["q0"])
            S.op("pool", lambda e: e.tensor_copy(out=q1[:, :, 64:128], in_=qd[:, :, 64:128]), reads=["qd"], writes=["q1"])
            for h in range(4):
                S.op("pe", lambda e, h=h: e.matmul(PA[:, h * 128:(h + 1) * 128], lhsT=ki[:, h, :], rhs=qd[:, h, :], start=True, stop=True),
                     reads=["ki", "qd"], writes=["PA"])
            S.op("dve", lambda e: e.tensor_tensor(out=ATm[:], in0=PA.rearrange("p (h t) -> p h t", h=4),
                                                  in1=maskU[:].unsqueeze(1).to_broadcast([128, 4, 128]), op=ALU.mult),
                 reads=["PA", "maskU"], writes=["ATm"])
        W1v = W1.rearrange("p (h v) -> p h v", h=4)

        def kv_update(c):
            for h in range(4):
                S.op("pe", lambda e, h=h: e.matmul(W1v[:, h, :], lhsT=ktail[c * 64:(c + 1) * 64, h * 128:(h + 1) * 128],
                                                   rhs=v_bf[c * 64:(c + 1) * 64, h * 256:(h + 1) * 256], start=True, stop=True),
                     reads=["ktail", "v_bf"], writes=["W1"])
            for h in range(4):
                S.op("dve", lambda e, h=h: e.scalar_tensor_tensor(out=S_f[:, h, :], in0=S_f[:, h, :], scalar=dec[:, h, c:c + 1],
                                                                  in1=W1v[:, h, :], op0=ALU.mult, op1=ALU.add),
                     reads=["S_f", "dec", "W1"], writes=["S_f"])

        kv_update(0)
        if full:
            S.op("act", lambda e: e.activation(out=Sb1[:], in_=S_f[:], func=AF.Copy), reads=["S_f"], writes=["Sb1"])
            W0v = W0.rearrange("p (h v) -> p h v", h=4)
            for h in range(4):
                S.op("pe", lambda e, h=h: e.matmul(W0v[:, h, :], lhsT=ATm[:, h, :], rhs=v_bf[:, h * 256:(h + 1) * 256], start=True, stop=False),
                     reads=["ATm", "v_bf"], writes=["W0a" if h < 2 else "W0b"])
                S.op("pe", lambda e, h=h: e.matmul(W0v[:, h, :], lhsT=q0[:, h, :], rhs=Sb0[:, h, :], start=False, stop=False),
                     reads=["q0", "Sb0"], writes=["W0a" if h < 2 else "W0b"])
                S.op("pe", lambda e, h=h: e.matmul(W0v[:, h, :], lhsT=q1[:, h, :], rhs=Sb1[:, h, :], start=False, stop=True),
                     reads=["q1", "Sb1"], writes=["W0a" if h < 2 else "W0b"])
        kv_update(1)
        if not full:
            return
        S.op("act", lambda e: e.activation(out=Sb0[:], in_=S_f[:], func=AF.Copy), reads=["S_f"], writes=["Sb0"])
        t1v = t1_[:].rearrange("p (h v) -> p h v", h=4)
        t2v = t2_[:].rearrange("p (h v) -> p h v", h=4)
        for h in range(4):
            S.op("act", lambda e, h=h: e.activation(out=t1v[:, h, :], in_=W0v[:, h, :], func=AF.Square, accum_out=ss[:, 2 + h:3 + h]),
                 reads=["W0a" if h < 2 else "W0b"], writes=["F1", "ss"])
        S.op("dve", lambda e: e.tensor_scalar(out=ss[:, 2:6], in0=ss[:, 2:6], scalar1=1.0 / 256.0, scalar2=EPS, op0=ALU.mult, op1=ALU.add),
             reads=["ss"], writes=["ss"])
        S.op("pool", lambda e: e.tensor_tensor(out=rstd[:, 2:6], in0=ss[:, 2:6], in1=neghalf[:, 0:4], op=ALU.pow),
             reads=["ss", "neghalf"], writes=["rstd"])
        proj_tm(W1[:, 0:512], ["W1"], wr, "wr", 0, 512)
        proj_tm(W1[:, 512:1024], ["W1"], wr, "wr", 512, 512)
        S.op("act", lambda e: e.activation(out=t2_[:], in_=W1, func=AF.Sigmoid), reads=["W1"], writes=["F0"])
        S.op("dve", lambda e: e.tensor_tensor(out=t2_[:], in0=W1, in1=t2_[:], op=ALU.mult), reads=["W1", "F0"], writes=["F0"])
        S.op("pool", lambda e: e.tensor_tensor(out=t2_[:], in0=t2_[:], in1=gnwB[:], op=ALU.mult), reads=["F0", "gnwB"], writes=["F0"])
        for h in range(4):
            S.op("dve", lambda e, h=h: e.scalar_tensor_tensor(out=t1v[:, h, :], in0=W0v[:, h, :], scalar=rstd[:, 2 + h:3 + h],
                                                              in1=t2v[:, h, :], op0=ALU.mult, op1=ALU.mult),
                 reads=["W0a" if h < 2 else "W0b", "rstd", "F0"], writes=["F1"])
        proj_tm(W1[:, 0:512], ["W1"], wga, "wga", 0, 512)
        proj_tm(W1[:, 512:1024], ["W1"], wga, "wga", 512, 512)
        S.op("act", lambda e: e.activation(out=t2_[:], in_=W1, func=AF.Sigmoid), reads=["W1"], writes=["F0"])
        S.op("pool", lambda e: e.tensor_tensor(out=t1_[:], in0=t1_[:], in1=t2_[:], op=ALU.mult), reads=["F1", "F0"], writes=["F1"])
        capB = []
        S.cap = capB
        proj_tm(W2a, ["W2a"], wsu, "wsu", 0, 512)
        proj_tm(W2b, ["W2b"], wsu, "wsu", 512, 512)
        S.op("act", lambda e: e.activation(out=u_[:], in_=W2, func=AF.Gelu), reads=kW2, writes=["F2"])
        proj_tm(W2a, ["W2a"], wsv, "wsv", 0, 512)
        proj_tm(W2b, ["W2b"], wsv, "wsv", 512, 512)
        S.op("act", lambda e: e.activation(out=gv_[:], in_=W2, func=AF.Gelu), reads=kW2, writes=["F3"])
        for hf in range(2):
            S.op("dve", lambda e, hf=hf: e.bn_stats(out=bnst[:, hf, :], in_=gv_[:, hf * 512:(hf + 1) * 512]), reads=["F3"], writes=["bnst"])
        S.op("dve", lambda e: e.bn_aggr(out=mv[:], in_=bnst[:].rearrange("p a s -> p (a s)")), reads=["bnst"], writes=["mv"])
        S.op("dve", lambda e: e.tensor_scalar(out=ss[:, 6:7], in0=mv[:, 1:2], scalar1=EPS, scalar2=None, op0=ALU.add),
             reads=["mv"], writes=["ss_s"])
        S.op("pool", lambda e: e.tensor_tensor(out=rstd[:, 6:7], in0=ss[:, 6:7], in1=neghalf[:, 0:1], op=ALU.pow),
             reads=["ss_s", "neghalf"], writes=["rstd_s"])
        S.op("dve", lambda e: e.tensor_scalar(out=gv_[:], in0=gv_[:], scalar1=mv[:, 0:1], scalar2=rstd[:, 6:7], op0=ALU.subtract, op1=ALU.mult),
             reads=["F3", "mv", "rstd_s"], writes=["F3"])
        S.op("pool", lambda e: e.tensor_tensor(out=gv_[:], in0=gv_[:], in1=lnwB[:], op=ALU.mult), reads=["F3", "lnwB"], writes=["F3"])
        S.op("pool", lambda e: e.tensor_tensor(out=vln[:], in0=gv_[:], in1=lnbB[:], op=ALU.add), reads=["F3", "lnbB"], writes=["vln"])
        proj_tm(W2a, ["W2a"], wgb, "wgb", 0, 512)
        proj_tm(W2b, ["W2b"], wgb, "wgb", 512, 512)
        S.op("act", lambda e: e.activation(out=gv_[:], in_=W2, func=AF.Sigmoid), reads=kW2, writes=["F3"])
        for g in range(8):
            S.op("pe", lambda e, g=g: e.matmul(W2[:, g * 128:(g + 1) * 128], lhsT=WsT[:, g, :], rhs=vln[:, g * 128:(g + 1) * 128], start=True, stop=True),
                 reads=["WsT", "vln"], writes=kW2)
        for g in range(8):
            S.op("dve", lambda e, g=g: e.scalar_tensor_tensor(out=u_[:, g * 128:(g + 1) * 128], in0=W2[:, g * 128:(g + 1) * 128],
                                                              scalar=Bsgu[:, g:g + 1], in1=u_[:, g * 128:(g + 1) * 128], op0=ALU.add, op1=ALU.mult),
                 reads=kW2 + ["Bsgu", "F2"], writes=["F2"])
        S.op("pool", lambda e: e.tensor_tensor(out=u_[:], in0=u_[:], in1=gv_[:], op=ALU.mult), reads=["F2", "F3"], writes=["F2"])
        S.cap = None
        ia = ib = 0
        na, nb = len(capA), len(capB)
        while ia < na or ib < nb:
            if ib >= nb or (ia < na and ia * nb <= ib * na):
                S.op(*capA[ia]); ia += 1
            else:
                S.op(*capB[ib]); ib += 1
        S.op("dve", lambda e: e.tensor_tensor(out=vln[:], in0=t1_[:], in1=u_[:], op=ALU.add), reads=["F1", "F2"], writes=["vln"])

    def p2_tail(k):
        xk = xts[k % 2]
        kx = "xt%d" % (k % 2)
        row0 = k * 128
        PAb = PA.bitcast(BF16)
        for ch in range(8):
            S.op("pe", lambda e, ch=ch: e.transpose(PAb[:, ch * 128:(ch + 1) * 128], vln[:, ch * 128:(ch + 1) * 128], ident_b[:]),
                 reads=["vln", "ident_b"], writes=["PA"])
        S.op("act", lambda e: e.activation(out=mTb[:].rearrange("p c t -> p (c t)"), in_=PAb[:, 0:1024], func=AF.Copy), reads=["PA"], writes=["mT"])
        for hf in range(2):
            for ch in range(8):
                S.op("pe", lambda e, hf=hf, ch=ch: e.matmul(W1[:, hf * 512:(hf + 1) * 512], lhsT=mTb[:, ch, :], rhs=woutg[:, ch, hf * 512:(hf + 1) * 512],
                                                            start=(ch == 0), stop=(ch == 7)),
                     reads=["mT", "woutg"], writes=["W1"])
        S.op("dve", lambda e: e.tensor_tensor(out=xk[:], in0=W1, in1=xk[:], op=ALU.add), reads=["W1", kx], writes=[kx])
        S.op("sp", lambda e: e.dma_start(out=out_d[row0:row0 + 128, :], in_=xk[:]), reads=[kx], writes=[("out_d", row0 // 128)], dma_chan="c_out%d" % (k % 2))

    load_late()
    alrT1 = alloc("alrT1", [16, 128], F32)
    dec1 = alloc("dec1", [128, 4, 2], F32)
    ss1 = alloc("ss1", [128, 8], F32)
    rstd1 = alloc("rstd1", [128, 8], F32)
    F3h = Fs[3]
    p1buf = [
        dict(xt=xt[:], xn=Fs[0][:], hT=hT, v=v_bf[:], alrT=alrT, bufE=bufE[:], lbuf=lbuf[:], ktail=ktail[:], dec=dec, ss=ss, rstd=rstd,
             Ww=P01[:, :], Wx=P23[:, :], Bc=P23[:, 0:512], Bd=P23[:, 512:1024]),
        dict(xt=Fs[1][:], xn=Fs[2][:], hT=vln[:].rearrange("p (c t) -> p c t", c=8), v=F3h[:, 0:512].bitcast(BF16), alrT=alrT1, bufE=F3h[:, 512:1024],
             lbuf=expnG[:], ktail=qd[:].rearrange("p h t -> p (h t)"), dec=dec1, ss=ss1, rstd=rstd1,
             Ww=P45[:, :], Wx=P67[:, :], Bc=P67[:, 0:512], Bd=P67[:, 512:1024]),
    ]

    def p1_stages(x_ap, seg, sl):
        B = p1buf[sl]
        K = lambda n: "%s_%d" % (n, sl)
        fcol = flags[:, seg:seg + 1]
        xt_, xn_, hT_, v_, alrT_, bufE_, lbuf_, ktail_, dec_, ss_, rstd_ = (B[k] for k in ("xt", "xn", "hT", "v", "alrT", "bufE", "lbuf", "ktail", "dec", "ss", "rstd"))
        Ww, Wx, Bc, Bd = B["Ww"], B["Wx"], B["Bc"], B["Bd"]
        st = []

        def s0():
            S.op("sp", lambda e: e.dma_start(out=xt_, in_=x_ap), writes=[K("xt")], dma_chan="c_p1xt%d" % sl)
            S.op("act", lambda e: e.activation(out=xn_, in_=xt_, func=AF.Square, accum_out=ss_[:, 0:1]), reads=[K("xt")], writes=[K("xn"), K("ss")])
            S.op("pool", lambda e: e.tensor_scalar(out=ss_[:, 1:2], in0=ss_[:, 0:1], scalar1=1.0 / D, scalar2=EPS, op0=ALU.mult, op1=ALU.add),
                 reads=[K("ss")], writes=[K("ssb")])
            S.op("pool", lambda e: e.tensor_tensor(out=rstd_[:, 0:1], in0=ss_[:, 1:2], in1=neghalf[:, 0:1], op=ALU.pow), reads=[K("ssb"), "neghalf"], writes=[K("rstd")])
        st.append(s0)

        def s1():
            S.op("act", lambda e: e.activation(out=xn_, in_=xt_, func=AF.Copy, scale=rstd_[:, 0:1]), reads=[K("xt"), K("rstd")], writes=[K("xn")])
            for ch in range(8):
                S.op("pe", lambda e, ch=ch: e.transpose(Ww[:, ch * 128:(ch + 1) * 128], xn_[:, ch * 128:(ch + 1) * 128], ident_f[:]),
                     reads=[K("xn"), "ident_f"], writes=[K("Ww")])
        st.append(s1)

        def s2():
            for ch in range(8):
                eng = "dve" if ch < 4 else "act"
                if eng == "dve":
                    S.op("dve", lambda e, ch=ch: e.tensor_scalar(out=hT_[:, ch, :], in0=Ww[:, ch * 128:(ch + 1) * 128],
                                                                 scalar1=cols[:, 0, ch:ch + 1], scalar2=cols[:, 1, ch:ch + 1], op0=ALU.mult, op1=ALU.add),
                         reads=[K("Ww"), "cols"], writes=[(K("hT"), ch)])
                else:
                    S.op("act", lambda e, ch=ch: e.activation(out=hT_[:, ch, :], in_=Ww[:, ch * 128:(ch + 1) * 128], func=AF.Identity,
                                                              scale=cols[:, 0, ch:ch + 1], bias=cols[:, 1, ch:ch + 1]),
                         reads=[K("Ww"), "cols"], writes=[(K("hT"), ch)])
        st.append(s2)

        def s3():
            for ch in range(8):
                S.op("pe", lambda e, ch=ch: e.matmul(Bc, lhsT=hT_[:, ch, :], rhs=wk[:, ch, :], start=(ch == 0), stop=(ch == 7)), reads=[(K("hT"), ch), "wk"], writes=[K("Bc")])
            for ch in range(8):
                S.op("pe", lambda e, ch=ch: e.matmul(Bd[0:16, 0:128], lhsT=walr[:, ch, :], rhs=hT_[:, ch, :], start=(ch == 0), stop=(ch == 7)),
                     reads=[(K("hT"), ch), "walr"], writes=[K("Bd")])
            for hf in range(2):
                for ch in range(8):
                    S.op("pe", lambda e, ch=ch, hf=hf: e.matmul(Ww[:, hf * 512:(hf + 1) * 512], lhsT=hT_[:, ch, :], rhs=wv[:, ch, hf * 512:(hf + 1) * 512],
                                                               start=(ch == 0), stop=(ch == 7)),
                         reads=[(K("hT"), ch), "wv"], writes=[K("Ww")])
            S.op("act", lambda e: e.activation(out=alrT_[:], in_=Bd[0:16, 0:128], func=AF.Copy), reads=[K("Bd")], writes=[K("alrT")])
        st.append(s3)

        def s4():
            S.op("pe", lambda e: e.matmul(Bd, lhsT=alrT_[:], rhs=wup_f[:], start=True, stop=False), reads=[K("alrT"), "wup_f"], writes=[K("Bd")])
            S.op("pe", lambda e: e.matmul(Bd, lhsT=ones_f[0:1, :], rhs=balpha_f[:], start=False, stop=True), reads=["ones_f", "balpha_f"], writes=[K("Bd")])
            S.op("dve", lambda e: e.tensor_scalar(out=v_, in0=Ww, scalar1=fcol, scalar2=None, op0=ALU.mult), reads=[K("Ww"), "flags"], writes=[K("v")])
            S.op("act", lambda e: e.activation(out=bufE_, in_=Bd, func=AF.Exp, scale=-1.0), reads=[K("Bd")], writes=[K("bufE")])
        st.append(s4)

        def s5():
            S.op("act", lambda e: e.activation(out=lbuf_, in_=bufE_, func=AF.Ln, bias=1.0, scale=1.0), reads=[K("bufE")], writes=[K("lbuf")])
            S.op("pe", lambda e: e.matmul(Bd, lhsT=Rm[:], rhs=lbuf_, start=True, stop=True), reads=["Rm", K("lbuf")], writes=[K("Bd")])
            for h in range(4):
                S.op("pe", lambda e, h=h: e.matmul(Ww[:, h * 128:(h + 1) * 128], lhsT=lbuf_[:, h * 128:(h + 1) * 128], rhs=Lm[:], start=True, stop=True),
                     reads=[K("lbuf"), "Lm", K("v")], writes=[K("Ww")])
        st.append(s5)

        def s6():
            S.op("act", lambda e: e.activation(out=bufE_, in_=Bd, func=AF.Exp), reads=[K("Bd")], writes=[K("bufE")])
            Wv4 = Ww[:, 0:512].rearrange("p (h t) -> p h t", h=4)
            S.op("act", lambda e: e.activation(out=dec_[:], in_=Wv4[:, :, 63:128:64], func=AF.Exp), reads=[K("Ww")], writes=[K("dec")])
            S.op("dve", lambda e: e.tensor_tensor(out=ktail_, in0=Bc, in1=bufE_, op=ALU.mult), reads=[K("Bc"), K("bufE")], writes=[K("ktail")])
        st.append(s6)

        def kv(c, Wdst, wkey):
            Wd = Wdst.rearrange("p (h v) -> p h v", h=4)
            for h in range(4):
                S.op("pe", lambda e, h=h: e.matmul(Wd[:, h, :], lhsT=ktail_[c * 64:(c + 1) * 64, h * 128:(h + 1) * 128],
                                                   rhs=v_[c * 64:(c + 1) * 64, h * 256:(h + 1) * 256], start=True, stop=True),
                     reads=[K("ktail"), K("v")] + ([K("dec")] if wkey == "Ww" else []), writes=[K(wkey)] if wkey != "Wx" else [K("Bc"), K("Bd")])
            for h in range(4):
                S.op("dve", lambda e, h=h: e.scalar_tensor_tensor(out=S_f[:, h, :], in0=S_f[:, h, :], scalar=dec_[:, h, c:c + 1],
                                                                  in1=Wd[:, h, :], op0=ALU.mult, op1=ALU.add),
                     reads=[("S_f", h), K("dec")] + ([K(wkey)] if wkey != "Wx" else [K("Bc"), K("Bd")]), writes=[("S_f", h)])
        st.append(lambda: kv(0, Wx, "Wx"))
        st.append(lambda: kv(1, Ww, "Ww"))
        return st

    tiles = []
    for seg in range(3):
        for ti in range(NT if stage not in (1, 3, 4) else NDBG):
            tiles.append((xpre[seg, ti * 128:(ti + 1) * 128, :], seg))
    SKW = 4
    stg = [p1_stages(x_ap, seg, k % 2) for k, (x_ap, seg) in enumerate(tiles)]
    nst = len(stg[0])
    for step in range(len(tiles) * SKW + nst):
        for k in range(len(tiles)):
            sidx = step - k * SKW
            if 0 <= sidx < nst:
                stg[k][sidx]()
        if step % 2 == 0:
            issue_cvt(1)
    S.barrier()
    S.op("act", lambda e: e.activation(out=Sb0[:], in_=S_f[:], func=AF.Copy), writes=["Sb0"])
    issue_cvt(1000)
    NT2 = NT if stage not in (1, 3, 4) else NDBG
    p2_front(0)
    for ti in range(NT2):
        p2_body(ti)
        if ti + 1 < NT2:
            p2_front(ti + 1)
        p2_tail(ti)

    if stage <= 2:
        S.emit(final_waits=["c_out0", "c_out1"])
        return nc

    S.barrier()
    M.off = mark_p3
    h2T = alloc("h2T", [128, 8, 2048], BF16)
    idx1T = alloc("idx1T", [128, 2048], BF16)
    idx2T = alloc("idx2T", [128, 2048], BF16)
    gT = alloc("gT", [128, 2048], BF16)
    gate2B = alloc("gate2B", [128, D], F32)
    finwB = alloc("finwB", [128, D], F32)
    mark_p3t = M.off
    wqb = alloc("wqb", [128, 8, 2048], BF16)
    k1T = alloc("k1T", [128, 128], BF16)
    k2T = alloc("k2T", [128, 128], BF16)
    xt2 = alloc("xt2", [128, D], F32)
    xn2 = alloc("xn2", [128, D], F32)
    qT = alloc("qT", [128, 16, 128], BF16)
    sc = alloc("sc", [128, 16, 128], F32)
    work = alloc("work", [128, 256], F32)
    vtop = alloc("vtop", [128, 16, 16], F32)
    iu = alloc("iu", [128, 16, 16], U32)
    itf = alloc("itf", [128, 16, 16], F32)
    cand = alloc("cand", [128, 8, 256], F32)
    ts = alloc("ts", [128, 8, 16], F32)
    posu = alloc("posu", [128, 8, 16], U32)
    k1u = alloc("k1u", [128, 8, 16], U32)
    k2u = alloc("k2u", [128, 8, 16], U32)
    k1f = alloc("k1f", [128, 8, 16], F32)
    k2f = alloc("k2f", [128, 8, 16], F32)
    ee = alloc("ee", [128, 8, 16], F32)
    zz = alloc("zz", [128, 8], F32)
    oh = alloc("oh", [128, 128, 16], F32)
    idx_tm = alloc("idx_tm", [128, 3, 128], F32)
    iota16 = alloc("iota16", [128, 16], F32)
    diag = alloc("diag", [128, 128], F32)
    ss2 = alloc("ss2", [128, 4], F32)
    rstd2 = alloc("rstd2", [128, 4], F32)

    S.op("act", lambda e: e.dma_start(out=finwB[:], in_=rowv_d[0:1, :].to_broadcast([128, D])), writes=["finwB"], dma_chan="c_finw")
    wq_v = wq_d.rearrange("(c p) e -> p c e", p=128)
    for ch in range(8):
        S.op("pool", lambda e, ch=ch: e.dma_start(out=wqb[:, ch, :], in_=wq_v[:, ch, :]), writes=["wqb"], dma_chan="c_wqb")
    S.op("pool", lambda e: e.dma_start(out=k1T[:], in_=k1T_d[:, :]), writes=["k1T"], dma_chan="c_k1T")
    S.op("pool", lambda e: e.dma_start(out=k2T[:], in_=k2T_d[:, :]), writes=["k2T"], dma_chan="c_k2T")
    S.op("dve", lambda e: e.tensor_copy(out=iota16[:], in_=iota_f[:, 0:16]), reads=["iota_f"], writes=["iota16"])
    for ch in range(8):
        S.op("dve", lambda e, ch=ch: e.tensor_scalar(out=diag[:], in0=ident_f[:], scalar1=cols[:, 4, ch:ch + 1], scalar2=None, op0=ALU.mult),
             reads=["ident_f", "cols"], writes=["diag"])
        S.op("pe", lambda e: e.matmul(PA[:, 0:128], lhsT=ones_f[:], rhs=diag[:], start=True, stop=True), reads=["ones_f", "diag"], writes=["PA"])
        S.op("act", lambda e, ch=ch: e.activation(out=gate2B[:, ch * 128:(ch + 1) * 128], in_=PA[:, 0:128], func=AF.Copy), reads=["PA"], writes=["gate2B"])

    W01 = [W0, W1]
    xt2b = [xt2, alloc("xt2b", [128, D], F32)]
    xn2b = [xn2, alloc("xn2b", [128, D], F32)]
    qTb = [qT, alloc("qTb", [128, 16, 128], BF16)]
    scb = [sc, alloc("scb", [128, 16, 128], F32)]
    work16 = alloc("work16", [128, 16, 128], F32)
    oh2 = alloc("oh2", [128, 128, 16], F32)
    ohs = [oh, oh2]
    Ireps = [alloc("Irep%d" % i, [128, 16, 128], F32) for i in range(2)]
    cand2 = alloc("cand2", [128, 8, 256], F32)
    NT25 = NT if stage != 3 else NDBG

    def front(ti):
        p = ti % 2
        xt2_, xn2_, qT_, sc_ = xt2b[p], xn2b[p], qTb[p], scb[p]
        kx, kn, kq, ks = "xt2_%d" % p, "xn2_%d" % p, "qT_%d" % p, "sc_%d" % p
        S.op("sp", lambda e: e.dma_start(out=xt2_[:], in_=out_d[ti * 128:(ti + 1) * 128, :]), reads=[("out_d", ti)], writes=[kx], dma_chan="c_xt2_%d" % p)
        S.op("act", lambda e: e.activation(out=xn2_[:], in_=xt2_[:], func=AF.Square, accum_out=ss2[:, p:p + 1]), reads=[kx], writes=[kn, ("ss2", p)])
        S.op("pool", lambda e: e.tensor_scalar(out=ss2[:, 2 + p:3 + p], in0=ss2[:, p:p + 1], scalar1=1.0 / D, scalar2=EPS, op0=ALU.mult, op1=ALU.add),
             reads=[("ss2", p)], writes=[("ss2b", p)])
        S.op("pool", lambda e: e.tensor_tensor(out=rstd2[:, p:p + 1], in0=ss2[:, 2 + p:3 + p], in1=neghalf[:, 0:1], op=ALU.pow), reads=[("ss2b", p), "neghalf"], writes=[("rstd2", p)])
        S.op("act", lambda e: e.activation(out=xn2_[:], in_=xt2_[:], func=AF.Copy, scale=rstd2[:, p:p + 1]), reads=[kx, ("rstd2", p)], writes=[kn])
        for ch in range(8):
            S.op("pe", lambda e, ch=ch: e.transpose(W2[:, ch * 128:(ch + 1) * 128], xn2_[:, ch * 128:(ch + 1) * 128], ident_f[:]),
                 reads=[kn, "ident_f"], writes=kW2)
        for ch in range(8):
            S.op("act", lambda e, ch=ch: e.activation(out=h2T[:, ch, ti * 128:(ti + 1) * 128], in_=W2[:, ch * 128:(ch + 1) * 128], func=AF.Identity,
                                                      scale=cols[:, 2, ch:ch + 1], bias=cols[:, 3, ch:ch + 1]),
                 reads=[kW2[ch // 4], "cols"], writes=[("h2T", ti, ch)])
        for blk in range(16):
            dst = W01[blk // 8][:, (blk % 8) * 128:(blk % 8 + 1) * 128]
            for ch in range(8):
                S.op("pe", lambda e, blk=blk, ch=ch, dst=dst: e.matmul(dst, lhsT=wqb[:, ch, blk * 128:(blk + 1) * 128], rhs=h2T[:, ch, ti * 128:(ti + 1) * 128],
                                                                      start=(ch == 0), stop=(ch == 7)),
                     reads=["wqb", ("h2T", ti, ch)], writes=["W%d" % (blk // 8)])
        qTf = qT_[:].rearrange("p b t -> p (b t)")
        S.op("act", lambda e: e.activation(out=qTf[:, 0:1024], in_=W0, func=AF.Copy), reads=["W0"], writes=[kq])
        S.op("act", lambda e: e.activation(out=qTf[:, 1024:2048], in_=W1, func=AF.Copy), reads=["W1"], writes=[kq])
        for blk in range(16):
            dst = W01[blk // 8][:, (blk % 8) * 128:(blk % 8 + 1) * 128]
            kT_ = k1T if blk % 2 == 0 else k2T
            S.op("pe", lambda e, blk=blk, dst=dst, kT_=kT_: e.matmul(dst, lhsT=qT_[:, blk, :], rhs=kT_[:], start=True, stop=True),
                 reads=[kq, "k1T", "k2T"], writes=["W%d" % (blk // 8)])
        scf = sc_[:].rearrange("p b k -> p (b k)")
        S.op("act", lambda e: e.activation(out=scf[:, 0:1024], in_=W0, func=AF.Copy), reads=["W0"], writes=[ks])
        S.op("act", lambda e: e.activation(out=scf[:, 1024:2048], in_=W1, func=AF.Copy), reads=["W1"], writes=[ks])

    def top16_multi(items):
        for (src, sk, n, vd, idd, wk_, tg) in items:
            S.op("dve", lambda e, src=src, vd=vd: e.max(out=vd[:, 0:8], in_=src), reads=[sk], writes=[("vt", tg)])
        for (src, sk, n, vd, idd, wk_, tg) in items:
            S.op("dve", lambda e, src=src, vd=vd, idd=idd: e.max_index(out=idd[:, 0:8], in_max=vd[:, 0:8], in_values=src), reads=[sk, ("vt", tg)], writes=[("it", tg)])
        for (src, sk, n, vd, idd, wk_, tg) in items:
            S.op("dve", lambda e, src=src, vd=vd, wk_=wk_: e.match_replace(out=wk_, in_to_replace=vd[:, 0:8], in_values=src, imm_value=-1e30),
                 reads=[sk, ("vt", tg)], writes=[("work", tg)])
        for (src, sk, n, vd, idd, wk_, tg) in items:
            S.op("dve", lambda e, vd=vd, wk_=wk_: e.max(out=vd[:, 8:16], in_=wk_), reads=[("work", tg)], writes=[("vt", tg)])
        for (src, sk, n, vd, idd, wk_, tg) in items:
            S.op("dve", lambda e, vd=vd, idd=idd, wk_=wk_: e.max_index(out=idd[:, 8:16], in_max=vd[:, 8:16], in_values=wk_), reads=[("work", tg), ("vt", tg)], writes=[("it", tg)])

    def back(ti, part):
        p = ti % 2
        sc_ = scb[p]
        ks = "sc_%d" % p
        vkeys = [("vt", t) for t in range(16)]
        ikeys_ = [("it", t) for t in range(16)]
        if part == 0:
            top16_multi([(sc_[:, blk, :], ks, 128, vtop[:, blk, :], iu[:, blk, :], work16[:, blk, :], blk) for blk in range(16)])
            S.op("dve", lambda e: e.tensor_copy(out=itf[:], in_=iu[:]), reads=ikeys_, writes=["itf"])
            for which in (0, 1):
                Irep = Ireps[which]
                for j in range(16):
                    S.op("act", lambda e, j=j, which=which, Irep=Irep: e.activation(out=Irep[:, j, :].rearrange("p (h k) -> p h k", h=8),
                                                                                    in_=itf[:, which::2, j:j + 1].to_broadcast([128, 8, 16]), func=AF.Copy),
                         reads=["itf"], writes=[("Irep", which, j)])
            for h in range(8):
                cv = cand[:, h, :].rearrange("p (a b) -> p a b", a=16)
                c2 = cand2[:, h, :].rearrange("p (a b) -> p a b", a=16)
                S.op("act", lambda e, h=h, cv=cv: e.activation(out=cv, in_=vtop[:, 2 * h, :].unsqueeze(2).to_broadcast([128, 16, 16]), func=AF.Copy),
                     reads=[("vt", 2 * h)], writes=[("cand", h)])
                S.op("act", lambda e, h=h, c2=c2: e.activation(out=c2, in_=vtop[:, 2 * h + 1, :].unsqueeze(1).to_broadcast([128, 16, 16]), func=AF.Copy),
                     reads=[("vt", 2 * h + 1)], writes=[("cand2", h)])
            return
        for hh in range(2):
            S.op("dve", lambda e, hh=hh: e.tensor_tensor(out=cand[:, hh * 4:(hh + 1) * 4, :], in0=cand[:, hh * 4:(hh + 1) * 4, :],
                                                         in1=cand2[:, hh * 4:(hh + 1) * 4, :], op=ALU.add),
                 reads=[("cand", h_) for h_ in range(hh * 4, hh * 4 + 4)] + [("cand2", h_) for h_ in range(hh * 4, hh * 4 + 4)],
                 writes=[("cand", h_) for h_ in range(hh * 4, hh * 4 + 4)])
        w8 = work16[:].rearrange("p (h a) k -> p h (a k)", h=8)
        top16_multi([(cand[:, h, :], ("cand", h), 256, ts[:, h, :], posu[:, h, :], w8[:, h, :], 100 + h) for h in range(8)])
        tkeys = [("vt", 100 + h) for h in range(8)]
        pkeys = [("it", 100 + h) for h in range(8)]
        S.op("dve", lambda e: e.tensor_tensor(out=ee[:], in0=ts[:], in1=ts[:, :, 0:1].to_broadcast([128, 8, 16]), op=ALU.subtract),
             reads=tkeys, writes=["ee"])
        S.op("act", lambda e: e.activation(out=ee[:], in_=ee[:], func=AF.Exp), reads=["ee"], writes=["ee"])
        S.op("dve", lambda e: e.tensor_single_scalar(out=k1u[:], in_=posu[:], scalar=4, op=ALU.logical_shift_right), reads=pkeys, writes=["k1u"])
        S.op("dve", lambda e: e.tensor_single_scalar(out=k2u[:], in_=posu[:], scalar=15, op=ALU.bitwise_and), reads=pkeys, writes=["k2u"])
        S.op("dve", lambda e: e.tensor_copy(out=k1f[:], in_=k1u[:]), reads=["k1u"], writes=["k1f"])
        S.op("dve", lambda e: e.tensor_copy(out=k2f[:], in_=k2u[:]), reads=["k2u"], writes=["k2f"])
        for which, kf, kfk in ((0, k1f, "k1f"), (1, k2f, "k2f")):
            Irep = Ireps[which]
            pr = ohs[which][:].rearrange("p a b -> p (a b)").rearrange("p (j m) -> p j m", j=16)
            kff = kf[:].rearrange("p h k -> p (h k)")
            for j in range(16):
                S.op("dve", lambda e, j=j, pr=pr, kff=kff, Irep=Irep: e.scalar_tensor_tensor(out=pr[:, j, :], in0=kff, scalar=float(j), in1=Irep[:, j, :],
                                                                                             op0=ALU.is_equal, op1=ALU.mult),
                     reads=[kfk, ("Irep", which, j)], writes=[("pr", which, j)])
            S.op("dve", lambda e, pr=pr: e.tensor_tensor(out=pr[:, 0:8, :], in0=pr[:, 0:8, :], in1=pr[:, 8:16, :], op=ALU.add),
                 reads=[("pr", which, j) for j in range(16)], writes=[("prs", which)])
            S.op("dve", lambda e, pr=pr: e.tensor_tensor(out=pr[:, 0:4, :], in0=pr[:, 0:4, :], in1=pr[:, 4:8, :], op=ALU.add),
                 reads=[("prs", which)], writes=[("prs", which)])
            S.op("dve", lambda e, pr=pr: e.tensor_tensor(out=pr[:, 0:2, :], in0=pr[:, 0:2, :], in1=pr[:, 2:4, :], op=ALU.add),
                 reads=[("prs", which)], writes=[("prs", which)])
            S.op("dve", lambda e, pr=pr, which=which: e.tensor_tensor(out=idx_tm[:, which, :], in0=pr[:, 0, :], in1=pr[:, 1, :], op=ALU.add),
                 reads=[("prs", which)], writes=[("idx_tm", which)])
        S.op("dve", lambda e: e.tensor_reduce(out=zz[:], in_=ee[:], axis=AX.X, op=ALU.add), reads=["ee"], writes=["zz"])
        S.op("dve", lambda e: e.reciprocal(out=zz[:], in_=zz[:]), reads=["zz"], writes=["zz"])
        S.op("dve", lambda e: e.tensor_tensor(out=idx_tm[:, 2, :].rearrange("p (h k) -> p h k", h=8), in0=ee[:],
                                              in1=zz[:].unsqueeze(2).to_broadcast([128, 8, 16]), op=ALU.mult),
             reads=["ee", "zz"], writes=[("idx_tm", 2)])
        for a, dstT in ((0, idx1T), (1, idx2T), (2, gT)):
            S.op("pe", lambda e, a=a: e.transpose(PA[:, a * 128:(a + 1) * 128], idx_tm[:, a, :], ident_f[:]), reads=[("idx_tm", a), "ident_f"], writes=["PA"])
        for a, dstT in ((0, idx1T), (1, idx2T), (2, gT)):
            S.op("act", lambda e, a=a, dstT=dstT: e.activation(out=dstT[:, ti * 128:(ti + 1) * 128], in_=PA[:, a * 128:(a + 1) * 128], func=AF.Copy),
                 reads=["PA"], writes=[("idxT", ti)])

    front(0)
    for ti in range(NT25):
        back(ti, 0)
        if ti + 1 < NT25:
            front(ti + 1)
        back(ti, 1)

    if stage == 3:
        dbg = dram("dbg", [128, 3, 128 * NDBG], kind="ExternalOutput")
        for a, dstT in ((0, idx1T), (1, idx2T), (2, gT)):
            S.op("sp", lambda e, a=a, dstT=dstT: e.dma_start(out=dbg[:, a, :], in_=dstT[:, 0:128 * NDBG]), reads=[("idxT", t) for t in range(NDBG)], writes=["dbg"], dma_chan="c_dbg")
        S.emit(final_waits=["c_out0", "c_out1", "c_dbg"])
        return nc

    S.barrier()
    M.off = mark_p3t
    TT = 256
    NB = 4
    G = alloc("G", [128, 128, TT], BF16)
    NBUF = 3
    dbuf = [alloc("dbuf%d" % i, [128, NB, 8, 128], BF16) for i in range(NBUF)]
    ubuf = [alloc("ubuf%d" % i, [128, NB, D], BF16) for i in range(NBUF)]
    SBT = 8
    p2oh = [alloc("p2oh%d" % i, [128, SBT, 128], BF16) for i in range(2)]
    p1t = [alloc("p1t%d" % i, [128, SBT, 128], BF16) for i in range(2)]
    p1w = [alloc("p1w%d" % i, [128, SBT, 128], BF16) for i in range(2)]
    Ab = [alloc("Ab%d" % i, [128, TT], BF16) for i in range(4)]
    Wb = [alloc("Wb%d" % i, [128, TT], BF16) for i in range(4)]
    x3 = [alloc("x3_%d" % i, [128, D], F32) for i in range(2)]
    y3 = [alloc("y3_%d" % i, [128, D], F32) for i in range(2)]
    ss3 = alloc("ss3", [128, 4], F32)
    rstd3 = alloc("rstd3", [128, 4], F32)
    print("SBUF used (P3):", M.off)
    PAB = [PA, PB]
    nwd = 0
    for T in range(2048 // TT if stage != 4 else 1):
        t0 = T * TT
        ikeys = [("idxT", t) for t in range(2 * T, 2 * T + 2)]
        W2h = [W2a, W2b]
        for sbi in range(TT // SBT):
            ts0 = t0 + sbi * SBT
            q = sbi % 2
            p2o, p1t_, p1w_ = p2oh[q], p1t[q], p1w[q]
            for tl in range(SBT):
                tk = ts0 + tl
                S.op("dve", lambda e, tk=tk, tl=tl, p2o=p2o: e.tensor_scalar(out=p2o[:, tl, :], in0=iota_b[:], scalar1=idx2T[:, tk:tk + 1], scalar2=None, op0=ALU.is_equal),
                     reads=ikeys + ["iota_b"], writes=[("p2oh", q, tl)])
                S.op("dve", lambda e, tk=tk, tl=tl, p1w_=p1w_: e.tensor_scalar(out=p1w_[:, tl, :], in0=iota_b[:], scalar1=idx1T[:, tk:tk + 1], scalar2=gT[:, tk:tk + 1],
                                                                             op0=ALU.is_equal, op1=ALU.mult),
                     reads=ikeys + ["iota_b"], writes=[("p1w", q, tl)])
            for grp in range(SBT // 4):
                hb = (sbi * (SBT // 4) + grp) % 2
                for tl in range(4):
                    tloc = grp * 4 + tl
                    S.op("pe", lambda e, tl=tl, tloc=tloc, hb=hb, p1w_=p1w_, p2o=p2o: e.matmul(W2h[hb][:, tl * 128:(tl + 1) * 128], lhsT=p1w_[:, tloc, :], rhs=p2o[:, tloc, :], start=True, stop=True),
                         reads=[("p1w", q, tloc), ("p2oh", q, tloc)], writes=[kW2[hb]])
                tg = sbi * SBT + grp * 4
                S.op("act", lambda e, tg=tg, hb=hb: e.activation(out=G[:, :, tg:tg + 4],
                                                                 in_=W2h[hb].rearrange("p (t i) -> p i t", t=4), func=AF.Copy),
                     reads=[kW2[hb]], writes=["G"])
        SK = 2
        NSL = 4
        Aps = [PA[:, 0:TT], PB[:, 0:TT]]
        binfo = {}
        for step in range(128 + SK):
            if step < 128:
                i2 = step
                j = i2 % NB
                if j == 0:
                    b = nwd % NBUF
                    nwd += 1
                    db_, ub_ = dbuf[b], ubuf[b]
                    S.op("sp", lambda e, i2=i2, db_=db_: e.dma_start(out=db_[:], in_=scr_down[:, i2:i2 + NB, :, :]), writes=["dbuf%d" % b], dma_chan="c_dbuf%d" % b)
                    S.op("sp", lambda e, i2=i2, ub_=ub_: e.dma_start(out=ub_[:], in_=scr_up[:, i2:i2 + NB, :]), writes=["ubuf%d" % b], dma_chan="c_ubuf%d" % b)
                binfo[i2] = (b, ub_, j)
                pp = i2 % NSL
                pq = i2 % 2
                pa = Aps[pq]
                for ch in range(8):
                    S.op("pe", lambda e, ch=ch, j=j, db_=db_, pa=pa, t0=t0: e.matmul(pa, lhsT=db_[:, j, ch, :], rhs=h2T[:, ch, t0:t0 + TT], start=(ch == 0), stop=(ch == 7)),
                         reads=["dbuf%d" % b, ("h2T", 2 * T), ("h2T", 2 * T + 1)], writes=["PAB%d" % pq])
                ab, wb_ = Ab[pp], Wb[pp]
                S.op("act", lambda e, ab=ab, pa=pa: e.activation(out=ab[:], in_=pa, func=AF.Gelu), reads=["PAB%d" % pq], writes=["Ab%d" % pp])
                S.op("pool", lambda e, ab=ab, wb_=wb_, i2=i2: e.tensor_tensor(out=wb_[:], in0=ab[:], in1=G[:, i2, :], op=ALU.mult),
                     reads=["Ab%d" % pp, "G"], writes=["Wb%d" % pp])
            if step >= SK:
                i2 = step - SK
                b2, ub2, j2 = binfo[i2]
                pp = i2 % NSL
                wb_ = Wb[pp]
                for tt in range(2):
                    for hf in range(2):
                        S.op("pe", lambda e, tt=tt, hf=hf, wb_=wb_, ub2=ub2, j2=j2, i2=i2: e.matmul(W01[tt][:, hf * 512:(hf + 1) * 512], lhsT=wb_[:, tt * 128:(tt + 1) * 128],
                                                                                                  rhs=ub2[:, j2, hf * 512:(hf + 1) * 512], start=(i2 == 0), stop=(i2 == 127)),
                             reads=["Wb%d" % pp, "ubuf%d" % b2], writes=["W%d" % tt])
        for tt in range(2):
            r0 = t0 + tt * 128
            okey = ("out_d", r0 // 128)
            xx, yy = x3[tt], y3[tt]
            S.op("sp", lambda e, r0=r0, xx=xx: e.dma_start(out=xx[:], in_=out_d[r0:r0 + 128, :]), reads=[okey], writes=["x3_%d" % tt], dma_chan="c_x3_%d" % tt)
            S.op("dve", lambda e, tt=tt, yy=yy: e.tensor_tensor(out=yy[:], in0=W01[tt], in1=gate2B[:], op=ALU.mult), reads=["W%d" % tt, "gate2B"], writes=["y3_%d" % tt])
            S.op("pool", lambda e, xx=xx, yy=yy: e.tensor_tensor(out=xx[:], in0=xx[:], in1=yy[:], op=ALU.add), reads=["x3_%d" % tt, "y3_%d" % tt], writes=["x3_%d" % tt])
            S.op("act", lambda e, xx=xx, yy=yy, tt=tt: e.activation(out=yy[:], in_=xx[:], func=AF.Square, accum_out=ss3[:, tt:tt + 1]),
                 reads=["x3_%d" % tt], writes=["y3_%d" % tt, "ss3"])
            S.op("dve", lambda e, tt=tt: e.tensor_scalar(out=ss3[:, 2 + tt:3 + tt], in0=ss3[:, tt:tt + 1], scalar1=1.0 / D, scalar2=EPS, op0=ALU.mult, op1=ALU.add),
                 reads=["ss3"], writes=["ss3"])
            S.op("pool", lambda e, tt=tt: e.tensor_tensor(out=rstd3[:, tt:tt + 1], in0=ss3[:, 2 + tt:3 + tt], in1=neghalf[:, 0:1], op=ALU.pow),
                 reads=["ss3", "neghalf"], writes=["rstd3"])
            S.op("dve", lambda e, xx=xx, yy=yy, tt=tt: e.scalar_tensor_tensor(out=yy[:], in0=xx[:], scalar=rstd3[:, tt:tt + 1], in1=finwB[:], op0=ALU.mult, op1=ALU.mult),
                 reads=["x3_%d" % tt, "rstd3", "finwB"], writes=["y3_%d" % tt])
            S.op("sp", lambda e, r0=r0, yy=yy: e.dma_start(out=out_d[r0:r0 + 128, :], in_=yy[:]), reads=["y3_%d" % tt], writes=[okey], dma_chan="c_fin%d" % tt)
    S.emit(final_waits=["c_out0", "c_out1", "c_fin0", "c_fin1"])
    return nc


def host_inputs(inputs):
    x = np.asarray(inputs["x"], np.float32)
    f = lambda k: np.asarray(inputs[k], np.float32)
    c = f("c")
    shared = {
        "w_ada": np.ascontiguousarray(f("w_ada")[0]),
        "b_ada": np.ascontiguousarray(f("b_ada")[0][None, :]),
        "w_in": np.ascontiguousarray(f("w_in")[0]),
        "w_alpha_up": np.ascontiguousarray(f("w_alpha_up")[0]),
        "b_alpha": np.ascontiguousarray(f("b_alpha")[0][None, :]),
        "sgu_wT": np.ascontiguousarray(f("sgu_w")[0].transpose(0, 2, 1)),
        "sgu_bT": np.ascontiguousarray(f("sgu_b")[0].T),
        "w_out": np.ascontiguousarray(f("w_out")[0]),
        "peer_w_q": np.ascontiguousarray(f("peer_w_q")[0]),
        "keys1T": np.ascontiguousarray(f("peer_keys1")[0].T),
        "keys2T": np.ascontiguousarray(f("peer_keys2")[0].T),
        "downT": np.ascontiguousarray(f("peer_down")[0].reshape(128, 128, 8, 128).transpose(3, 1, 2, 0)),
        "up": np.ascontiguousarray(f("peer_up")[0].reshape(128, 128, D)),
        "colv": np.ascontiguousarray(np.stack([f("norm1_w")[0].reshape(8, 128).T, f("norm2_w")[0].reshape(8, 128).T], axis=1)),
        "rowv": np.ascontiguousarray(np.stack([f("final_norm_w"), f("gla_norm_w")[0], f("sgu_ln_w")[0], f("sgu_ln_b")[0]], axis=0)),
    }
    maps = []
    for i in range(8):
        b, j = i // 4, i % 4
        xpre = np.zeros((3, 2048, D), np.float32)
        flags = np.zeros((128, 4), np.float32)
        flags[:, 3] = 1.0
        for s in range(3):
            qidx = s - (3 - j)
            if qidx >= 0:
                xpre[s] = x[b, qidx * 2048:(qidx + 1) * 2048]
                flags[:, s] = 1.0
        m = dict(shared)
        m["xpre"] = xpre
        m["xown"] = np.ascontiguousarray(x[b, j * 2048:(j + 1) * 2048])
        m["flags"] = flags
        m["cT"] = np.ascontiguousarray(c[b].reshape(8, 128).T)
        maps.append(m)
    return maps


_NC_CACHE = {}


def kernel(**inputs):
    maps = host_inputs(inputs)
    if "nc" not in _NC_CACHE:
        _NC_CACHE["nc"] = build()
    nc = _NC_CACHE["nc"]
    res = run_bass_kernel_spmd(nc, maps, core_ids=list(range(8)))
    out = np.zeros((2, 8192, D), np.float32)
    for i in range(8):
        b, j = i // 4, i % 4
        out[b, j * 2048:(j + 1) * 2048] = res.results[i]["out"]
    return out
```

```python
import contextlib
import numpy as np
import concourse.bass as bass
import concourse.mybir as mybir
from concourse.bass_utils import run_bass_kernel_spmd

F32 = mybir.dt.float32
BF16 = mybir.dt.bfloat16
U32 = mybir.dt.uint32
AF = mybir.ActivationFunctionType
ALU = mybir.AluOpType
AX = mybir.AxisListType

ENGS = ("pe", "act", "dve", "pool", "sp")
EPS = 1e-6
NT = 16
import os
NDBG = int(os.environ.get('NDBG', '1'))
D = 1024
IN_SIZES = (512, 512, 1024, 1024, 16, 1024, 1024, 1024, 1024)
IN_OFF = [int(v) for v in np.cumsum((0,) + IN_SIZES)]
OQ, OK_, OV, OR, OALR, OSU, OSV, OGA, OGB = IN_OFF[:9]


class _Op:
    __slots__ = ("eng", "fn", "waits", "sig", "is_dma", "chan")


class Sched:
    def __init__(self, nc):
        self.nc = nc
        self.q = {e: [] for e in ENGS}
        self.chan_count = {}
        self.last_w = {}
        self.readers = {}
        self.cap = None

    def _add_wait(self, op, tok):
        if tok is None:
            return
        if tok[0] == "e":
            if tok[1] == op.eng and tok[1] == "pe":
                return
            if tok[2].sig is None:
                tok[2].sig = 1
        op.waits.append(tok)

    def op(self, eng, fn, reads=(), writes=(), dma_chan=None):
        if self.cap is not None:
            self.cap.append((eng, fn, tuple(reads), tuple(writes), dma_chan))
            return None
        o = _Op()
        o.eng = eng
        o.fn = fn
        o.waits = []
        o.sig = None
        o.is_dma = dma_chan is not None
        o.chan = dma_chan
        for k in reads:
            self._add_wait(o, self.last_w.get(k))
        for k in writes:
            self._add_wait(o, self.last_w.get(k))
            lastr = {}
            for t in self.readers.get(k, ()):
                if t[0] == "e":
                    lastr[t[1]] = t
                else:
                    self._add_wait(o, t)
            for t in lastr.values():
                self._add_wait(o, t)
        if o.is_dma:
            self.chan_count[dma_chan] = self.chan_count.get(dma_chan, 0) + 1
            tok = ("d", dma_chan, 16 * self.chan_count[dma_chan])
        else:
            tok = ("e", eng, o)
        for k in writes:
            self.last_w[k] = tok
            self.readers[k] = []
        for k in reads:
            self.readers.setdefault(k, []).append(tok)
        self.q[eng].append(o)
        return o

    def barrier(self):
        toks = []
        for e in ENGS:
            last = None
            for o in reversed(self.q[e]):
                if not o.is_dma:
                    last = o
                    break
            if last is not None:
                toks.append(("e", e, last))
        for c, n in self.chan_count.items():
            toks.append(("d", c, 16 * n))
        for e in ENGS:
            o = _Op()
            o.eng = e
            o.fn = lambda eng: eng.nop()
            o.waits = []
            o.sig = None
            o.is_dma = False
            o.chan = None
            for t in toks:
                if t[0] == "e":
                    if t[2].sig is None:
                        t[2].sig = 1
                o.waits.append(t)
            self.q[e].append(o)
        self.last_w = {}
        self.readers = {}

    def emit(self, final_waits=()):
        nc = self.nc
        for e in ENGS:
            n = 0
            for o in self.q[e]:
                if o.sig is not None and not o.is_dma:
                    n += 1
                    o.sig = n
        chans = sorted(self.chan_count.keys(), key=str)
        print("ops:", {e: len(self.q[e]) for e in ENGS}, "chan max:", max(self.chan_count.values()) * 16)
        with contextlib.ExitStack() as st:
            esem = {e: st.enter_context(nc.semaphore("s_" + e)) for e in ENGS}
            csem = {c: st.enter_context(nc.semaphore("c_%d" % i)) for i, c in enumerate(chans)}
            block = st.enter_context(nc.Block())

            def run(ename, eng):
                seen = {}
                for o in self.q[ename]:
                    for w in o.waits:
                        if w[0] == "e":
                            key = ("e", w[1]); val = w[2].sig; sem = esem[w[1]]
                        else:
                            key = ("d", w[1]); val = w[2]; sem = csem[w[1]]
                        if seen.get(key, 0) >= val:
                            continue
                        seen[key] = val
                        eng.wait_ge(sem, val)
                    ins = o.fn(eng)
                    if o.is_dma:
                        ins.then_inc(csem[o.chan], 16)
                    elif o.sig is not None:
                        ins.then_inc(esem[ename], 1)
                if ename == "sp":
                    for c in final_waits:
                        eng.wait_ge(csem[c], 16 * self.chan_count[c])

            @block.sync
            def _(eng):
                run("sp", eng)

            @block.scalar
            def _(eng):
                run("act", eng)

            @block.vector
            def _(eng):
                run("dve", eng)

            @block.gpsimd
            def _(eng):
                run("pool", eng)

            @block.tensor
            def _(eng):
                run("pe", eng)


class Mem:
    def __init__(self, nc, base=20608, limit=229376):
        self.nc = nc
        self.off = base
        self.limit = limit
        self.n = 0

    def alloc(self, name, shape, dt):
        size = 1
        for s in shape[1:]:
            size *= s
        size *= {F32: 4, BF16: 2, U32: 4}[dt]
        size = (size + 63) // 64 * 64
        assert self.off + size <= self.limit, (name, self.off, size)
        self.n += 1
        t = self.nc.alloc_sbuf_tensor_at("%s_%d" % (name, self.n), list(shape), dt, offset=self.off)
        self.off += size
        return t


def _dtsize(dt):
    return {F32: 4, BF16: 2, U32: 4}[dt]


def build(stage=9):
    nc = bass.Bass("TRN2", target_bir_lowering=False)
    S = Sched(nc)
    M = Mem(nc)
    M_alloc = M.alloc

    def dram(name, shape, kind="ExternalInput", dt=F32):
        return nc.dram_tensor(name, list(shape), dt, kind=kind).ap()

    xpre = dram("xpre", [3, 2048, D])
    xown = dram("xown", [2048, D])
    flags_d = dram("flags", [128, 4])
    cT_d = dram("cT", [128, 8])
    colv_d = dram("colv", [128, 2, 8])
    rowv_d = dram("rowv", [4, D])
    wada_d = dram("w_ada", [D, 6 * D])
    bada_d = dram("b_ada", [1, 6 * D])
    win_d = dram("w_in", [D, IN_OFF[9]])
    wup_d = dram("w_alpha_up", [16, 512])
    balpha_d = dram("b_alpha", [1, 512])
    sguw_d = dram("sgu_wT", [8, 128, 128])
    sgub_d = dram("sgu_bT", [128, 8])
    wout_d = dram("w_out", [D, D])
    wq_d = dram("peer_w_q", [D, 2048])
    k1T_d = dram("keys1T", [128, 128])
    k2T_d = dram("keys2T", [128, 128])
    downT_d = dram("downT", [128, 128, 8, 128])
    up_d = dram("up", [128, 128, D])
    out_d = dram("out", [2048, D], kind="ExternalOutput")
    scr_down = nc.dram_tensor("scr_down", [128, 128, 8, 128], BF16).ap()
    scr_up = nc.dram_tensor("scr_up", [128, 128, D], BF16).ap()
    cvt_jobs = []
    for g in range(32):
        cvt_jobs.append((scr_down[:, g * 4:(g + 1) * 4, :, :], downT_d[:, g * 4:(g + 1) * 4, :, :]))
        cvt_jobs.append((scr_up[:, g * 4:(g + 1) * 4, :], up_d[:, g * 4:(g + 1) * 4, :]))

    def issue_cvt(n):
        for _ in range(n):
            if cvt_jobs:
                o_, i_ = cvt_jobs.pop(0)
                S.op("pool", lambda e, o_=o_, i_=i_: e.dma_start(out=o_, in_=i_), writes=["scr"], dma_chan="c_cvt")

    P01 = nc.alloc_psum_tensor("P01", [128, 1024], F32)
    P23 = nc.alloc_psum_tensor("P23", [128, 1024], F32)
    P45 = nc.alloc_psum_tensor("P45", [128, 1024], F32)
    P67 = nc.alloc_psum_tensor("P67", [128, 1024], F32)
    W0 = P01[:, :]; W1 = P23[:, :]; W2 = P67[:, :]
    W0a = P01[:, 0:512]; W0b = P01[:, 512:1024]
    W2a = P67[:, 0:512]; W2b = P67[:, 512:1024]
    PA = P45[:, 0:512]; PB = P45[:, 512:1024]
    kW2 = ["W2a", "W2b"]

    def alloc(name, shape, dt):
        return M_alloc(name, shape, dt)

    ident_f = alloc("ident_f", [128, 128], F32)
    ident_b = alloc("ident_b", [128, 128], BF16)
    iota_f = alloc("iota_f", [128, 128], F32)
    iota_b = alloc("iota_b", [128, 128], BF16)
    Lm = alloc("Lm", [128, 128], F32)
    Rm = alloc("Rm", [128, 128], F32)
    maskU = alloc("maskU", [128, 128], F32)
    ones_f = alloc("ones_f", [128, 128], F32)
    neghalf = alloc("neghalf", [128, 8], F32)
    flags = alloc("flagsb", [128, 4], F32)
    cols = alloc("cols", [128, 6, 8], F32)
    mark_p3 = M.off
    Bsgu = alloc("Bsgu", [128, 8], F32)
    WsT = alloc("WsT", [128, 8, 128], BF16)
    wup_f = alloc("wup_f", [16, 512], F32)
    balpha_f = alloc("balpha_f", [1, 512], F32)
    lnwB = alloc("lnwB", [128, D], F32)
    lnbB = alloc("lnbB", [128, D], F32)
    gnwB = alloc("gnwB", [128, D], F32)
    woutg = alloc("woutg", [128, 8, D], BF16)
    wk = alloc("wk", [128, 8, 512], BF16)
    wv = alloc("wv", [128, 8, 1024], BF16)
    walr = alloc("walr", [128, 8, 16], BF16)
    S_f = alloc("S_f", [128, 4, 256], F32)
    Sb0 = alloc("Sb0", [128, 4, 256], BF16)
    Sb1 = alloc("Sb1", [128, 4, 256], BF16)
    mark_late = M.off

    pid = alloc("pid", [128, 1], F32)
    tmpA = alloc("tmpA", [128, 128], F32)
    tmpB = alloc("tmpB", [128, 128], F32)
    cj = alloc("cj", [128, 1], F32)
    scB = alloc("scB", [128, 8, 128], F32)
    cT = alloc("cTs", [128, 8], F32)
    sigc = alloc("sigc", [128, 8], F32)
    colv = alloc("colvs", [128, 2, 8], F32)
    modB = alloc("modB", [128, 6 * D], F32)
    wbuf = [alloc("wbuf%d" % i, [128, 8, 512], F32) for i in range(4)]
    bb = [alloc("bb%d" % i, [128, 512], F32) for i in range(4)]
    wo_tmp = alloc("wo_tmp", [128, 4, D], F32)
    sgu_tmp = alloc("sgu_tmp", [128, 8, 128], F32)

    def iota(e, out, pattern, base, cm):
        return e.iota(out, pattern=pattern, base=base, channel_multiplier=cm,
                      allow_small_or_imprecise_dtypes=True)

    S.op("pool", lambda e: iota(e, iota_f[:], [[1, 128]], 0, 0), writes=["iota_f"])
    S.op("dve", lambda e: e.tensor_copy(out=iota_b[:], in_=iota_f[:]), reads=["iota_f"], writes=["iota_b"])
    S.op("pool", lambda e: iota(e, pid[:], [[0, 1]], 0, 1), writes=["pid"])
    S.op("pool", lambda e: iota(e, tmpA[:], [[1, 128]], 0, -1), writes=["tmpA"])
    S.op("pool", lambda e: e.memset(ones_f[:], 1.0), writes=["ones_f"])
    S.op("pool", lambda e: e.memset(neghalf[:], -0.5), writes=["neghalf"])
    S.op("pool", lambda e: e.memset(S_f[:], 0.0), writes=["S_f"])
    S.op("dve", lambda e: e.tensor_single_scalar(out=ident_f[:], in_=tmpA[:], scalar=0.0, op=ALU.is_equal),
         reads=["tmpA"], writes=["ident_f"])
    S.op("dve", lambda e: e.tensor_copy(out=ident_b[:], in_=ident_f[:]), reads=["ident_f"], writes=["ident_b"])
    S.op("dve", lambda e: e.tensor_single_scalar(out=cj[:], in_=pid[:], scalar=64.0, op=ALU.is_ge),
         reads=["pid"], writes=["cj"])
    S.op("dve", lambda e: e.tensor_scalar(out=tmpB[:], in0=iota_f[:], scalar1=64.0, scalar2=cj[:, 0:1],
                                          op0=ALU.is_ge, op1=ALU.is_equal),
         reads=["iota_f", "cj"], writes=["tmpB"])
    maskS = alloc("maskS", [128, 128], F32)
    S.op("dve", lambda e: e.tensor_single_scalar(out=maskS[:], in_=tmpA[:], scalar=0.0, op=ALU.is_ge),
         reads=["tmpA"], writes=["maskS"])
    S.op("dve", lambda e: e.tensor_tensor(out=maskU[:], in0=maskS[:], in1=tmpB[:], op=ALU.mult),
         reads=["maskS", "tmpB"], writes=["maskU"])
    S.op("dve", lambda e: e.tensor_scalar(out=Lm[:], in0=maskU[:], scalar1=-1.0 / 16.0, scalar2=None, op0=ALU.mult),
         reads=["maskU"], writes=["Lm"])
    S.op("dve", lambda e: e.tensor_tensor(out=Rm[:], in0=tmpB[:], in1=maskU[:], op=ALU.subtract),
         reads=["tmpB", "maskU"], writes=["Rm"])
    S.op("dve", lambda e: e.tensor_scalar(out=Rm[:], in0=Rm[:], scalar1=-1.0 / 16.0, scalar2=None, op0=ALU.mult),
         reads=["Rm"], writes=["Rm"])

    S.op("sp", lambda e: e.dma_start(out=flags[:], in_=flags_d[:, :]), writes=["flags"], dma_chan="c_flags")
    S.op("sp", lambda e: e.dma_start(out=cT[:], in_=cT_d[:, :]), writes=["cT"], dma_chan="c_cT")
    S.op("sp", lambda e: e.dma_start(out=colv[:], in_=colv_d[:, :, :]), writes=["colv"], dma_chan="c_colv")
    S.op("sp", lambda e: e.dma_start(out=Bsgu[:], in_=sgub_d[:, :]), writes=["Bsgu"], dma_chan="c_bsgu")
    S.op("sp", lambda e: e.dma_start(out=wup_f[:], in_=wup_d[:, :]), writes=["wup_f"], dma_chan="c_wup")
    S.op("sp", lambda e: e.dma_start(out=balpha_f[:], in_=balpha_d[:, :]), writes=["balpha_f"], dma_chan="c_balpha")
    S.op("act", lambda e: e.dma_start(out=gnwB[:], in_=rowv_d[1:2, :].to_broadcast([128, D])), writes=["gnwB"], dma_chan="c_gnw")
    S.op("act", lambda e: e.dma_start(out=lnwB[:], in_=rowv_d[2:3, :].to_broadcast([128, D])), writes=["lnwB"], dma_chan="c_lnw")
    S.op("act", lambda e: e.dma_start(out=lnbB[:], in_=rowv_d[3:4, :].to_broadcast([128, D])), writes=["lnbB"], dma_chan="c_lnb")
    S.op("sp", lambda e: e.dma_start(out=sgu_tmp[:], in_=sguw_d.rearrange("g j i -> j g i")), writes=["sgu_tmp"], dma_chan="c_sguw")
    win_v = win_d.rearrange("(c p) e -> p c e", p=128)
    for ch in range(8):
        S.op("pool", lambda e, ch=ch: e.dma_start(out=wk[:, ch, :], in_=win_v[:, ch, OK_:OK_ + 512]), writes=["wk"], dma_chan="c_wk")
        S.op("pool", lambda e, ch=ch: e.dma_start(out=wv[:, ch, :], in_=win_v[:, ch, OV:OV + 1024]), writes=["wv"], dma_chan="c_wv")
        S.op("pool", lambda e, ch=ch: e.dma_start(out=walr[:, ch, :], in_=win_v[:, ch, OALR:OALR + 16]), writes=["walr"], dma_chan="c_walr")

    S.op("dve", lambda e: e.tensor_tensor(out=WsT[:], in0=sgu_tmp[:], in1=maskS[:].unsqueeze(1).to_broadcast([128, 8, 128]), op=ALU.mult),
         reads=["sgu_tmp", "maskS"], writes=["WsT"])

    S.op("act", lambda e: e.activation(out=sigc[:], in_=cT[:], func=AF.Sigmoid), reads=["cT"], writes=["sigc"])
    S.op("dve", lambda e: e.tensor_tensor(out=sigc[:], in0=sigc[:], in1=cT[:], op=ALU.mult), reads=["sigc", "cT"], writes=["sigc"])
    S.op("dve", lambda e: e.tensor_copy(out=scB[:], in_=sigc[:].unsqueeze(2).to_broadcast([128, 8, 128])),
         reads=["sigc"], writes=["scB"])

    wada_v = wada_d.rearrange("(c p) e -> p c e", p=128)
    for g in range(12):
        b = g % 4
        wb_, bb_ = wbuf[b], bb[b]
        kq = "sp" if b % 2 == 0 else "act"
        S.op(kq, lambda e, g=g, wb_=wb_: e.dma_start(out=wb_[:], in_=wada_v[:, :, g * 512:(g + 1) * 512]),
             writes=["wbuf%d" % b], dma_chan="c_wbuf%d" % b)
        S.op(kq, lambda e, g=g, bb_=bb_: e.dma_start(out=bb_[:], in_=bada_d[0:1, g * 512:(g + 1) * 512].to_broadcast([128, 512])),
             writes=["bb%d" % b], dma_chan="c_bb%d" % b)
        for ch in range(8):
            S.op("pe", lambda e, ch=ch, wb_=wb_: e.matmul(PA, lhsT=scB[:, ch, :], rhs=wb_[:, ch, :], start=(ch == 0), stop=(ch == 7)),
                 reads=["scB", "wbuf%d" % b], writes=["PA"])
        S.op("dve", lambda e, g=g, bb_=bb_: e.tensor_tensor(out=modB[:, g * 512:(g + 1) * 512], in0=PA, in1=bb_[:], op=ALU.add),
             reads=["PA", "bb%d" % b], writes=["modB"])

    def col_from_mod(dst_idx, mod_off):
        for ch in range(8):
            S.op("pe", lambda e, ch=ch: e.transpose(PB[:, 0:128], modB[:, mod_off + ch * 128: mod_off + (ch + 1) * 128], ident_f[:]),
                 reads=["modB", "ident_f"], writes=["PB"])
            S.op("dve", lambda e, ch=ch: e.tensor_copy(out=cols[:, dst_idx, ch:ch + 1], in_=PB[:, 0:1]),
                 reads=["PB"], writes=["cols"])
    col_from_mod(1, 0 * D)
    col_from_mod(0, 1 * D)
    col_from_mod(3, 3 * D)
    col_from_mod(2, 4 * D)
    col_from_mod(4, 5 * D)
    for di, ci in ((0, 0), (2, 1)):
        S.op("dve", lambda e, di=di, ci=ci: e.scalar_tensor_tensor(out=cols[:, di, :], in0=cols[:, di, :], scalar=1.0,
                                                                   in1=colv[:, ci, :], op0=ALU.add, op1=ALU.mult),
             reads=["cols", "colv"], writes=["cols"])

    wout_v = wout_d.rearrange("(c p) e -> p c e", p=128)
    for hf in range(2):
        S.op("sp", lambda e, hf=hf: e.dma_start(out=wo_tmp[:], in_=wout_v[:, hf * 4:(hf + 1) * 4, :]), writes=["wo_tmp"], dma_chan="c_wo")
        S.op("dve", lambda e, hf=hf: e.tensor_tensor(out=woutg[:, hf * 4:(hf + 1) * 4, :], in0=wo_tmp[:],
                                                     in1=modB[:, 2 * D:3 * D].unsqueeze(1).to_broadcast([128, 4, D]), op=ALU.mult),
             reads=["wo_tmp", "modB"], writes=["woutg"])

    if stage == 0:
        dbg = dram("dbg", [128, 2048], kind="ExternalOutput")
        S.op("sp", lambda e: e.dma_start(out=dbg[:, 0:48], in_=cols[:].rearrange("p a b -> p (a b)")), reads=["cols"], writes=["dbg"], dma_chan="c_out")
        S.op("sp", lambda e: e.dma_start(out=dbg[:, 128:256], in_=Lm[:]), reads=["Lm"], writes=["dbg"], dma_chan="c_out")
        S.op("sp", lambda e: e.dma_start(out=dbg[:, 256:384], in_=Rm[:]), reads=["Rm"], writes=["dbg"], dma_chan="c_out")
        S.op("sp", lambda e: e.dma_start(out=dbg[:, 1024:2048], in_=modB[:, 2048:3072]), reads=["modB"], writes=["dbg"], dma_chan="c_out")
        S.emit(final_waits=["c_out0", "c_out1"])
        return nc
    S.barrier()
    M.off = mark_late

    wq = alloc("wq", [128, 8, 512], BF16)
    wr = alloc("wr", [128, 8, 1024], BF16)
    wsu = alloc("wsu", [128, 8, 1024], BF16)
    wsv = alloc("wsv", [128, 8, 1024], BF16)
    wga = alloc("wga", [128, 8, 1024], BF16)
    wgb = alloc("wgb", [128, 8, 1024], BF16)
    mark_work = M.off

    xt = alloc("xt", [128, D], F32)
    xtB = alloc("xtB", [128, D], F32)
    mTb = alloc("mTb", [128, 8, 128], BF16)
    Fs = [alloc("F%d" % i, [128, D], F32) for i in range(4)]
    xn, t2_, t1_, u_, gv_ = Fs[0], Fs[0], Fs[1], Fs[2], Fs[3]
    hT = alloc("hT", [128, 8, 128], BF16)
    v_bf = alloc("v_bf", [128, D], BF16)
    alrT = alloc("alrT", [16, 128], F32)
    bufE = alloc("bufE", [128, 512], F32)
    lbuf = alloc("lbuf", [128, 512], F32)
    expnG = alloc("expnG", [128, 512], F32)
    ktail = alloc("ktail", [128, 512], BF16)
    dec = alloc("dec", [128, 4, 2], F32)
    ss = alloc("ss", [128, 8], F32)
    rstd = alloc("rstd", [128, 8], F32)
    qd = alloc("qd", [128, 4, 128], BF16)
    q0 = alloc("q0", [128, 4, 128], BF16)
    q1 = alloc("q1", [128, 4, 128], BF16)
    ki = alloc("ki", [128, 4, 128], BF16)
    ATm = alloc("ATm", [128, 4, 128], BF16)
    vln = alloc("vln", [128, D], BF16)
    bnst = alloc("bnst", [128, 2, 6], F32)
    mv = alloc("mv", [128, 2], F32)
    print("SBUF used (P2):", M.off)

    S.op("pool", lambda e: e.memset(q0[:], 0.0), writes=["q0"])
    S.op("pool", lambda e: e.memset(q1[:], 0.0), writes=["q1"])

    def load_late():
        for ch in range(8):
            for (wt, off, n, nm) in ((wq, OQ, 512, "wq"), (wr, OR, 1024, "wr"), (wga, OGA, 1024, "wga"),
                                     (wsu, OSU, 1024, "wsu"), (wsv, OSV, 1024, "wsv"), (wgb, OGB, 1024, "wgb")):
                S.op("pool", lambda e, ch=ch, wt=wt, off=off, n=n: e.dma_start(out=wt[:, ch, :], in_=win_v[:, ch, off:off + n]),
                     writes=[nm], dma_chan="c_" + nm)

    def proj_tm(dst, dkeys, wt, wkey, c0, n):
        for ch in range(8):
            S.op("pe", lambda e, ch=ch: e.matmul(dst, lhsT=hT[:, ch, :], rhs=wt[:, ch, c0:c0 + n], start=(ch == 0), stop=(ch == 7)),
                 reads=[("hT", ch), wkey], writes=dkeys)

    xts = [xt, xtB]

    def p2_front(k):
        xk = xts[k % 2]
        kx = "xt%d" % (k % 2)
        x_ap = xown[k * 128:(k + 1) * 128, :]
        S.op("sp", lambda e: e.dma_start(out=xk[:], in_=x_ap), writes=[kx], dma_chan="c_xt%d" % (k % 2))
        S.op("act", lambda e: e.activation(out=xn[:], in_=xk[:], func=AF.Square, accum_out=ss[:, 0:1]),
             reads=[kx], writes=["F0", "ss_f"])
        S.op("dve", lambda e: e.tensor_scalar(out=ss[:, 1:2], in0=ss[:, 0:1], scalar1=1.0 / D, scalar2=EPS, op0=ALU.mult, op1=ALU.add),
             reads=["ss_f"], writes=["ss_f2"])
        S.op("pool", lambda e: e.tensor_tensor(out=rstd[:, 0:1], in0=ss[:, 1:2], in1=neghalf[:, 0:1], op=ALU.pow),
             reads=["ss_f2", "neghalf"], writes=["rstd_f"])
        S.op("dve", lambda e: e.tensor_scalar(out=xn[:], in0=xk[:], scalar1=rstd[:, 0:1], scalar2=None, op0=ALU.mult),
             reads=[kx, "rstd_f"], writes=["F0"])
        for ch in range(8):
            S.op("pe", lambda e, ch=ch: e.transpose(W0[:, ch * 128:(ch + 1) * 128], xn[:, ch * 128:(ch + 1) * 128], ident_f[:]),
                 reads=["F0", "ident_f"], writes=["W0a", "W0b"])
        for ch in range(8):
            if ch < 4:
                S.op("dve", lambda e, ch=ch: e.tensor_scalar(out=hT[:, ch, :], in0=W0[:, ch * 128:(ch + 1) * 128],
                                                             scalar1=cols[:, 0, ch:ch + 1], scalar2=cols[:, 1, ch:ch + 1],
                                                             op0=ALU.mult, op1=ALU.add),
                     reads=["W0a" if ch < 4 else "W0b", "cols"], writes=[("hT", ch)])
            else:
                S.op("act", lambda e, ch=ch: e.activation(out=hT[:, ch, :], in_=W0[:, ch * 128:(ch + 1) * 128], func=AF.Identity,
                                                          scale=cols[:, 0, ch:ch + 1], bias=cols[:, 1, ch:ch + 1]),
                     reads=["W0a" if ch < 4 else "W0b", "cols"], writes=[("hT", ch)])

    def p2_body(k):
        seg, full, row0 = 3, True, k * 128
        fcol = flags[:, seg:seg + 1]
        capA = []
        S.cap = capA
        proj_tm(PA, ["PA"], wk, "wk", 0, 512)
        proj_tm(W1[:, 0:512], ["W1"], wv, "wv", 0, 512)
        proj_tm(W1[:, 512:1024], ["W1"], wv, "wv", 512, 512)
        for ch in range(8):
            S.op("pe", lambda e, ch=ch: e.matmul(PB[0:16, 0:128], lhsT=walr[:, ch, :], rhs=hT[:, ch, :], start=(ch == 0), stop=(ch == 7)),
                 reads=[("hT", ch), "walr"], writes=["PB"])
        S.op("dve", lambda e: e.tensor_scalar(out=v_bf[:], in0=W1, scalar1=fcol, scalar2=None, op0=ALU.mult),
             reads=["W1", "flags"], writes=["v_bf"])
        S.op("act", lambda e: e.activation(out=alrT[:], in_=PB[0:16, 0:128], func=AF.Copy), reads=["PB"], writes=["alrT"])
        S.op("pe", lambda e: e.matmul(W0a, lhsT=alrT[:], rhs=wup_f[:], start=True, stop=False),
             reads=["alrT", "wup_f"], writes=["W0a"])
        S.op("pe", lambda e: e.matmul(W0a, lhsT=ones_f[0:1, :], rhs=balpha_f[:], start=False, stop=True),
             reads=["ones_f", "balpha_f"], writes=["W0a"])
        S.op("act", lambda e: e.activation(out=bufE[:], in_=W0a, func=AF.Exp, scale=-1.0), reads=["W0a"], writes=["bufE"])
        S.op("act", lambda e: e.activation(out=lbuf[:], in_=bufE[:], func=AF.Ln, bias=1.0, scale=1.0), reads=["bufE"], writes=["lbuf"])
        S.op("pe", lambda e: e.matmul(W0b, lhsT=Rm[:], rhs=lbuf[:], start=True, stop=True), reads=["Rm", "lbuf"], writes=["W0b"])
        for h in range(4):
            S.op("pe", lambda e, h=h: e.matmul(PB[:, h * 128:(h + 1) * 128], lhsT=lbuf[:, h * 128:(h + 1) * 128], rhs=Lm[:], start=True, stop=True),
                 reads=["lbuf", "Lm"], writes=["PB"])
        S.op("act", lambda e: e.activation(out=bufE[:], in_=W0b, func=AF.Exp), reads=["W0b"], writes=["bufE"])
        S.op("dve", lambda e: e.tensor_tensor(out=ktail[:], in0=PA, in1=bufE[:], op=ALU.mult), reads=["PA", "bufE"], writes=["ktail"])
        PBv = PB.rearrange("p (h t) -> p h t", h=4)
        S.op("act", lambda e: e.activation(out=dec[:], in_=PBv[:, :, 63:128:64], func=AF.Exp), reads=["PB"], writes=["dec"])
        if full:
            for h in range(4):
                for ch in range(8):
                    S.op("pe", lambda e, h=h, ch=ch: e.matmul(W0a[:, h * 128:(h + 1) * 128], lhsT=wq[:, ch, h * 128:(h + 1) * 128], rhs=hT[:, ch, :],
                                                             start=(ch == 0), stop=(ch == 7)),
                         reads=[("hT", ch), "wq"], writes=["W0a"])
            for h in range(4):
                for ch in range(8):
                    S.op("pe", lambda e, h=h, ch=ch: e.matmul(W0b[:, h * 128:(h + 1) * 128], lhsT=wk[:, ch, h * 128:(h + 1) * 128], rhs=hT[:, ch, :],
                                                             start=(ch == 0), stop=(ch == 7)),
                         reads=[("hT", ch), "wk"], writes=["W0b"])
            S.op("act", lambda e: e.activation(out=lbuf[:], in_=PB, func=AF.Exp), reads=["PB"], writes=["lbuf"])
            S.op("act", lambda e: e.activation(out=expnG[:], in_=PB, func=AF.Exp, scale=-1.0), reads=["PB"], writes=["expnG"])
            qdf = qd[:].rearrange("p h t -> p (h t)")
            kif = ki[:].rearrange("p h t -> p (h t)")
            S.op("dve", lambda e: e.scalar_tensor_tensor(out=qdf, in0=W0a, scalar=128.0 ** -0.5, in1=lbuf[:], op0=ALU.mult, op1=ALU.mult),
                 reads=["W0a", "lbuf"], writes=["qd"])
            S.op("dve", lambda e: e.tensor_tensor(out=kif, in0=W0b, in1=expnG[:], op=ALU.mult), reads=["W0b", "expnG"], writes=["ki"])
            S.op("pool", lambda e: e.tensor_copy(out=q0[:, :, 0:64], in_=qd[:, :, 0:64]), reads=["qd"], writes=["q0"])
            S.op("pool", lambda e: e.tensor_copy(out=q1[:, :, 64:128], in_=qd[:, :, 64:128]), reads=["qd"], writes=["q1"])
            for h in range(4):
                S.op("pe", lambda e, h=h: e.matmul(PA[:, h * 128:(h + 1) * 128], lhsT=ki[:, h, :], rhs=qd[:, h, :], start=True, stop=True),
                     reads=["ki", "qd"], writes=["PA"])
            S.op("dve", lambda e: e.tensor_tensor(out=ATm[:], in0=PA.rearrange("p (h t) -> p h t", h=4),
                                                  in1=maskU[:].unsqueeze(1).to_broadcast([128, 4, 128]), op=ALU.mult),
                 reads=["PA", "maskU"], writes=["ATm"])
        W1v = W1.rearrange("p (h v) -> p h v", h=4)

        def kv_update(c):
            for h in range(4):
                S.op("pe", lambda e, h=h: e.matmul(W1v[:, h, :], lhsT=ktail[c * 64:(c + 1) * 64, h * 128:(h + 1) * 128],
                                                   rhs=v_bf[c * 64:(c + 1) * 64, h * 256:(h + 1) * 256], start=True, stop=True),
                     reads=["ktail", "v_bf"], writes=["W1"])
            for h in range(4):
                S.op("dve", lambda e, h=h: e.scalar_tensor_tensor(out=S_f[:, h, :], in0=S_f[:, h, :], scalar=dec[:, h, c:c + 1],
                                                                  in1=W1v[:, h, :], op0=ALU.mult, op1=ALU.add),
                     reads=["S_f", "dec", "W1"], writes=["S_f"])

        kv_update(0)
        if full:
            S.op("act", lambda e: e.activation(out=Sb1[:], in_=S_f[:], func=AF.Copy), reads=["S_f"], writes=["Sb1"])
            W0v = W0.rearrange("p (h v) -> p h v", h=4)
            for h in range(4):
                S.op("pe", lambda e, h=h: e.matmul(W0v[:, h, :], lhsT=ATm[:, h, :], rhs=v_bf[:, h * 256:(h + 1) * 256], start=True, stop=False),
                     reads=["ATm", "v_bf"], writes=["W0a" if h < 2 else "W0b"])
                S.op("pe", lambda e, h=h: e.matmul(W0v[:, h, :], lhsT=q0[:, h, :], rhs=Sb0[:, h, :], start=False, stop=False),
                     reads=["q0", "Sb0"], writes=["W0a" if h < 2 else "W0b"])
                S.op("pe", lambda e, h=h: e.matmul(W0v[:, h, :], lhsT=q1[:, h, :], rhs=Sb1[:, h, :], start=False, stop=True),
                     reads=["q1", "Sb1"], writes=["W0a" if h < 2 else "W0b"])
        kv_update(1)
        if not full:
            return
        S.op("act", lambda e: e.activation(out=Sb0[:], in_=S_f[:], func=AF.Copy), reads=["S_f"], writes=["Sb0"])
        t1v = t1_[:].rearrange("p (h v) -> p h v", h=4)
        t2v = t2_[:].rearrange("p (h v) -> p h v", h=4)
        for h in range(4):
            S.op("act", lambda e, h=h: e.activation(out=t1v[:, h, :], in_=W0v[:, h, :], func=AF.Square, accum_out=ss[:, 2 + h:3 + h]),
                 reads=["W0a" if h < 2 else "W0b"], writes=["F1", "ss"])
        S.op("dve", lambda e: e.tensor_scalar(out=ss[:, 2:6], in0=ss[:, 2:6], scalar1=1.0 / 256.0, scalar2=EPS, op0=ALU.mult, op1=ALU.add),
             reads=["ss"], writes=["ss"])
        S.op("pool", lambda e: e.tensor_tensor(out=rstd[:, 2:6], in0=ss[:, 2:6], in1=neghalf[:, 0:4], op=ALU.pow),
             reads=["ss", "neghalf"], writes=["rstd"])
        proj_tm(W1[:, 0:512], ["W1"], wr, "wr", 0, 512)
        proj_tm(W1[:, 512:1024], ["W1"], wr, "wr", 512, 512)
        S.op("act", lambda e: e.activation(out=t2_[:], in_=W1, func=AF.Sigmoid), reads=["W1"], writes=["F0"])
        S.op("dve", lambda e: e.tensor_tensor(out=t2_[:], in0=W1, in1=t2_[:], op=ALU.mult), reads=["W1", "F0"], writes=["F0"])
        S.op("pool", lambda e: e.tensor_tensor(out=t2_[:], in0=t2_[:], in1=gnwB[:], op=ALU.mult), reads=["F0", "gnwB"], writes=["F0"])
        for h in range(4):
            S.op("dve", lambda e, h=h: e.scalar_tensor_tensor(out=t1v[:, h, :], in0=W0v[:, h, :], scalar=rstd[:, 2 + h:3 + h],
                                                              in1=t2v[:, h, :], op0=ALU.mult, op1=ALU.mult),
                 reads=["W0a" if h < 2 else "W0b", "rstd", "F0"], writes=["F1"])
        proj_tm(W1[:, 0:512], ["W1"], wga, "wga", 0, 512)
        proj_tm(W1[:, 512:1024], ["W1"], wga, "wga", 512, 512)
        S.op("act", lambda e: e.activation(out=t2_[:], in_=W1, func=AF.Sigmoid), reads=["W1"], writes=["F0"])
        S.op("pool", lambda e: e.tensor_tensor(out=t1_[:], in0=t1_[:], in1=t2_[:], op=ALU.mult), reads=["F1", "F0"], writes=["F1"])
        capB = []
        S.cap = capB
        proj_tm(W2a, ["W2a"], wsu, "wsu", 0, 512)
        proj_tm(W2b, ["W2b"], wsu, "wsu", 512, 512)
        S.op("act", lambda e: e.activation(out=u_[:], in_=W2, func=AF.Gelu), reads=kW2, writes=["F2"])
        proj_tm(W2a, ["W2a"], wsv, "wsv", 0, 512)
        proj_tm(W2b, ["W2b"], wsv, "wsv", 512, 512)
        S.op("act", lambda e: e.activation(out=gv_[:], in_=W2, func=AF.Gelu), reads=kW2, writes=["F3"])
        for hf in range(2):
            S.op("dve", lambda e, hf=hf: e.bn_stats(out=bnst[:, hf, :], in_=gv_[:, hf * 512:(hf + 1) * 512]), reads=["F3"], writes=["bnst"])
        S.op("dve", lambda e: e.bn_aggr(out=mv[:], in_=bnst[:].rearrange("p a s -> p (a s)")), reads=["bnst"], writes=["mv"])
        S.op("dve", lambda e: e.tensor_scalar(out=ss[:, 6:7], in0=mv[:, 1:2], scalar1=EPS, scalar2=None, op0=ALU.add),
             reads=["mv"], writes=["ss_s"])
        S.op("pool", lambda e: e.tensor_tensor(out=rstd[:, 6:7], in0=ss[:, 6:7], in1=neghalf[:, 0:1], op=ALU.pow),
             reads=["ss_s", "neghalf"], writes=["rstd_s"])
        S.op("dve", lambda e: e.tensor_scalar(out=gv_[:], in0=gv_[:], scalar1=mv[:, 0:1], scalar2=rstd[:, 6:7], op0=ALU.subtract, op1=ALU.mult),
             reads=["F3", "mv", "rstd_s"], writes=["F3"])
        S.op("pool", lambda e: e.tensor_tensor(out=gv_[:], in0=gv_[:], in1=lnwB[:], op=ALU.mult), reads=["F3", "lnwB"], writes=["F3"])
        S.op("pool", lambda e: e.tensor_tensor(out=vln[:], in0=gv_[:], in1=lnbB[:], op=ALU.add), reads=["F3", "lnbB"], writes=["vln"])
        proj_tm(W2a, ["W2a"], wgb, "wgb", 0, 512)
        proj_tm(W2b, ["W2b"], wgb, "wgb", 512, 512)
        S.op("act", lambda e: e.activation(out=gv_[:], in_=W2, func=AF.Sigmoid), reads=kW2, writes=["F3"])
        for g in range(8):
            S.op("pe", lambda e, g=g: e.matmul(W2[:, g * 128:(g + 1) * 128], lhsT=WsT[:, g, :], rhs=vln[:, g * 128:(g + 1) * 128], start=True, stop=True),
                 reads=["WsT", "vln"], writes=kW2)
        for g in range(8):
            S.op("dve", lambda e, g=g: e.scalar_tensor_tensor(out=u_[:, g * 128:(g + 1) * 128], in0=W2[:, g * 128:(g + 1) * 128],
                                                              scalar=Bsgu[:, g:g + 1], in1=u_[:, g * 128:(g + 1) * 128], op0=ALU.add, op1=ALU.mult),
                 reads=kW2 + ["Bsgu", "F2"], writes=["F2"])
        S.op("pool", lambda e: e.tensor_tensor(out=u_[:], in0=u_[:], in1=gv_[:], op=ALU.mult), reads=["F2", "F3"], writes=["F2"])
        S.cap = None
        ia = ib = 0
        na, nb = len(capA), len(capB)
        while ia < na or ib < nb:
            if ib >= nb or (ia < na and ia * nb <= ib * na):
                S.op(*capA[ia]); ia += 1
            else:
                S.op(*capB[ib]); ib += 1
        S.op("dve", lambda e: e.tensor_tensor(out=vln[:], in0=t1_[:], in1=u_[:], op=ALU.add), reads=["F1", "F2"], writes=["vln"])

    def p2_tail(k):
        xk = xts[k % 2]
        kx = "xt%d" % (k % 2)
        row0 = k * 128
        PAb = PA.bitcast(BF16)
        for ch in range(8):
            S.op("pe", lambda e, ch=ch: e.transpose(PAb[:, ch * 128:(ch + 1) * 128], vln[:, ch * 128:(ch + 1) * 128], ident_b[:]),
                 reads=["vln", "ident_b"], writes=["PA"])
        S.op("act", lambda e: e.activation(out=mTb[:].rearrange("p c t -> p (c t)"), in_=PAb[:, 0:1024], func=AF.Copy), reads=["PA"], writes=["mT"])
        for hf in range(2):
            for ch in range(8):
                S.op("pe", lambda e, hf=hf, ch=ch: e.matmul(W1[:, hf * 512:(hf + 1) * 512], lhsT=mTb[:, ch, :], rhs=woutg[:, ch, hf * 512:(hf + 1) * 512],
                                                            start=(ch == 0), stop=(ch == 7)),
                     reads=["mT", "woutg"], writes=["W1"])
        S.op("dve", lambda e: e.tensor_tensor(out=xk[:], in0=W1, in1=xk[:], op=ALU.add), reads=["W1", kx], writes=[kx])
        S.op("sp", lambda e: e.dma_start(out=out_d[row0:row0 + 128, :], in_=xk[:]), reads=[kx], writes=[("out_d", row0 // 128)], dma_chan="c_out%d" % (k % 2))

    load_late()
    alrT1 = alloc("alrT1", [16, 128], F32)
    dec1 = alloc("dec1", [128, 4, 2], F32)
    ss1 = alloc("ss1", [128, 8], F32)
    rstd1 = alloc("rstd1", [128, 8], F32)
    F3h = Fs[3]
    p1buf = [
        dict(xt=xt[:], xn=Fs[0][:], hT=hT, v=v_bf[:], alrT=alrT, bufE=bufE[:], lbuf=lbuf[:], ktail=ktail[:], dec=dec, ss=ss, rstd=rstd,
             Ww=P01[:, :], Wx=P23[:, :], Bc=P23[:, 0:512], Bd=P23[:, 512:1024]),
        dict(xt=Fs[1][:], xn=Fs[2][:], hT=vln[:].rearrange("p (c t) -> p c t", c=8), v=F3h[:, 0:512].bitcast(BF16), alrT=alrT1, bufE=F3h[:, 512:1024],
             lbuf=expnG[:], ktail=qd[:].rearrange("p h t -> p (h t)"), dec=dec1, ss=ss1, rstd=rstd1,
             Ww=P45[:, :], Wx=P67[:, :], Bc=P67[:, 0:512], Bd=P67[:, 512:1024]),
    ]

    def p1_stages(x_ap, seg, sl):
        B = p1buf[sl]
        K = lambda n: "%s_%d" % (n, sl)
        fcol = flags[:, seg:seg + 1]
        xt_, xn_, hT_, v_, alrT_, bufE_, lbuf_, ktail_, dec_, ss_, rstd_ = (B[k] for k in ("xt", "xn", "hT", "v", "alrT", "bufE", "lbuf", "ktail", "dec", "ss", "rstd"))
        Ww, Wx, Bc, Bd = B["Ww"], B["Wx"], B["Bc"], B["Bd"]
        st = []

        def s0():
            S.op("sp", lambda e: e.dma_start(out=xt_, in_=x_ap), writes=[K("xt")], dma_chan="c_p1xt%d" % sl)
            S.op("act", lambda e: e.activation(out=xn_, in_=xt_, func=AF.Square, accum_out=ss_[:, 0:1]), reads=[K("xt")], writes=[K("xn"), K("ss")])
            S.op("pool", lambda e: e.tensor_scalar(out=ss_[:, 1:2], in0=ss_[:, 0:1], scalar1=1.0 / D, scalar2=EPS, op0=ALU.mult, op1=ALU.add),
                 reads=[K("ss")], writes=[K("ssb")])
            S.op("pool", lambda e: e.tensor_tensor(out=rstd_[:, 0:1], in0=ss_[:, 1:2], in1=neghalf[:, 0:1], op=ALU.pow), reads=[K("ssb"), "neghalf"], writes=[K("rstd")])
        st.append(s0)

        def s1():
            S.op("act", lambda e: e.activation(out=xn_, in_=xt_, func=AF.Copy, scale=rstd_[:, 0:1]), reads=[K("xt"), K("rstd")], writes=[K("xn")])
            for ch in range(8):
                S.op("pe", lambda e, ch=ch: e.transpose(Ww[:, ch * 128:(ch + 1) * 128], xn_[:, ch * 128:(ch + 1) * 128], ident_f[:]),
                     reads=[K("xn"), "ident_f"], writes=[K("Ww")])
        st.append(s1)

        def s2():
            for ch in range(8):
                eng = "dve" if ch < 4 else "act"
                if eng == "dve":
                    S.op("dve", lambda e, ch=ch: e.tensor_scalar(out=hT_[:, ch, :], in0=Ww[:, ch * 128:(ch + 1) * 128],
                                                                 scalar1=cols[:, 0, ch:ch + 1], scalar2=cols[:, 1, ch:ch + 1], op0=ALU.mult, op1=ALU.add),
                         reads=[K("Ww"), "cols"], writes=[(K("hT"), ch)])
                else:
                    S.op("act", lambda e, ch=ch: e.activation(out=hT_[:, ch, :], in_=Ww[:, ch * 128:(ch + 1) * 128], func=AF.Identity,
                                                              scale=cols[:, 0, ch:ch + 1], bias=cols[:, 1, ch:ch + 1]),
                         reads=[K("Ww"), "cols"], writes=[(K("hT"), ch)])
        st.append(s2)

        def s3():
            for ch in range(8):
                S.op("pe", lambda e, ch=ch: e.matmul(Bc, lhsT=hT_[:, ch, :], rhs=wk[:, ch, :], start=(ch == 0), stop=(ch == 7)), reads=[(K("hT"), ch), "wk"], writes=[K("Bc")])
            for ch in range(8):
                S.op("pe", lambda e, ch=ch: e.matmul(Bd[0:16, 0:128], lhsT=walr[:, ch, :], rhs=hT_[:, ch, :], start=(ch == 0), stop=(ch == 7)),
                     reads=[(K("hT"), ch), "walr"], writes=[K("Bd")])
            for hf in range(2):
                for ch in range(8):
                    S.op("pe", lambda e, ch=ch, hf=hf: e.matmul(Ww[:, hf * 512:(hf + 1) * 512], lhsT=hT_[:, ch, :], rhs=wv[:, ch, hf * 512:(hf + 1) * 512],
                                                               start=(ch == 0), stop=(ch == 7)),
                         reads=[(K("hT"), ch), "wv"], writes=[K("Ww")])
            S.op("act", lambda e: e.activation(out=alrT_[:], in_=Bd[0:16, 0:128], func=AF.Copy), reads=[K("Bd")], writes=[K("alrT")])
        st.append(s3)

        def s4():
            S.op("pe", lambda e: e.matmul(Bd, lhsT=alrT_[:], rhs=wup_f[:], start=True, stop=False), reads=[K("alrT"), "wup_f"], writes=[K("Bd")])
            S.op("pe", lambda e: e.matmul(Bd, lhsT=ones_f[0:1, :], rhs=balpha_f[:], start=False, stop=True), reads=["ones_f", "balpha_f"], writes=[K("Bd")])
            S.op("dve", lambda e: e.tensor_scalar(out=v_, in0=Ww, scalar1=fcol, scalar2=None, op0=ALU.mult), reads=[K("Ww"), "flags"], writes=[K("v")])
            S.op("act", lambda e: e.activation(out=bufE_, in_=Bd, func=AF.Exp, scale=-1.0), reads=[K("Bd")], writes=[K("bufE")])
        st.append(s4)

        def s5():
            S.op("act", lambda e: e.activation(out=lbuf_, in_=bufE_, func=AF.Ln, bias=1.0, scale=1.0), reads=[K("bufE")], writes=[K("lbuf")])
            S.op("pe", lambda e: e.matmul(Bd, lhsT=Rm[:], rhs=lbuf_, start=True, stop=True), reads=["Rm", K("lbuf")], writes=[K("Bd")])
            for h in range(4):
                S.op("pe", lambda e, h=h: e.matmul(Ww[:, h * 128:(h + 1) * 128], lhsT=lbuf_[:, h * 128:(h + 1) * 128], rhs=Lm[:], start=True, stop=True),
                     reads=[K("lbuf"), "Lm", K("v")], writes=[K("Ww")])
        st.append(s5)

        def s6():
            S.op("act", lambda e: e.activation(out=bufE_, in_=Bd, func=AF.Exp), reads=[K("Bd")], writes=[K("bufE")])
            Wv4 = Ww[:, 0:512].rearrange("p (h t) -> p h t", h=4)
            S.op("act", lambda e: e.activation(out=dec_[:], in_=Wv4[:, :, 63:128:64], func=AF.Exp), reads=[K("Ww")], writes=[K("dec")])
            S.op("dve", lambda e: e.tensor_tensor(out=ktail_, in0=Bc, in1=bufE_, op=ALU.mult), reads=[K("Bc"), K("bufE")], writes=[K("ktail")])
        st.append(s6)

        def kv(c, Wdst, wkey):
            Wd = Wdst.rearrange("p (h v) -> p h v", h=4)
            for h in range(4):
                S.op("pe", lambda e, h=h: e.matmul(Wd[:, h, :], lhsT=ktail_[c * 64:(c + 1) * 64, h * 128:(h + 1) * 128],
                                                   rhs=v_[c * 64:(c + 1) * 64, h * 256:(h + 1) * 256], start=True, stop=True),
                     reads=[K("ktail"), K("v")] + ([K("dec")] if wkey == "Ww" else []), writes=[K(wkey)] if wkey != "Wx" else [K("Bc"), K("Bd")])
            for h in range(4):
                S.op("dve", lambda e, h=h: e.scalar_tensor_tensor(out=S_f[:, h, :], in0=S_f[:, h, :], scalar=dec_[:, h, c:c + 1],
                                                                  in1=Wd[:, h, :], op0=ALU.mult, op1=ALU.add),
                     reads=[("S_f", h), K("dec")] + ([K(wkey)] if wkey != "Wx" else [K("Bc"), K("Bd")]), writes=[("S_f", h)])
        st.append(lambda: kv(0, Wx, "Wx"))
        st.append(lambda: kv(1, Ww, "Ww"))
        return st

    tiles = []
    for seg in range(3):
        for ti in range(NT if stage not in (1, 3, 4) else NDBG):
            tiles.append((xpre[seg, ti * 128:(ti + 1) * 128, :], seg))
    SKW = 4
    stg = [p1_stages(x_ap, seg, k % 2) for k, (x_ap, seg) in enumerate(tiles)]
    nst = len(stg[0])
    for step in range(len(tiles) * SKW + nst):
        for k in range(len(tiles)):
            sidx = step - k * SKW
            if 0 <= sidx < nst:
                stg[k][sidx]()
        if step % 2 == 0:
            issue_cvt(1)
    S.barrier()
    S.op("act", lambda e: e.activation(out=Sb0[:], in_=S_f[:], func=AF.Copy), writes=["Sb0"])
    issue_cvt(1000)
    NT2 = NT if stage not in (1, 3, 4) else NDBG
    p2_front(0)
    for ti in range(NT2):
        p2_body(ti)
        if ti + 1 < NT2:
            p2_front(ti + 1)
        p2_tail(ti)

    if stage <= 2:
        S.emit(final_waits=["c_out0", "c_out1"])
        return nc

    S.barrier()
    M.off = mark_p3
    h2T = alloc("h2T", [128, 8, 2048], BF16)
    idx1T = alloc("idx1T", [128, 2048], BF16)
    idx2T = alloc("idx2T", [128, 2048], BF16)
    gT = alloc("gT", [128, 2048], BF16)
    gate2B = alloc("gate2B", [128, D], F32)
    finwB = alloc("finwB", [128, D], F32)
    mark_p3t = M.off
    wqb = alloc("wqb", [128, 8, 2048], BF16)
    k1T = alloc("k1T", [128, 128], BF16)
    k2T = alloc("k2T", [128, 128], BF16)
    xt2 = alloc("xt2", [128, D], F32)
    xn2 = alloc("xn2", [128, D], F32)
    qT = alloc("qT", [128, 16, 128], BF16)
    sc = alloc("sc", [128, 16, 128], F32)
    work = alloc("work", [128, 256], F32)
    vtop = alloc("vtop", [128, 16, 16], F32)
    iu = alloc("iu", [128, 16, 16], U32)
    itf = alloc("itf", [128, 16, 16], F32)
    cand = alloc("cand", [128, 8, 256], F32)
    ts = alloc("ts", [128, 8, 16], F32)
    posu = alloc("posu", [128, 8, 16], U32)
    k1u = alloc("k1u", [128, 8, 16], U32)
    k2u = alloc("k2u", [128, 8, 16], U32)
    k1f = alloc("k1f", [128, 8, 16], F32)
    k2f = alloc("k2f", [128, 8, 16], F32)
    ee = alloc("ee", [128, 8, 16], F32)
    zz = alloc("zz", [128, 8], F32)
    oh = alloc("oh", [128, 128, 16], F32)
    idx_tm = alloc("idx_tm", [128, 3, 128], F32)
    iota16 = alloc("iota16", [128, 16], F32)
    diag = alloc("diag", [128, 128], F32)
    ss2 = alloc("ss2", [128, 4], F32)
    rstd2 = alloc("rstd2", [128, 4], F32)

    S.op("act", lambda e: e.dma_start(out=finwB[:], in_=rowv_d[0:1, :].to_broadcast([128, D])), writes=["finwB"], dma_chan="c_finw")
    wq_v = wq_d.rearrange("(c p) e -> p c e", p=128)
    for ch in range(8):
        S.op("pool", lambda e, ch=ch: e.dma_start(out=wqb[:, ch, :], in_=wq_v[:, ch, :]), writes=["wqb"], dma_chan="c_wqb")
    S.op("pool", lambda e: e.dma_start(out=k1T[:], in_=k1T_d[:, :]), writes=["k1T"], dma_chan="c_k1T")
    S.op("pool", lambda e: e.dma_start(out=k2T[:], in_=k2T_d[:, :]), writes=["k2T"], dma_chan="c_k2T")
    S.op("dve", lambda e: e.tensor_copy(out=iota16[:], in_=iota_f[:, 0:16]), reads=["iota_f"], writes=["iota16"])
    for ch in range(8):
        S.op("dve", lambda e, ch=ch: e.tensor_scalar(out=diag[:], in0=ident_f[:], scalar1=cols[:, 4, ch:ch + 1], scalar2=None, op0=ALU.mult),
             reads=["ident_f", "cols"], writes=["diag"])
        S.op("pe", lambda e: e.matmul(PA[:, 0:128], lhsT=ones_f[:], rhs=diag[:], start=True, stop=True), reads=["ones_f", "diag"], writes=["PA"])
        S.op("act", lambda e, ch=ch: e.activation(out=gate2B[:, ch * 128:(ch + 1) * 128], in_=PA[:, 0:128], func=AF.Copy), reads=["PA"], writes=["gate2B"])

    W01 = [W0, W1]
    xt2b = [xt2, alloc("xt2b", [128, D], F32)]
    xn2b = [xn2, alloc("xn2b", [128, D], F32)]
    qTb = [qT, alloc("qTb", [128, 16, 128], BF16)]
    scb = [sc, alloc("scb", [128, 16, 128], F32)]
    work16 = alloc("work16", [128, 16, 128], F32)
    oh2 = alloc("oh2", [128, 128, 16], F32)
    ohs = [oh, oh2]
    Ireps = [alloc("Irep%d" % i, [128, 16, 128], F32) for i in range(2)]
    NT25 = NT if stage != 3 else NDBG

    def front(ti):
        p = ti % 2
        xt2_, xn2_, qT_, sc_ = xt2b[p], xn2b[p], qTb[p], scb[p]
        kx, kn, kq, ks = "xt2_%d" % p, "xn2_%d" % p, "qT_%d" % p, "sc_%d" % p
        S.op("sp", lambda e: e.dma_start(out=xt2_[:], in_=out_d[ti * 128:(ti + 1) * 128, :]), reads=[("out_d", ti)], writes=[kx], dma_chan="c_xt2_%d" % p)
        S.op("act", lambda e: e.activation(out=xn2_[:], in_=xt2_[:], func=AF.Square, accum_out=ss2[:, p:p + 1]), reads=[kx], writes=[kn, ("ss2", p)])
        S.op("pool", lambda e: e.tensor_scalar(out=ss2[:, 2 + p:3 + p], in0=ss2[:, p:p + 1], scalar1=1.0 / D, scalar2=EPS, op0=ALU.mult, op1=ALU.add),
             reads=[("ss2", p)], writes=[("ss2b", p)])
        S.op("pool", lambda e: e.tensor_tensor(out=rstd2[:, p:p + 1], in0=ss2[:, 2 + p:3 + p], in1=neghalf[:, 0:1], op=ALU.pow), reads=[("ss2b", p), "neghalf"], writes=[("rstd2", p)])
        S.op("act", lambda e: e.activation(out=xn2_[:], in_=xt2_[:], func=AF.Copy, scale=rstd2[:, p:p + 1]), reads=[kx, ("rstd2", p)], writes=[kn])
        for ch in range(8):
            S.op("pe", lambda e, ch=ch: e.transpose(W2[:, ch * 128:(ch + 1) * 128], xn2_[:, ch * 128:(ch + 1) * 128], ident_f[:]),
                 reads=[kn, "ident_f"], writes=kW2)
        for ch in range(8):
            S.op("act", lambda e, ch=ch: e.activation(out=h2T[:, ch, ti * 128:(ti + 1) * 128], in_=W2[:, ch * 128:(ch + 1) * 128], func=AF.Identity,
                                                      scale=cols[:, 2, ch:ch + 1], bias=cols[:, 3, ch:ch + 1]),
                 reads=[kW2[ch // 4], "cols"], writes=[("h2T", ti, ch)])
        for blk in range(16):
            dst = W01[blk // 8][:, (blk % 8) * 128:(blk % 8 + 1) * 128]
            for ch in range(8):
                S.op("pe", lambda e, blk=blk, ch=ch, dst=dst: e.matmul(dst, lhsT=wqb[:, ch, blk * 128:(blk + 1) * 128], rhs=h2T[:, ch, ti * 128:(ti + 1) * 128],
                                                                      start=(ch == 0), stop=(ch == 7)),
                     reads=["wqb", ("h2T", ti, ch)], writes=["W%d" % (blk // 8)])
        qTf = qT_[:].rearrange("p b t -> p (b t)")
        S.op("act", lambda e: e.activation(out=qTf[:, 0:1024], in_=W0, func=AF.Copy), reads=["W0"], writes=[kq])
        S.op("act", lambda e: e.activation(out=qTf[:, 1024:2048], in_=W1, func=AF.Copy), reads=["W1"], writes=[kq])
        for blk in range(16):
            dst = W01[blk // 8][:, (blk % 8) * 128:(blk % 8 + 1) * 128]
            kT_ = k1T if blk % 2 == 0 else k2T
            S.op("pe", lambda e, blk=blk, dst=dst, kT_=kT_: e.matmul(dst, lhsT=qT_[:, blk, :], rhs=kT_[:], start=True, stop=True),
                 reads=[kq, "k1T", "k2T"], writes=["W%d" % (blk // 8)])
        scf = sc_[:].rearrange("p b k -> p (b k)")
        S.op("act", lambda e: e.activation(out=scf[:, 0:1024], in_=W0, func=AF.Copy), reads=["W0"], writes=[ks])
        S.op("act", lambda e: e.activation(out=scf[:, 1024:2048], in_=W1, func=AF.Copy), reads=["W1"], writes=[ks])

    def top16_multi(items):
        for (src, sk, n, vd, idd, wk_, tg) in items:
            S.op("dve", lambda e, src=src, vd=vd: e.max(out=vd[:, 0:8], in_=src), reads=[sk], writes=[("vt", tg)])
        for (src, sk, n, vd, idd, wk_, tg) in items:
            S.op("dve", lambda e, src=src, vd=vd, idd=idd: e.max_index(out=idd[:, 0:8], in_max=vd[:, 0:8], in_values=src), reads=[sk, ("vt", tg)], writes=[("it", tg)])
        for (src, sk, n, vd, idd, wk_, tg) in items:
            S.op("dve", lambda e, src=src, vd=vd, wk_=wk_: e.match_replace(out=wk_, in_to_replace=vd[:, 0:8], in_values=src, imm_value=-1e30),
                 reads=[sk, ("vt", tg)], writes=[("work", tg)])
        for (src, sk, n, vd, idd, wk_, tg) in items:
            S.op("dve", lambda e, vd=vd, wk_=wk_: e.max(out=vd[:, 8:16], in_=wk_), reads=[("work", tg)], writes=[("vt", tg)])
        for (src, sk, n, vd, idd, wk_, tg) in items:
            S.op("dve", lambda e, vd=vd, idd=idd, wk_=wk_: e.max_index(out=idd[:, 8:16], in_max=vd[:, 8:16], in_values=wk_), reads=[("work", tg), ("vt", tg)], writes=[("it", tg)])

    def back(ti, part):
        p = ti % 2
        sc_ = scb[p]
        ks = "sc_%d" % p
        vkeys = [("vt", t) for t in range(16)]
        ikeys_ = [("it", t) for t in range(16)]
        if part == 0:
            top16_multi([(sc_[:, blk, :], ks, 128, vtop[:, blk, :], iu[:, blk, :], work16[:, blk, :], blk) for blk in range(16)])
            S.op("dve", lambda e: e.tensor_copy(out=itf[:], in_=iu[:]), reads=ikeys_, writes=["itf"])
            for which in (0, 1):
                Irep = Ireps[which]
                for j in range(16):
                    S.op("act", lambda e, j=j, which=which, Irep=Irep: e.activation(out=Irep[:, j, :].rearrange("p (h k) -> p h k", h=8),
                                                                                    in_=itf[:, which::2, j:j + 1].to_broadcast([128, 8, 16]), func=AF.Copy),
                         reads=["itf"], writes=[("Irep", which, j)])
            return
        for h in range(8):
            cv = cand[:, h, :].rearrange("p (a b) -> p a b", a=16)
            S.op("dve", lambda e, h=h, cv=cv: e.tensor_tensor(out=cv, in0=vtop[:, 2 * h, :].unsqueeze(2).to_broadcast([128, 16, 16]),
                                                              in1=vtop[:, 2 * h + 1, :].unsqueeze(1).to_broadcast([128, 16, 16]), op=ALU.add),
                 reads=[("vt", 2 * h), ("vt", 2 * h + 1)], writes=[("cand", h)])
        w8 = work16[:].rearrange("p (h a) k -> p h (a k)", h=8)
        top16_multi([(cand[:, h, :], ("cand", h), 256, ts[:, h, :], posu[:, h, :], w8[:, h, :], 100 + h) for h in range(8)])
        tkeys = [("vt", 100 + h) for h in range(8)]
        pkeys = [("it", 100 + h) for h in range(8)]
        S.op("dve", lambda e: e.tensor_tensor(out=ee[:], in0=ts[:], in1=ts[:, :, 0:1].to_broadcast([128, 8, 16]), op=ALU.subtract),
             reads=tkeys, writes=["ee"])
        S.op("act", lambda e: e.activation(out=ee[:], in_=ee[:], func=AF.Exp), reads=["ee"], writes=["ee"])
        S.op("dve", lambda e: e.tensor_single_scalar(out=k1u[:], in_=posu[:], scalar=4, op=ALU.logical_shift_right), reads=pkeys, writes=["k1u"])
        S.op("dve", lambda e: e.tensor_single_scalar(out=k2u[:], in_=posu[:], scalar=15, op=ALU.bitwise_and), reads=pkeys, writes=["k2u"])
        S.op("dve", lambda e: e.tensor_copy(out=k1f[:], in_=k1u[:]), reads=["k1u"], writes=["k1f"])
        S.op("dve", lambda e: e.tensor_copy(out=k2f[:], in_=k2u[:]), reads=["k2u"], writes=["k2f"])
        for which, kf, kfk in ((0, k1f, "k1f"), (1, k2f, "k2f")):
            Irep = Ireps[which]
            pr = ohs[which][:].rearrange("p a b -> p (a b)").rearrange("p (j m) -> p j m", j=16)
            kff = kf[:].rearrange("p h k -> p (h k)")
            for j in range(16):
                S.op("dve", lambda e, j=j, pr=pr, kff=kff, Irep=Irep: e.scalar_tensor_tensor(out=pr[:, j, :], in0=kff, scalar=float(j), in1=Irep[:, j, :],
                                                                                             op0=ALU.is_equal, op1=ALU.mult),
                     reads=[kfk, ("Irep", which, j)], writes=[("pr", which, j)])
            S.op("dve", lambda e, pr=pr: e.tensor_tensor(out=pr[:, 0:8, :], in0=pr[:, 0:8, :], in1=pr[:, 8:16, :], op=ALU.add),
                 reads=[("pr", which, j) for j in range(16)], writes=[("prs", which)])
            S.op("dve", lambda e, pr=pr: e.tensor_tensor(out=pr[:, 0:4, :], in0=pr[:, 0:4, :], in1=pr[:, 4:8, :], op=ALU.add),
                 reads=[("prs", which)], writes=[("prs", which)])
            S.op("dve", lambda e, pr=pr: e.tensor_tensor(out=pr[:, 0:2, :], in0=pr[:, 0:2, :], in1=pr[:, 2:4, :], op=ALU.add),
                 reads=[("prs", which)], writes=[("prs", which)])
            S.op("dve", lambda e, pr=pr, which=which: e.tensor_tensor(out=idx_tm[:, which, :], in0=pr[:, 0, :], in1=pr[:, 1, :], op=ALU.add),
                 reads=[("prs", which)], writes=[("idx_tm", which)])
        S.op("dve", lambda e: e.tensor_reduce(out=zz[:], in_=ee[:], axis=AX.X, op=ALU.add), reads=["ee"], writes=["zz"])
        S.op("dve", lambda e: e.reciprocal(out=zz[:], in_=zz[:]), reads=["zz"], writes=["zz"])
        S.op("dve", lambda e: e.tensor_tensor(out=idx_tm[:, 2, :].rearrange("p (h k) -> p h k", h=8), in0=ee[:],
                                              in1=zz[:].unsqueeze(2).to_broadcast([128, 8, 16]), op=ALU.mult),
             reads=["ee", "zz"], writes=[("idx_tm", 2)])
        for a, dstT in ((0, idx1T), (1, idx2T), (2, gT)):
            S.op("pe", lambda e, a=a: e.transpose(PA[:, a * 128:(a + 1) * 128], idx_tm[:, a, :], ident_f[:]), reads=[("idx_tm", a), "ident_f"], writes=["PA"])
        for a, dstT in ((0, idx1T), (1, idx2T), (2, gT)):
            S.op("act", lambda e, a=a, dstT=dstT: e.activation(out=dstT[:, ti * 128:(ti + 1) * 128], in_=PA[:, a * 128:(a + 1) * 128], func=AF.Copy),
                 reads=["PA"], writes=[("idxT", ti)])

    front(0)
    for ti in range(NT25):
        back(ti, 0)
        if ti + 1 < NT25:
            front(ti + 1)
        back(ti, 1)

    if stage == 3:
        dbg = dram("dbg", [128, 3, 128 * NDBG], kind="ExternalOutput")
        for a, dstT in ((0, idx1T), (1, idx2T), (2, gT)):
            S.op("sp", lambda e, a=a, dstT=dstT: e.dma_start(out=dbg[:, a, :], in_=dstT[:, 0:128 * NDBG]), reads=[("idxT", t) for t in range(NDBG)], writes=["dbg"], dma_chan="c_dbg")
        S.emit(final_waits=["c_out0", "c_out1", "c_dbg"])
        return nc

    S.barrier()
    M.off = mark_p3t
    TT = 256
    NB = 4
    G = alloc("G", [128, 128, TT], BF16)
    NBUF = 3
    dbuf = [alloc("dbuf%d" % i, [128, NB, 8, 128], BF16) for i in range(NBUF)]
    ubuf = [alloc("ubuf%d" % i, [128, NB, D], BF16) for i in range(NBUF)]
    SBT = 8
    p2oh = [alloc("p2oh%d" % i, [128, SBT, 128], BF16) for i in range(2)]
    p1t = [alloc("p1t%d" % i, [128, SBT, 128], BF16) for i in range(2)]
    p1w = [alloc("p1w%d" % i, [128, SBT, 128], BF16) for i in range(2)]
    Ab = [alloc("Ab%d" % i, [128, TT], BF16) for i in range(4)]
    Wb = [alloc("Wb%d" % i, [128, TT], BF16) for i in range(4)]
    x3 = [alloc("x3_%d" % i, [128, D], F32) for i in range(2)]
    y3 = [alloc("y3_%d" % i, [128, D], F32) for i in range(2)]
    ss3 = alloc("ss3", [128, 4], F32)
    rstd3 = alloc("rstd3", [128, 4], F32)
    print("SBUF used (P3):", M.off)
    PAB = [PA, PB]
    nwd = 0
    for T in range(2048 // TT if stage != 4 else 1):
        t0 = T * TT
        ikeys = [("idxT", t) for t in range(2 * T, 2 * T + 2)]
        W2h = [W2a, W2b]
        for sbi in range(TT // SBT):
            ts0 = t0 + sbi * SBT
            q = sbi % 2
            p2o, p1t_, p1w_ = p2oh[q], p1t[q], p1w[q]
            for tl in range(SBT):
                tk = ts0 + tl
                S.op("dve", lambda e, tk=tk, tl=tl, p2o=p2o: e.tensor_scalar(out=p2o[:, tl, :], in0=iota_b[:], scalar1=idx2T[:, tk:tk + 1], scalar2=None, op0=ALU.is_equal),
                     reads=ikeys + ["iota_b"], writes=[("p2oh", q, tl)])
                S.op("dve", lambda e, tk=tk, tl=tl, p1w_=p1w_: e.tensor_scalar(out=p1w_[:, tl, :], in0=iota_b[:], scalar1=idx1T[:, tk:tk + 1], scalar2=gT[:, tk:tk + 1],
                                                                             op0=ALU.is_equal, op1=ALU.mult),
                     reads=ikeys + ["iota_b"], writes=[("p1w", q, tl)])
            for grp in range(SBT // 4):
                hb = (sbi * (SBT // 4) + grp) % 2
                for tl in range(4):
                    tloc = grp * 4 + tl
                    S.op("pe", lambda e, tl=tl, tloc=tloc, hb=hb, p1w_=p1w_, p2o=p2o: e.matmul(W2h[hb][:, tl * 128:(tl + 1) * 128], lhsT=p1w_[:, tloc, :], rhs=p2o[:, tloc, :], start=True, stop=True),
                         reads=[("p1w", q, tloc), ("p2oh", q, tloc)], writes=[kW2[hb]])
                tg = sbi * SBT + grp * 4
                S.op("act", lambda e, tg=tg, hb=hb: e.activation(out=G[:, :, tg:tg + 4],
                                                                 in_=W2h[hb].rearrange("p (t i) -> p i t", t=4), func=AF.Copy),
                     reads=[kW2[hb]], writes=["G"])
        SK = 2
        NSL = 4
        Aps = [PA[:, 0:TT], PB[:, 0:TT]]
        binfo = {}
        for step in range(128 + SK):
            if step < 128:
                i2 = step
                j = i2 % NB
                if j == 0:
                    b = nwd % NBUF
                    nwd += 1
                    db_, ub_ = dbuf[b], ubuf[b]
                    S.op("sp", lambda e, i2=i2, db_=db_: e.dma_start(out=db_[:], in_=scr_down[:, i2:i2 + NB, :, :]), writes=["dbuf%d" % b], dma_chan="c_dbuf%d" % b)
                    S.op("sp", lambda e, i2=i2, ub_=ub_: e.dma_start(out=ub_[:], in_=scr_up[:, i2:i2 + NB, :]), writes=["ubuf%d" % b], dma_chan="c_ubuf%d" % b)
                binfo[i2] = (b, ub_, j)
                pp = i2 % NSL
                pq = i2 % 2
                pa = Aps[pq]
                for ch in range(8):
                    S.op("pe", lambda e, ch=ch, j=j, db_=db_, pa=pa, t0=t0: e.matmul(pa, lhsT=db_[:, j, ch, :], rhs=h2T[:, ch, t0:t0 + TT], start=(ch == 0), stop=(ch == 7)),
                         reads=["dbuf%d" % b, ("h2T", 2 * T), ("h2T", 2 * T + 1)], writes=["PAB%d" % pq])
                ab, wb_ = Ab[pp], Wb[pp]
                S.op("act", lambda e, ab=ab, pa=pa: e.activation(out=ab[:], in_=pa, func=AF.Gelu), reads=["PAB%d" % pq], writes=["Ab%d" % pp])
                S.op("pool", lambda e, ab=ab, wb_=wb_, i2=i2: e.tensor_tensor(out=wb_[:], in0=ab[:], in1=G[:, i2, :], op=ALU.mult),
                     reads=["Ab%d" % pp, "G"], writes=["Wb%d" % pp])
            if step >= SK:
                i2 = step - SK
                b2, ub2, j2 = binfo[i2]
                pp = i2 % NSL
                wb_ = Wb[pp]
                for tt in range(2):
                    for hf in range(2):
                        S.op("pe", lambda e, tt=tt, hf=hf, wb_=wb_, ub2=ub2, j2=j2, i2=i2: e.matmul(W01[tt][:, hf * 512:(hf + 1) * 512], lhsT=wb_[:, tt * 128:(tt + 1) * 128],
                                                                                                  rhs=ub2[:, j2, hf * 512:(hf + 1) * 512], start=(i2 == 0), stop=(i2 == 127)),
                             reads=["Wb%d" % pp, "ubuf%d" % b2], writes=["W%d" % tt])
        for tt in range(2):
            r0 = t0 + tt * 128
            okey = ("out_d", r0 // 128)
            xx, yy = x3[tt], y3[tt]
            S.op("sp", lambda e, r0=r0, xx=xx: e.dma_start(out=xx[:], in_=out_d[r0:r0 + 128, :]), reads=[okey], writes=["x3_%d" % tt], dma_chan="c_x3_%d" % tt)
            S.op("dve", lambda e, tt=tt, yy=yy: e.tensor_tensor(out=yy[:], in0=W01[tt], in1=gate2B[:], op=ALU.mult), reads=["W%d" % tt, "gate2B"], writes=["y3_%d" % tt])
            S.op("pool", lambda e, xx=xx, yy=yy: e.tensor_tensor(out=xx[:], in0=xx[:], in1=yy[:], op=ALU.add), reads=["x3_%d" % tt, "y3_%d" % tt], writes=["x3_%d" % tt])
            S.op("act", lambda e, xx=xx, yy=yy, tt=tt: e.activation(out=yy[:], in_=xx[:], func=AF.Square, accum_out=ss3[:, tt:tt + 1]),
                 reads=["x3_%d" % tt], writes=["y3_%d" % tt, "ss3"])
            S.op("dve", lambda e, tt=tt: e.tensor_scalar(out=ss3[:, 2 + tt:3 + tt], in0=ss3[:, tt:tt + 1], scalar1=1.0 / D, scalar2=EPS, op0=ALU.mult, op1=ALU.add),
                 reads=["ss3"], writes=["ss3"])
            S.op("pool", lambda e, tt=tt: e.tensor_tensor(out=rstd3[:, tt:tt + 1], in0=ss3[:, 2 + tt:3 + tt], in1=neghalf[:, 0:1], op=ALU.pow),
                 reads=["ss3", "neghalf"], writes=["rstd3"])
            S.op("dve", lambda e, xx=xx, yy=yy, tt=tt: e.scalar_tensor_tensor(out=yy[:], in0=xx[:], scalar=rstd3[:, tt:tt + 1], in1=finwB[:], op0=ALU.mult, op1=ALU.mult),
                 reads=["x3_%d" % tt, "rstd3", "finwB"], writes=["y3_%d" % tt])
            S.op("sp", lambda e, r0=r0, yy=yy: e.dma_start(out=out_d[r0:r0 + 128, :], in_=yy[:]), reads=["y3_%d" % tt], writes=[okey], dma_chan="c_fin%d" % tt)
    S.emit(final_waits=["c_out0", "c_out1", "c_fin0", "c_fin1"])
    return nc


def host_inputs(inputs):
    x = np.asarray(inputs["x"], np.float32)
    f = lambda k: np.asarray(inputs[k], np.float32)
    c = f("c")
    shared = {
        "w_ada": np.ascontiguousarray(f("w_ada")[0]),
        "b_ada": np.ascontiguousarray(f("b_ada")[0][None, :]),
        "w_in": np.ascontiguousarray(f("w_in")[0]),
        "w_alpha_up": np.ascontiguousarray(f("w_alpha_up")[0]),
        "b_alpha": np.ascontiguousarray(f("b_alpha")[0][None, :]),
        "sgu_wT": np.ascontiguousarray(f("sgu_w")[0].transpose(0, 2, 1)),
        "sgu_bT": np.ascontiguousarray(f("sgu_b")[0].T),
        "w_out": np.ascontiguousarray(f("w_out")[0]),
        "peer_w_q": np.ascontiguousarray(f("peer_w_q")[0]),
        "keys1T": np.ascontiguousarray(f("peer_keys1")[0].T),
        "keys2T": np.ascontiguousarray(f("peer_keys2")[0].T),
        "downT": np.ascontiguousarray(f("peer_down")[0].reshape(128, 128, 8, 128).transpose(3, 1, 2, 0)),
        "up": np.ascontiguousarray(f("peer_up")[0].reshape(128, 128, D)),
        "colv": np.ascontiguousarray(np.stack([f("norm1_w")[0].reshape(8, 128).T, f("norm2_w")[0].reshape(8, 128).T], axis=1)),
        "rowv": np.ascontiguousarray(np.stack([f("final_norm_w"), f("gla_norm_w")[0], f("sgu_ln_w")[0], f("sgu_ln_b")[0]], axis=0)),
    }
    maps = []
    for i in range(8):
        b, j = i // 4, i % 4
        xpre = np.zeros((3, 2048, D), np.float32)
        flags = np.zeros((128, 4), np.float32)
        flags[:, 3] = 1.0
        for s in range(3):
            qidx = s - (3 - j)
            if qidx >= 0:
                xpre[s] = x[b, qidx * 2048:(qidx + 1) * 2048]
                flags[:, s] = 1.0
        m = dict(shared)
        m["xpre"] = xpre
        m["xown"] = np.ascontiguousarray(x[b, j * 2048:(j + 1) * 2048])
        m["flags"] = flags
        m["cT"] = np.ascontiguousarray(c[b].reshape(8, 128).T)
        maps.append(m)
    return maps


_NC_CACHE = {}


def kernel(**inputs):
    maps = host_inputs(inputs)
    if "nc" not in _NC_CACHE:
        _NC_CACHE["nc"] = build()
    nc = _NC_CACHE["nc"]
    res = run_bass_kernel_spmd(nc, maps, core_ids=list(range(8)))
    out = np.zeros((2, 8192, D), np.float32)
    for i in range(8):
        b, j = i // 4, i % 4
        out[b, j * 2048:(j + 1) * 2048] = res.results[i]["out"]
    return out
```

```python
import contextlib
import numpy as np
import concourse.bass as bass
import concourse.mybir as mybir
from concourse.bass_utils import run_bass_kernel_spmd

F32 = mybir.dt.float32
BF16 = mybir.dt.bfloat16
U32 = mybir.dt.uint32
AF = mybir.ActivationFunctionType
ALU = mybir.AluOpType
AX = mybir.AxisListType

ENGS = ("pe", "act", "dve", "pool", "sp")
EPS = 1e-6
NT = 16
import os
NDBG = int(os.environ.get('NDBG', '1'))
D = 1024
IN_SIZES = (512, 512, 1024, 1024, 16, 1024, 1024, 1024, 1024)
IN_OFF = [int(v) for v in np.cumsum((0,) + IN_SIZES)]
OQ, OK_, OV, OR, OALR, OSU, OSV, OGA, OGB = IN_OFF[:9]


class _Op:
    __slots__ = ("eng", "fn", "waits", "sig", "is_dma", "chan")


class Sched:
    def __init__(self, nc):
        self.nc = nc
        self.q = {e: [] for e in ENGS}
        self.chan_count = {}
        self.last_w = {}
        self.readers = {}
        self.cap = None

    def _add_wait(self, op, tok):
        if tok is None:
            return
        if tok[0] == "e":
            if tok[1] == op.eng and tok[1] == "pe":
                return
            if tok[2].sig is None:
                tok[2].sig = 1
        op.waits.append(tok)

    def op(self, eng, fn, reads=(), writes=(), dma_chan=None):
        if self.cap is not None:
            self.cap.append((eng, fn, tuple(reads), tuple(writes), dma_chan))
            return None
        o = _Op()
        o.eng = eng
        o.fn = fn
        o.waits = []
        o.sig = None
        o.is_dma = dma_chan is not None
        o.chan = dma_chan
        for k in reads:
            self._add_wait(o, self.last_w.get(k))
        for k in writes:
            self._add_wait(o, self.last_w.get(k))
            lastr = {}
            for t in self.readers.get(k, ()):
                if t[0] == "e":
                    lastr[t[1]] = t
                else:
                    self._add_wait(o, t)
            for t in lastr.values():
                self._add_wait(o, t)
        if o.is_dma:
            self.chan_count[dma_chan] = self.chan_count.get(dma_chan, 0) + 1
            tok = ("d", dma_chan, 16 * self.chan_count[dma_chan])
        else:
            tok = ("e", eng, o)
        for k in writes:
            self.last_w[k] = tok
            self.readers[k] = []
        for k in reads:
            self.readers.setdefault(k, []).append(tok)
        self.q[eng].append(o)
        return o

    def barrier(self):
        toks = []
        for e in ENGS:
            last = None
            for o in reversed(self.q[e]):
                if not o.is_dma:
                    last = o
                    break
            if last is not None:
                toks.append(("e", e, last))
        for c, n in self.chan_count.items():
            toks.append(("d", c, 16 * n))
        for e in ENGS:
            o = _Op()
            o.eng = e
            o.fn = lambda eng: eng.nop()
            o.waits = []
            o.sig = None
            o.is_dma = False
            o.chan = None
            for t in toks:
                if t[0] == "e":
                    if t[2].sig is None:
                        t[2].sig = 1
                o.waits.append(t)
            self.q[e].append(o)
        self.last_w = {}
        self.readers = {}

    def emit(self, final_waits=()):
        nc = self.nc
        for e in ENGS:
            n = 0
            for o in self.q[e]:
                if o.sig is not None and not o.is_dma:
                    n += 1
                    o.sig = n
        chans = sorted(self.chan_count.keys(), key=str)
        print("ops:", {e: len(self.q[e]) for e in ENGS}, "chan max:", max(self.chan_count.values()) * 16)
        with contextlib.ExitStack() as st:
            esem = {e: st.enter_context(nc.semaphore("s_" + e)) for e in ENGS}
            csem = {c: st.enter_context(nc.semaphore("c_%d" % i)) for i, c in enumerate(chans)}
            block = st.enter_context(nc.Block())

            def run(ename, eng):
                seen = {}
                for o in self.q[ename]:
                    for w in o.waits:
                        if w[0] == "e":
                            key = ("e", w[1]); val = w[2].sig; sem = esem[w[1]]
                        else:
                            key = ("d", w[1]); val = w[2]; sem = csem[w[1]]
                        if seen.get(key, 0) >= val:
                            continue
                        seen[key] = val
                        eng.wait_ge(sem, val)
                    ins = o.fn(eng)
                    if o.is_dma:
                        ins.then_inc(csem[o.chan], 16)
                    elif o.sig is not None:
                        ins.then_inc(esem[ename], 1)
                if ename == "sp":
                    for c in final_waits:
                        eng.wait_ge(csem[c], 16 * self.chan_count[c])

            @block.sync
            def _(eng):
                run("sp", eng)

            @block.scalar
            def _(eng):
                run("act", eng)

            @block.vector
            def _(eng):
                run("dve", eng)

            @block.gpsimd
            def _(eng):
                run("pool", eng)

            @block.tensor
            def _(eng):
                run("pe", eng)


class Mem:
    def __init__(self, nc, base=20608, limit=229376):
        self.nc = nc
        self.off = base
        self.limit = limit
        self.n = 0

    def alloc(self, name, shape, dt):
        size = 1
        for s in shape[1:]:
            size *= s
        size *= {F32: 4, BF16: 2, U32: 4}[dt]
        size = (size + 63) // 64 * 64
        assert self.off + size <= self.limit, (name, self.off, size)
        self.n += 1
        t = self.nc.alloc_sbuf_tensor_at("%s_%d" % (name, self.n), list(shape), dt, offset=self.off)
        self.off += size
        return t


def _dtsize(dt):
    return {F32: 4, BF16: 2, U32: 4}[dt]


def build(stage=9):
    nc = bass.Bass("TRN2", target_bir_lowering=False)
    S = Sched(nc)
    M = Mem(nc)
    M_alloc = M.alloc

    def dram(name, shape, kind="ExternalInput", dt=F32):
        return nc.dram_tensor(name, list(shape), dt, kind=kind).ap()

    xpre = dram("xpre", [3, 2048, D])
    xown = dram("xown", [2048, D])
    flags_d = dram("flags", [128, 4])
    cT_d = dram("cT", [128, 8])
    colv_d = dram("colv", [128, 2, 8])
    rowv_d = dram("rowv", [4, D])
    wada_d = dram("w_ada", [D, 6 * D])
    bada_d = dram("b_ada", [1, 6 * D])
    win_d = dram("w_in", [D, IN_OFF[9]])
    wup_d = dram("w_alpha_up", [16, 512])
    balpha_d = dram("b_alpha", [1, 512])
    sguw_d = dram("sgu_wT", [8, 128, 128])
    sgub_d = dram("sgu_bT", [128, 8])
    wout_d = dram("w_out", [D, D])
    wq_d = dram("peer_w_q", [D, 2048])
    k1T_d = dram("keys1T", [128, 128])
    k2T_d = dram("keys2T", [128, 128])
    downT_d = dram("downT", [128, 128, 8, 128])
    up_d = dram("up", [128, 128, D])
    out_d = dram("out", [2048, D], kind="ExternalOutput")
    scr_down = nc.dram_tensor("scr_down", [128, 128, 8, 128], BF16).ap()
    scr_up = nc.dram_tensor("scr_up", [128, 128, D], BF16).ap()
    cvt_jobs = []
    for g in range(32):
        cvt_jobs.append((scr_down[:, g * 4:(g + 1) * 4, :, :], downT_d[:, g * 4:(g + 1) * 4, :, :]))
        cvt_jobs.append((scr_up[:, g * 4:(g + 1) * 4, :], up_d[:, g * 4:(g + 1) * 4, :]))

    def issue_cvt(n):
        for _ in range(n):
            if cvt_jobs:
                o_, i_ = cvt_jobs.pop(0)
                S.op("pool", lambda e, o_=o_, i_=i_: e.dma_start(out=o_, in_=i_), writes=["scr"], dma_chan="c_cvt")

    P01 = nc.alloc_psum_tensor("P01", [128, 1024], F32)
    P23 = nc.alloc_psum_tensor("P23", [128, 1024], F32)
    P45 = nc.alloc_psum_tensor("P45", [128, 1024], F32)
    P67 = nc.alloc_psum_tensor("P67", [128, 1024], F32)
    W0 = P01[:, :]; W1 = P23[:, :]; W2 = P67[:, :]
    W0a = P01[:, 0:512]; W0b = P01[:, 512:1024]
    W2a = P67[:, 0:512]; W2b = P67[:, 512:1024]
    PA = P45[:, 0:512]; PB = P45[:, 512:1024]
    kW2 = ["W2a", "W2b"]

    def alloc(name, shape, dt):
        return M_alloc(name, shape, dt)

    ident_f = alloc("ident_f", [128, 128], F32)
    ident_b = alloc("ident_b", [128, 128], BF16)
    iota_f = alloc("iota_f", [128, 128], F32)
    iota_b = alloc("iota_b", [128, 128], BF16)
    Lm = alloc("Lm", [128, 128], F32)
    Rm = alloc("Rm", [128, 128], F32)
    maskU = alloc("maskU", [128, 128], F32)
    ones_f = alloc("ones_f", [128, 128], F32)
    neghalf = alloc("neghalf", [128, 8], F32)
    flags = alloc("flagsb", [128, 4], F32)
    cols = alloc("cols", [128, 6, 8], F32)
    mark_p3 = M.off
    Bsgu = alloc("Bsgu", [128, 8], F32)
    WsT = alloc("WsT", [128, 8, 128], BF16)
    wup_f = alloc("wup_f", [16, 512], F32)
    balpha_f = alloc("balpha_f", [1, 512], F32)
    lnwB = alloc("lnwB", [128, D], F32)
    lnbB = alloc("lnbB", [128, D], F32)
    gnwB = alloc("gnwB", [128, D], F32)
    woutg = alloc("woutg", [128, 8, D], BF16)
    wk = alloc("wk", [128, 8, 512], BF16)
    wv = alloc("wv", [128, 8, 1024], BF16)
    walr = alloc("walr", [128, 8, 16], BF16)
    S_f = alloc("S_f", [128, 4, 256], F32)
    Sb0 = alloc("Sb0", [128, 4, 256], BF16)
    Sb1 = alloc("Sb1", [128, 4, 256], BF16)
    mark_late = M.off

    pid = alloc("pid", [128, 1], F32)
    tmpA = alloc("tmpA", [128, 128], F32)
    tmpB = alloc("tmpB", [128, 128], F32)
    cj = alloc("cj", [128, 1], F32)
    scB = alloc("scB", [128, 8, 128], F32)
    cT = alloc("cTs", [128, 8], F32)
    sigc = alloc("sigc", [128, 8], F32)
    colv = alloc("colvs", [128, 2, 8], F32)
    modB = alloc("modB", [128, 6 * D], F32)
    wbuf = [alloc("wbuf%d" % i, [128, 8, 512], F32) for i in range(2)]
    bb = [alloc("bb%d" % i, [128, 512], F32) for i in range(2)]
    wo_tmp = alloc("wo_tmp", [128, 4, D], F32)
    sgu_tmp = alloc("sgu_tmp", [128, 8, 128], F32)

    def iota(e, out, pattern, base, cm):
        return e.iota(out, pattern=pattern, base=base, channel_multiplier=cm,
                      allow_small_or_imprecise_dtypes=True)

    S.op("pool", lambda e: iota(e, iota_f[:], [[1, 128]], 0, 0), writes=["iota_f"])
    S.op("dve", lambda e: e.tensor_copy(out=iota_b[:], in_=iota_f[:]), reads=["iota_f"], writes=["iota_b"])
    S.op("pool", lambda e: iota(e, pid[:], [[0, 1]], 0, 1), writes=["pid"])
    S.op("pool", lambda e: iota(e, tmpA[:], [[1, 128]], 0, -1), writes=["tmpA"])
    S.op("pool", lambda e: e.memset(ones_f[:], 1.0), writes=["ones_f"])
    S.op("pool", lambda e: e.memset(neghalf[:], -0.5), writes=["neghalf"])
    S.op("pool", lambda e: e.memset(S_f[:], 0.0), writes=["S_f"])
    S.op("dve", lambda e: e.tensor_single_scalar(out=ident_f[:], in_=tmpA[:], scalar=0.0, op=ALU.is_equal),
         reads=["tmpA"], writes=["ident_f"])
    S.op("dve", lambda e: e.tensor_copy(out=ident_b[:], in_=ident_f[:]), reads=["ident_f"], writes=["ident_b"])
    S.op("dve", lambda e: e.tensor_single_scalar(out=cj[:], in_=pid[:], scalar=64.0, op=ALU.is_ge),
         reads=["pid"], writes=["cj"])
    S.op("dve", lambda e: e.tensor_scalar(out=tmpB[:], in0=iota_f[:], scalar1=64.0, scalar2=cj[:, 0:1],
                                          op0=ALU.is_ge, op1=ALU.is_equal),
         reads=["iota_f", "cj"], writes=["tmpB"])
    maskS = alloc("maskS", [128, 128], F32)
    S.op("dve", lambda e: e.tensor_single_scalar(out=maskS[:], in_=tmpA[:], scalar=0.0, op=ALU.is_ge),
         reads=["tmpA"], writes=["maskS"])
    S.op("dve", lambda e: e.tensor_tensor(out=maskU[:], in0=maskS[:], in1=tmpB[:], op=ALU.mult),
         reads=["maskS", "tmpB"], writes=["maskU"])
    S.op("dve", lambda e: e.tensor_scalar(out=Lm[:], in0=maskU[:], scalar1=-1.0 / 16.0, scalar2=None, op0=ALU.mult),
         reads=["maskU"], writes=["Lm"])
    S.op("dve", lambda e: e.tensor_tensor(out=Rm[:], in0=tmpB[:], in1=maskU[:], op=ALU.subtract),
         reads=["tmpB", "maskU"], writes=["Rm"])
    S.op("dve", lambda e: e.tensor_scalar(out=Rm[:], in0=Rm[:], scalar1=-1.0 / 16.0, scalar2=None, op0=ALU.mult),
         reads=["Rm"], writes=["Rm"])

    S.op("sp", lambda e: e.dma_start(out=flags[:], in_=flags_d[:, :]), writes=["flags"], dma_chan="c_flags")
    S.op("sp", lambda e: e.dma_start(out=cT[:], in_=cT_d[:, :]), writes=["cT"], dma_chan="c_cT")
    S.op("sp", lambda e: e.dma_start(out=colv[:], in_=colv_d[:, :, :]), writes=["colv"], dma_chan="c_colv")
    S.op("sp", lambda e: e.dma_start(out=Bsgu[:], in_=sgub_d[:, :]), writes=["Bsgu"], dma_chan="c_bsgu")
    S.op("sp", lambda e: e.dma_start(out=wup_f[:], in_=wup_d[:, :]), writes=["wup_f"], dma_chan="c_wup")
    S.op("sp", lambda e: e.dma_start(out=balpha_f[:], in_=balpha_d[:, :]), writes=["balpha_f"], dma_chan="c_balpha")
    S.op("act", lambda e: e.dma_start(out=gnwB[:], in_=rowv_d[1:2, :].to_broadcast([128, D])), writes=["gnwB"], dma_chan="c_gnw")
    S.op("act", lambda e: e.dma_start(out=lnwB[:], in_=rowv_d[2:3, :].to_broadcast([128, D])), writes=["lnwB"], dma_chan="c_lnw")
    S.op("act", lambda e: e.dma_start(out=lnbB[:], in_=rowv_d[3:4, :].to_broadcast([128, D])), writes=["lnbB"], dma_chan="c_lnb")
    S.op("sp", lambda e: e.dma_start(out=sgu_tmp[:], in_=sguw_d.rearrange("g j i -> j g i")), writes=["sgu_tmp"], dma_chan="c_sguw")
    win_v = win_d.rearrange("(c p) e -> p c e", p=128)
    for ch in range(8):
        S.op("pool", lambda e, ch=ch: e.dma_start(out=wk[:, ch, :], in_=win_v[:, ch, OK_:OK_ + 512]), writes=["wk"], dma_chan="c_wk")
        S.op("pool", lambda e, ch=ch: e.dma_start(out=wv[:, ch, :], in_=win_v[:, ch, OV:OV + 1024]), writes=["wv"], dma_chan="c_wv")
        S.op("pool", lambda e, ch=ch: e.dma_start(out=walr[:, ch, :], in_=win_v[:, ch, OALR:OALR + 16]), writes=["walr"], dma_chan="c_walr")

    S.op("dve", lambda e: e.tensor_tensor(out=WsT[:], in0=sgu_tmp[:], in1=maskS[:].unsqueeze(1).to_broadcast([128, 8, 128]), op=ALU.mult),
         reads=["sgu_tmp", "maskS"], writes=["WsT"])

    S.op("act", lambda e: e.activation(out=sigc[:], in_=cT[:], func=AF.Sigmoid), reads=["cT"], writes=["sigc"])
    S.op("dve", lambda e: e.tensor_tensor(out=sigc[:], in0=sigc[:], in1=cT[:], op=ALU.mult), reads=["sigc", "cT"], writes=["sigc"])
    S.op("dve", lambda e: e.tensor_copy(out=scB[:], in_=sigc[:].unsqueeze(2).to_broadcast([128, 8, 128])),
         reads=["sigc"], writes=["scB"])

    wada_v = wada_d.rearrange("(c p) e -> p c e", p=128)
    for g in range(12):
        b = g % 2
        wb_, bb_ = wbuf[b], bb[b]
        kq = "sp" if b == 0 else "act"
        S.op(kq, lambda e, g=g, wb_=wb_: e.dma_start(out=wb_[:], in_=wada_v[:, :, g * 512:(g + 1) * 512]),
             writes=["wbuf%d" % b], dma_chan="c_wbuf%d" % b)
        S.op(kq, lambda e, g=g, bb_=bb_: e.dma_start(out=bb_[:], in_=bada_d[0:1, g * 512:(g + 1) * 512].to_broadcast([128, 512])),
             writes=["bb%d" % b], dma_chan="c_bb%d" % b)
        for ch in range(8):
            S.op("pe", lambda e, ch=ch, wb_=wb_: e.matmul(PA, lhsT=scB[:, ch, :], rhs=wb_[:, ch, :], start=(ch == 0), stop=(ch == 7)),
                 reads=["scB", "wbuf%d" % b], writes=["PA"])
        S.op("dve", lambda e, g=g, bb_=bb_: e.tensor_tensor(out=modB[:, g * 512:(g + 1) * 512], in0=PA, in1=bb_[:], op=ALU.add),
             reads=["PA", "bb%d" % b], writes=["modB"])

    def col_from_mod(dst_idx, mod_off):
        for ch in range(8):
            S.op("pe", lambda e, ch=ch: e.transpose(PB[:, 0:128], modB[:, mod_off + ch * 128: mod_off + (ch + 1) * 128], ident_f[:]),
                 reads=["modB", "ident_f"], writes=["PB"])
            S.op("dve", lambda e, ch=ch: e.tensor_copy(out=cols[:, dst_idx, ch:ch + 1], in_=PB[:, 0:1]),
                 reads=["PB"], writes=["cols"])
    col_from_mod(1, 0 * D)
    col_from_mod(0, 1 * D)
    col_from_mod(3, 3 * D)
    col_from_mod(2, 4 * D)
    col_from_mod(4, 5 * D)
    for di, ci in ((0, 0), (2, 1)):
        S.op("dve", lambda e, di=di, ci=ci: e.scalar_tensor_tensor(out=cols[:, di, :], in0=cols[:, di, :], scalar=1.0,
                                                                   in1=colv[:, ci, :], op0=ALU.add, op1=ALU.mult),
             reads=["cols", "colv"], writes=["cols"])

    wout_v = wout_d.rearrange("(c p) e -> p c e", p=128)
    for hf in range(2):
        S.op("sp", lambda e, hf=hf: e.dma_start(out=wo_tmp[:], in_=wout_v[:, hf * 4:(hf + 1) * 4, :]), writes=["wo_tmp"], dma_chan="c_wo")
        S.op("dve", lambda e, hf=hf: e.tensor_tensor(out=woutg[:, hf * 4:(hf + 1) * 4, :], in0=wo_tmp[:],
                                                     in1=modB[:, 2 * D:3 * D].unsqueeze(1).to_broadcast([128, 4, D]), op=ALU.mult),
             reads=["wo_tmp", "modB"], writes=["woutg"])

    if stage == 0:
        dbg = dram("dbg", [128, 2048], kind="ExternalOutput")
        S.op("sp", lambda e: e.dma_start(out=dbg[:, 0:48], in_=cols[:].rearrange("p a b -> p (a b)")), reads=["cols"], writes=["dbg"], dma_chan="c_out")
        S.op("sp", lambda e: e.dma_start(out=dbg[:, 128:256], in_=Lm[:]), reads=["Lm"], writes=["dbg"], dma_chan="c_out")
        S.op("sp", lambda e: e.dma_start(out=dbg[:, 256:384], in_=Rm[:]), reads=["Rm"], writes=["dbg"], dma_chan="c_out")
        S.op("sp", lambda e: e.dma_start(out=dbg[:, 1024:2048], in_=modB[:, 2048:3072]), reads=["modB"], writes=["dbg"], dma_chan="c_out")
        S.emit(final_waits=["c_out0", "c_out1"])
        return nc
    S.barrier()
    M.off = mark_late

    wq = alloc("wq", [128, 8, 512], BF16)
    wr = alloc("wr", [128, 8, 1024], BF16)
    wsu = alloc("wsu", [128, 8, 1024], BF16)
    wsv = alloc("wsv", [128, 8, 1024], BF16)
    wga = alloc("wga", [128, 8, 1024], BF16)
    wgb = alloc("wgb", [128, 8, 1024], BF16)
    mark_work = M.off

    xt = alloc("xt", [128, D], F32)
    xtB = alloc("xtB", [128, D], F32)
    mTb = alloc("mTb", [128, 8, 128], BF16)
    Fs = [alloc("F%d" % i, [128, D], F32) for i in range(4)]
    xn, t2_, t1_, u_, gv_ = Fs[0], Fs[0], Fs[1], Fs[2], Fs[3]
    hT = alloc("hT", [128, 8, 128], BF16)
    v_bf = alloc("v_bf", [128, D], BF16)
    alrT = alloc("alrT", [16, 128], F32)
    bufE = alloc("bufE", [128, 512], F32)
    lbuf = alloc("lbuf", [128, 512], F32)
    expnG = alloc("expnG", [128, 512], F32)
    ktail = alloc("ktail", [128, 512], BF16)
    dec = alloc("dec", [128, 4, 2], F32)
    ss = alloc("ss", [128, 8], F32)
    rstd = alloc("rstd", [128, 8], F32)
    qd = alloc("qd", [128, 4, 128], BF16)
    q0 = alloc("q0", [128, 4, 128], BF16)
    q1 = alloc("q1", [128, 4, 128], BF16)
    ki = alloc("ki", [128, 4, 128], BF16)
    ATm = alloc("ATm", [128, 4, 128], BF16)
    vln = alloc("vln", [128, D], BF16)
    bnst = alloc("bnst", [128, 2, 6], F32)
    mv = alloc("mv", [128, 2], F32)
    print("SBUF used (P2):", M.off)

    S.op("pool", lambda e: e.memset(q0[:], 0.0), writes=["q0"])
    S.op("pool", lambda e: e.memset(q1[:], 0.0), writes=["q1"])

    def load_late():
        for ch in range(8):
            for (wt, off, n, nm) in ((wq, OQ, 512, "wq"), (wr, OR, 1024, "wr"), (wga, OGA, 1024, "wga"),
                                     (wsu, OSU, 1024, "wsu"), (wsv, OSV, 1024, "wsv"), (wgb, OGB, 1024, "wgb")):
                S.op("pool", lambda e, ch=ch, wt=wt, off=off, n=n: e.dma_start(out=wt[:, ch, :], in_=win_v[:, ch, off:off + n]),
                     writes=[nm], dma_chan="c_" + nm)

    def proj_tm(dst, dkeys, wt, wkey, c0, n):
        for ch in range(8):
            S.op("pe", lambda e, ch=ch: e.matmul(dst, lhsT=hT[:, ch, :], rhs=wt[:, ch, c0:c0 + n], start=(ch == 0), stop=(ch == 7)),
                 reads=[("hT", ch), wkey], writes=dkeys)

    xts = [xt, xtB]

    def p2_front(k):
        xk = xts[k % 2]
        kx = "xt%d" % (k % 2)
        x_ap = xown[k * 128:(k + 1) * 128, :]
        S.op("sp", lambda e: e.dma_start(out=xk[:], in_=x_ap), writes=[kx], dma_chan="c_xt%d" % (k % 2))
        S.op("act", lambda e: e.activation(out=xn[:], in_=xk[:], func=AF.Square, accum_out=ss[:, 0:1]),
             reads=[kx], writes=["F0", "ss_f"])
        S.op("dve", lambda e: e.tensor_scalar(out=ss[:, 1:2], in0=ss[:, 0:1], scalar1=1.0 / D, scalar2=EPS, op0=ALU.mult, op1=ALU.add),
             reads=["ss_f"], writes=["ss_f2"])
        S.op("pool", lambda e: e.tensor_tensor(out=rstd[:, 0:1], in0=ss[:, 1:2], in1=neghalf[:, 0:1], op=ALU.pow),
             reads=["ss_f2", "neghalf"], writes=["rstd_f"])
        S.op("dve", lambda e: e.tensor_scalar(out=xn[:], in0=xk[:], scalar1=rstd[:, 0:1], scalar2=None, op0=ALU.mult),
             reads=[kx, "rstd_f"], writes=["F0"])
        for ch in range(8):
            S.op("pe", lambda e, ch=ch: e.transpose(W0[:, ch * 128:(ch + 1) * 128], xn[:, ch * 128:(ch + 1) * 128], ident_f[:]),
                 reads=["F0", "ident_f"], writes=["W0a", "W0b"])
        for ch in range(8):
            if ch < 4:
                S.op("dve", lambda e, ch=ch: e.tensor_scalar(out=hT[:, ch, :], in0=W0[:, ch * 128:(ch + 1) * 128],
                                                             scalar1=cols[:, 0, ch:ch + 1], scalar2=cols[:, 1, ch:ch + 1],
                                                             op0=ALU.mult, op1=ALU.add),
                     reads=["W0a" if ch < 4 else "W0b", "cols"], writes=[("hT", ch)])
            else:
                S.op("act", lambda e, ch=ch: e.activation(out=hT[:, ch, :], in_=W0[:, ch * 128:(ch + 1) * 128], func=AF.Identity,
                                                          scale=cols[:, 0, ch:ch + 1], bias=cols[:, 1, ch:ch + 1]),
                     reads=["W0a" if ch < 4 else "W0b", "cols"], writes=[("hT", ch)])

    def p2_body(k):
        seg, full, row0 = 3, True, k * 128
        fcol = flags[:, seg:seg + 1]
        capA = []
        S.cap = capA
        proj_tm(PA, ["PA"], wk, "wk", 0, 512)
        proj_tm(W1[:, 0:512], ["W1"], wv, "wv", 0, 512)
        proj_tm(W1[:, 512:1024], ["W1"], wv, "wv", 512, 512)
        for ch in range(8):
            S.op("pe", lambda e, ch=ch: e.matmul(PB[0:16, 0:128], lhsT=walr[:, ch, :], rhs=hT[:, ch, :], start=(ch == 0), stop=(ch == 7)),
                 reads=[("hT", ch), "walr"], writes=["PB"])
        S.op("dve", lambda e: e.tensor_scalar(out=v_bf[:], in0=W1, scalar1=fcol, scalar2=None, op0=ALU.mult),
             reads=["W1", "flags"], writes=["v_bf"])
        S.op("act", lambda e: e.activation(out=alrT[:], in_=PB[0:16, 0:128], func=AF.Copy), reads=["PB"], writes=["alrT"])
        S.op("pe", lambda e: e.matmul(W0a, lhsT=alrT[:], rhs=wup_f[:], start=True, stop=False),
             reads=["alrT", "wup_f"], writes=["W0a"])
        S.op("pe", lambda e: e.matmul(W0a, lhsT=ones_f[0:1, :], rhs=balpha_f[:], start=False, stop=True),
             reads=["ones_f", "balpha_f"], writes=["W0a"])
        S.op("act", lambda e: e.activation(out=bufE[:], in_=W0a, func=AF.Exp, scale=-1.0), reads=["W0a"], writes=["bufE"])
        S.op("act", lambda e: e.activation(out=lbuf[:], in_=bufE[:], func=AF.Ln, bias=1.0, scale=1.0), reads=["bufE"], writes=["lbuf"])
        S.op("pe", lambda e: e.matmul(W0b, lhsT=Rm[:], rhs=lbuf[:], start=True, stop=True), reads=["Rm", "lbuf"], writes=["W0b"])
        for h in range(4):
            S.op("pe", lambda e, h=h: e.matmul(PB[:, h * 128:(h + 1) * 128], lhsT=lbuf[:, h * 128:(h + 1) * 128], rhs=Lm[:], start=True, stop=True),
                 reads=["lbuf", "Lm"], writes=["PB"])
        S.op("act", lambda e: e.activation(out=bufE[:], in_=W0b, func=AF.Exp), reads=["W0b"], writes=["bufE"])
        S.op("dve", lambda e: e.tensor_tensor(out=ktail[:], in0=PA, in1=bufE[:], op=ALU.mult), reads=["PA", "bufE"], writes=["ktail"])
        PBv = PB.rearrange("p (h t) -> p h t", h=4)
        S.op("act", lambda e: e.activation(out=dec[:], in_=PBv[:, :, 63:128:64], func=AF.Exp), reads=["PB"], writes=["dec"])
        if full:
            for h in range(4):
                for ch in range(8):
                    S.op("pe", lambda e, h=h, ch=ch: e.matmul(W0a[:, h * 128:(h + 1) * 128], lhsT=wq[:, ch, h * 128:(h + 1) * 128], rhs=hT[:, ch, :],
                                                             start=(ch == 0), stop=(ch == 7)),
                         reads=[("hT", ch), "wq"], writes=["W0a"])
            for h in range(4):
                for ch in range(8):
                    S.op("pe", lambda e, h=h, ch=ch: e.matmul(W0b[:, h * 128:(h + 1) * 128], lhsT=wk[:, ch, h * 128:(h + 1) * 128], rhs=hT[:, ch, :],
                                                             start=(ch == 0), stop=(ch == 7)),
                         reads=[("hT", ch), "wk"], writes=["W0b"])
            S.op("act", lambda e: e.activation(out=lbuf[:], in_=PB, func=AF.Exp), reads=["PB"], writes=["lbuf"])
            S.op("act", lambda e: e.activation(out=expnG[:], in_=PB, func=AF.Exp, scale=-1.0), reads=["PB"], writes=["expnG"])
            qdf = qd[:].rearrange("p h t -> p (h t)")
            kif = ki[:].rearrange("p h t -> p (h t)")
            S.op("dve", lambda e: e.scalar_tensor_tensor(out=qdf, in0=W0a, scalar=128.0 ** -0.5, in1=lbuf[:], op0=ALU.mult, op1=ALU.mult),
                 reads=["W0a", "lbuf"], writes=["qd"])
            S.op("dve", lambda e: e.tensor_tensor(out=kif, in0=W0b, in1=expnG[:], op=ALU.mult), reads=["W0b", "expnG"], writes=["ki"])
            S.op("pool", lambda e: e.tensor_copy(out=q0[:, :, 0:64], in_=qd[:, :, 0:64]), reads=["qd"], writes=["q0"])
            S.op("pool", lambda e: e.tensor_copy(out=q1[:, :, 64:128], in_=qd[:, :, 64:128]), reads=["qd"], writes=["q1"])
            for h in range(4):
                S.op("pe", lambda e, h=h: e.matmul(PA[:, h * 128:(h + 1) * 128], lhsT=ki[:, h, :], rhs=qd[:, h, :], start=True, stop=True),
                     reads=["ki", "qd"], writes=["PA"])
            S.op("dve", lambda e: e.tensor_tensor(out=ATm[:], in0=PA.rearrange("p (h t) -> p h t", h=4),
                                                  in1=maskU[:].unsqueeze(1).to_broadcast([128, 4, 128]), op=ALU.mult),
                 reads=["PA", "maskU"], writes=["ATm"])
        W1v = W1.rearrange("p (h v) -> p h v", h=4)

        def kv_update(c):
            for h in range(4):
                S.op("pe", lambda e, h=h: e.matmul(W1v[:, h, :], lhsT=ktail[c * 64:(c + 1) * 64, h * 128:(h + 1) * 128],
                                                   rhs=v_bf[c * 64:(c + 1) * 64, h * 256:(h + 1) * 256], start=True, stop=True),
                     reads=["ktail", "v_bf"], writes=["W1"])
            for h in range(4):
                S.op("dve", lambda e, h=h: e.scalar_tensor_tensor(out=S_f[:, h, :], in0=S_f[:, h, :], scalar=dec[:, h, c:c + 1],
                                                                  in1=W1v[:, h, :], op0=ALU.mult, op1=ALU.add),
                     reads=["S_f", "dec", "W1"], writes=["S_f"])

        kv_update(0)
        if full:
            S.op("act", lambda e: e.activation(out=Sb1[:], in_=S_f[:], func=AF.Copy), reads=["S_f"], writes=["Sb1"])
            W0v = W0.rearrange("p (h v) -> p h v", h=4)
            for h in range(4):
                S.op("pe", lambda e, h=h: e.matmul(W0v[:, h, :], lhsT=ATm[:, h, :], rhs=v_bf[:, h * 256:(h + 1) * 256], start=True, stop=False),
                     reads=["ATm", "v_bf"], writes=["W0a" if h < 2 else "W0b"])
                S.op("pe", lambda e, h=h: e.matmul(W0v[:, h, :], lhsT=q0[:, h, :], rhs=Sb0[:, h, :], start=False, stop=False),
                     reads=["q0", "Sb0"], writes=["W0a" if h < 2 else "W0b"])
                S.op("pe", lambda e, h=h: e.matmul(W0v[:, h, :], lhsT=q1[:, h, :], rhs=Sb1[:, h, :], start=False, stop=True),
                     reads=["q1", "Sb1"], writes=["W0a" if h < 2 else "W0b"])
        kv_update(1)
        if not full:
            return
        S.op("act", lambda e: e.activation(out=Sb0[:], in_=S_f[:], func=AF.Copy), reads=["S_f"], writes=["Sb0"])
        t1v = t1_[:].rearrange("p (h v) -> p h v", h=4)
        t2v = t2_[:].rearrange("p (h v) -> p h v", h=4)
        for h in range(4):
            S.op("act", lambda e, h=h: e.activation(out=t1v[:, h, :], in_=W0v[:, h, :], func=AF.Square, accum_out=ss[:, 2 + h:3 + h]),
                 reads=["W0a" if h < 2 else "W0b"], writes=["F1", "ss"])
        S.op("dve", lambda e: e.tensor_scalar(out=ss[:, 2:6], in0=ss[:, 2:6], scalar1=1.0 / 256.0, scalar2=EPS, op0=ALU.mult, op1=ALU.add),
             reads=["ss"], writes=["ss"])
        S.op("pool", lambda e: e.tensor_tensor(out=rstd[:, 2:6], in0=ss[:, 2:6], in1=neghalf[:, 0:4], op=ALU.pow),
             reads=["ss", "neghalf"], writes=["rstd"])
        proj_tm(W1[:, 0:512], ["W1"], wr, "wr", 0, 512)
        proj_tm(W1[:, 512:1024], ["W1"], wr, "wr", 512, 512)
        S.op("act", lambda e: e.activation(out=t2_[:], in_=W1, func=AF.Sigmoid), reads=["W1"], writes=["F0"])
        S.op("dve", lambda e: e.tensor_tensor(out=t2_[:], in0=W1, in1=t2_[:], op=ALU.mult), reads=["W1", "F0"], writes=["F0"])
        S.op("pool", lambda e: e.tensor_tensor(out=t2_[:], in0=t2_[:], in1=gnwB[:], op=ALU.mult), reads=["F0", "gnwB"], writes=["F0"])
        for h in range(4):
            S.op("dve", lambda e, h=h: e.scalar_tensor_tensor(out=t1v[:, h, :], in0=W0v[:, h, :], scalar=rstd[:, 2 + h:3 + h],
                                                              in1=t2v[:, h, :], op0=ALU.mult, op1=ALU.mult),
                 reads=["W0a" if h < 2 else "W0b", "rstd", "F0"], writes=["F1"])
        proj_tm(W1[:, 0:512], ["W1"], wga, "wga", 0, 512)
        proj_tm(W1[:, 512:1024], ["W1"], wga, "wga", 512, 512)
        S.op("act", lambda e: e.activation(out=t2_[:], in_=W1, func=AF.Sigmoid), reads=["W1"], writes=["F0"])
        S.op("pool", lambda e: e.tensor_tensor(out=t1_[:], in0=t1_[:], in1=t2_[:], op=ALU.mult), reads=["F1", "F0"], writes=["F1"])
        capB = []
        S.cap = capB
        proj_tm(W2a, ["W2a"], wsu, "wsu", 0, 512)
        proj_tm(W2b, ["W2b"], wsu, "wsu", 512, 512)
        S.op("act", lambda e: e.activation(out=u_[:], in_=W2, func=AF.Gelu), reads=kW2, writes=["F2"])
        proj_tm(W2a, ["W2a"], wsv, "wsv", 0, 512)
        proj_tm(W2b, ["W2b"], wsv, "wsv", 512, 512)
        S.op("act", lambda e: e.activation(out=gv_[:], in_=W2, func=AF.Gelu), reads=kW2, writes=["F3"])
        for hf in range(2):
            S.op("dve", lambda e, hf=hf: e.bn_stats(out=bnst[:, hf, :], in_=gv_[:, hf * 512:(hf + 1) * 512]), reads=["F3"], writes=["bnst"])
        S.op("dve", lambda e: e.bn_aggr(out=mv[:], in_=bnst[:].rearrange("p a s -> p (a s)")), reads=["bnst"], writes=["mv"])
        S.op("dve", lambda e: e.tensor_scalar(out=ss[:, 6:7], in0=mv[:, 1:2], scalar1=EPS, scalar2=None, op0=ALU.add),
             reads=["mv"], writes=["ss_s"])
        S.op("pool", lambda e: e.tensor_tensor(out=rstd[:, 6:7], in0=ss[:, 6:7], in1=neghalf[:, 0:1], op=ALU.pow),
             reads=["ss_s", "neghalf"], writes=["rstd_s"])
        S.op("dve", lambda e: e.tensor_scalar(out=gv_[:], in0=gv_[:], scalar1=mv[:, 0:1], scalar2=rstd[:, 6:7], op0=ALU.subtract, op1=ALU.mult),
             reads=["F3", "mv", "rstd_s"], writes=["F3"])
        S.op("pool", lambda e: e.tensor_tensor(out=gv_[:], in0=gv_[:], in1=lnwB[:], op=ALU.mult), reads=["F3", "lnwB"], writes=["F3"])
        S.op("pool", lambda e: e.tensor_tensor(out=vln[:], in0=gv_[:], in1=lnbB[:], op=ALU.add), reads=["F3", "lnbB"], writes=["vln"])
        proj_tm(W2a, ["W2a"], wgb, "wgb", 0, 512)
        proj_tm(W2b, ["W2b"], wgb, "wgb", 512, 512)
        S.op("act", lambda e: e.activation(out=gv_[:], in_=W2, func=AF.Sigmoid), reads=kW2, writes=["F3"])
        for g in range(8):
            S.op("pe", lambda e, g=g: e.matmul(W2[:, g * 128:(g + 1) * 128], lhsT=WsT[:, g, :], rhs=vln[:, g * 128:(g + 1) * 128], start=True, stop=True),
                 reads=["WsT", "vln"], writes=kW2)
        for g in range(8):
            S.op("dve", lambda e, g=g: e.scalar_tensor_tensor(out=u_[:, g * 128:(g + 1) * 128], in0=W2[:, g * 128:(g + 1) * 128],
                                                              scalar=Bsgu[:, g:g + 1], in1=u_[:, g * 128:(g + 1) * 128], op0=ALU.add, op1=ALU.mult),
                 reads=kW2 + ["Bsgu", "F2"], writes=["F2"])
        S.op("pool", lambda e: e.tensor_tensor(out=u_[:], in0=u_[:], in1=gv_[:], op=ALU.mult), reads=["F2", "F3"], writes=["F2"])
        S.cap = None
        ia = ib = 0
        na, nb = len(capA), len(capB)
        while ia < na or ib < nb:
            if ib >= nb or (ia < na and ia * nb <= ib * na):
                S.op(*capA[ia]); ia += 1
            else:
                S.op(*capB[ib]); ib += 1
        S.op("dve", lambda e: e.tensor_tensor(out=vln[:], in0=t1_[:], in1=u_[:], op=ALU.add), reads=["F1", "F2"], writes=["vln"])

    def p2_tail(k):
        xk = xts[k % 2]
        kx = "xt%d" % (k % 2)
        row0 = k * 128
        PAb = PA.bitcast(BF16)
        for ch in range(8):
            S.op("pe", lambda e, ch=ch: e.transpose(PAb[:, ch * 128:(ch + 1) * 128], vln[:, ch * 128:(ch + 1) * 128], ident_b[:]),
                 reads=["vln", "ident_b"], writes=["PA"])
        S.op("act", lambda e: e.activation(out=mTb[:].rearrange("p c t -> p (c t)"), in_=PAb[:, 0:1024], func=AF.Copy), reads=["PA"], writes=["mT"])
        for hf in range(2):
            for ch in range(8):
                S.op("pe", lambda e, hf=hf, ch=ch: e.matmul(W1[:, hf * 512:(hf + 1) * 512], lhsT=mTb[:, ch, :], rhs=woutg[:, ch, hf * 512:(hf + 1) * 512],
                                                            start=(ch == 0), stop=(ch == 7)),
                     reads=["mT", "woutg"], writes=["W1"])
        S.op("dve", lambda e: e.tensor_tensor(out=xk[:], in0=W1, in1=xk[:], op=ALU.add), reads=["W1", kx], writes=[kx])
        S.op("sp", lambda e: e.dma_start(out=out_d[row0:row0 + 128, :], in_=xk[:]), reads=[kx], writes=[("out_d", row0 // 128)], dma_chan="c_out%d" % (k % 2))

    load_late()
    alrT1 = alloc("alrT1", [16, 128], F32)
    dec1 = alloc("dec1", [128, 4, 2], F32)
    ss1 = alloc("ss1", [128, 8], F32)
    rstd1 = alloc("rstd1", [128, 8], F32)
    F3h = Fs[3]
    p1buf = [
        dict(xt=xt[:], xn=Fs[0][:], hT=hT, v=v_bf[:], alrT=alrT, bufE=bufE[:], lbuf=lbuf[:], ktail=ktail[:], dec=dec, ss=ss, rstd=rstd,
             Ww=P01[:, :], Wx=P23[:, :], Bc=P23[:, 0:512], Bd=P23[:, 512:1024]),
        dict(xt=Fs[1][:], xn=Fs[2][:], hT=vln[:].rearrange("p (c t) -> p c t", c=8), v=F3h[:, 0:512].bitcast(BF16), alrT=alrT1, bufE=F3h[:, 512:1024],
             lbuf=expnG[:], ktail=qd[:].rearrange("p h t -> p (h t)"), dec=dec1, ss=ss1, rstd=rstd1,
             Ww=P45[:, :], Wx=P67[:, :], Bc=P67[:, 0:512], Bd=P67[:, 512:1024]),
    ]

    def p1_stages(x_ap, seg, sl):
        B = p1buf[sl]
        K = lambda n: "%s_%d" % (n, sl)
        fcol = flags[:, seg:seg + 1]
        xt_, xn_, hT_, v_, alrT_, bufE_, lbuf_, ktail_, dec_, ss_, rstd_ = (B[k] for k in ("xt", "xn", "hT", "v", "alrT", "bufE", "lbuf", "ktail", "dec", "ss", "rstd"))
        Ww, Wx, Bc, Bd = B["Ww"], B["Wx"], B["Bc"], B["Bd"]
        st = []

        def s0():
            S.op("sp", lambda e: e.dma_start(out=xt_, in_=x_ap), writes=[K("xt")], dma_chan="c_p1xt%d" % sl)
            S.op("act", lambda e: e.activation(out=xn_, in_=xt_, func=AF.Square, accum_out=ss_[:, 0:1]), reads=[K("xt")], writes=[K("xn"), K("ss")])
            S.op("pool", lambda e: e.tensor_scalar(out=ss_[:, 1:2], in0=ss_[:, 0:1], scalar1=1.0 / D, scalar2=EPS, op0=ALU.mult, op1=ALU.add),
                 reads=[K("ss")], writes=[K("ssb")])
            S.op("pool", lambda e: e.tensor_tensor(out=rstd_[:, 0:1], in0=ss_[:, 1:2], in1=neghalf[:, 0:1], op=ALU.pow), reads=[K("ssb"), "neghalf"], writes=[K("rstd")])
        st.append(s0)

        def s1():
            S.op("act", lambda e: e.activation(out=xn_, in_=xt_, func=AF.Copy, scale=rstd_[:, 0:1]), reads=[K("xt"), K("rstd")], writes=[K("xn")])
            for ch in range(8):
                S.op("pe", lambda e, ch=ch: e.transpose(Ww[:, ch * 128:(ch + 1) * 128], xn_[:, ch * 128:(ch + 1) * 128], ident_f[:]),
                     reads=[K("xn"), "ident_f"], writes=[K("Ww")])
        st.append(s1)

        def s2():
            for ch in range(8):
                eng = "dve" if ch < 4 else "act"
                if eng == "dve":
                    S.op("dve", lambda e, ch=ch: e.tensor_scalar(out=hT_[:, ch, :], in0=Ww[:, ch * 128:(ch + 1) * 128],
                                                                 scalar1=cols[:, 0, ch:ch + 1], scalar2=cols[:, 1, ch:ch + 1], op0=ALU.mult, op1=ALU.add),
                         reads=[K("Ww"), "cols"], writes=[(K("hT"), ch)])
                else:
                    S.op("act", lambda e, ch=ch: e.activation(out=hT_[:, ch, :], in_=Ww[:, ch * 128:(ch + 1) * 128], func=AF.Identity,
                                                              scale=cols[:, 0, ch:ch + 1], bias=cols[:, 1, ch:ch + 1]),
                         reads=[K("Ww"), "cols"], writes=[(K("hT"), ch)])
        st.append(s2)

        def s3():
            for ch in range(8):
                S.op("pe", lambda e, ch=ch: e.matmul(Bc, lhsT=hT_[:, ch, :], rhs=wk[:, ch, :], start=(ch == 0), stop=(ch == 7)), reads=[(K("hT"), ch), "wk"], writes=[K("Bc")])
            for ch in range(8):
                S.op("pe", lambda e, ch=ch: e.matmul(Bd[0:16, 0:128], lhsT=walr[:, ch, :], rhs=hT_[:, ch, :], start=(ch == 0), stop=(ch == 7)),
                     reads=[(K("hT"), ch), "walr"], writes=[K("Bd")])
            for hf in range(2):
                for ch in range(8):
                    S.op("pe", lambda e, ch=ch, hf=hf: e.matmul(Ww[:, hf * 512:(hf + 1) * 512], lhsT=hT_[:, ch, :], rhs=wv[:, ch, hf * 512:(hf + 1) * 512],
                                                               start=(ch == 0), stop=(ch == 7)),
                         reads=[(K("hT"), ch), "wv"], writes=[K("Ww")])
            S.op("act", lambda e: e.activation(out=alrT_[:], in_=Bd[0:16, 0:128], func=AF.Copy), reads=[K("Bd")], writes=[K("alrT")])
        st.append(s3)

        def s4():
            S.op("pe", lambda e: e.matmul(Bd, lhsT=alrT_[:], rhs=wup_f[:], start=True, stop=False), reads=[K("alrT"), "wup_f"], writes=[K("Bd")])
            S.op("pe", lambda e: e.matmul(Bd, lhsT=ones_f[0:1, :], rhs=balpha_f[:], start=False, stop=True), reads=["ones_f", "balpha_f"], writes=[K("Bd")])
            S.op("dve", lambda e: e.tensor_scalar(out=v_, in0=Ww, scalar1=fcol, scalar2=None, op0=ALU.mult), reads=[K("Ww"), "flags"], writes=[K("v")])
            S.op("act", lambda e: e.activation(out=bufE_, in_=Bd, func=AF.Exp, scale=-1.0), reads=[K("Bd")], writes=[K("bufE")])
        st.append(s4)

        def s5():
            S.op("act", lambda e: e.activation(out=lbuf_, in_=bufE_, func=AF.Ln, bias=1.0, scale=1.0), reads=[K("bufE")], writes=[K("lbuf")])
            S.op("pe", lambda e: e.matmul(Bd, lhsT=Rm[:], rhs=lbuf_, start=True, stop=True), reads=["Rm", K("lbuf")], writes=[K("Bd")])
            for h in range(4):
                S.op("pe", lambda e, h=h: e.matmul(Ww[:, h * 128:(h + 1) * 128], lhsT=lbuf_[:, h * 128:(h + 1) * 128], rhs=Lm[:], start=True, stop=True),
                     reads=[K("lbuf"), "Lm", K("v")], writes=[K("Ww")])
        st.append(s5)

        def s6():
            S.op("act", lambda e: e.activation(out=bufE_, in_=Bd, func=AF.Exp), reads=[K("Bd")], writes=[K("bufE")])
            Wv4 = Ww[:, 0:512].rearrange("p (h t) -> p h t", h=4)
            S.op("act", lambda e: e.activation(out=dec_[:], in_=Wv4[:, :, 63:128:64], func=AF.Exp), reads=[K("Ww")], writes=[K("dec")])
            S.op("dve", lambda e: e.tensor_tensor(out=ktail_, in0=Bc, in1=bufE_, op=ALU.mult), reads=[K("Bc"), K("bufE")], writes=[K("ktail")])
        st.append(s6)

        def kv(c, Wdst, wkey):
            Wd = Wdst.rearrange("p (h v) -> p h v", h=4)
            for h in range(4):
                S.op("pe", lambda e, h=h: e.matmul(Wd[:, h, :], lhsT=ktail_[c * 64:(c + 1) * 64, h * 128:(h + 1) * 128],
                                                   rhs=v_[c * 64:(c + 1) * 64, h * 256:(h + 1) * 256], start=True, stop=True),
                     reads=[K("ktail"), K("v")] + ([K("dec")] if wkey == "Ww" else []), writes=[K(wkey)] if wkey != "Wx" else [K("Bc"), K("Bd")])
            for h in range(4):
                S.op("dve", lambda e, h=h: e.scalar_tensor_tensor(out=S_f[:, h, :], in0=S_f[:, h, :], scalar=dec_[:, h, c:c + 1],
                                                                  in1=Wd[:, h, :], op0=ALU.mult, op1=ALU.add),
                     reads=[("S_f", h), K("dec")] + ([K(wkey)] if wkey != "Wx" else [K("Bc"), K("Bd")]), writes=[("S_f", h)])
        st.append(lambda: kv(0, Wx, "Wx"))
        st.append(lambda: kv(1, Ww, "Ww"))
        return st

    tiles = []
    for seg in range(3):
        for ti in range(NT if stage not in (1, 3, 4) else NDBG):
            tiles.append((xpre[seg, ti * 128:(ti + 1) * 128, :], seg))
    SKW = 4
    stg = [p1_stages(x_ap, seg, k % 2) for k, (x_ap, seg) in enumerate(tiles)]
    nst = len(stg[0])
    for step in range(len(tiles) * SKW + nst):
        for k in range(len(tiles)):
            sidx = step - k * SKW
            if 0 <= sidx < nst:
                stg[k][sidx]()
        if step % 2 == 0:
            issue_cvt(1)
    S.barrier()
    S.op("act", lambda e: e.activation(out=Sb0[:], in_=S_f[:], func=AF.Copy), writes=["Sb0"])
    issue_cvt(1000)
    NT2 = NT if stage not in (1, 3, 4) else NDBG
    p2_front(0)
    for ti in range(NT2):
        p2_body(ti)
        if ti + 1 < NT2:
            p2_front(ti + 1)
        p2_tail(ti)

    if stage <= 2:
        S.emit(final_waits=["c_out0", "c_out1"])
        return nc

    S.barrier()
    M.off = mark_p3
    h2T = alloc("h2T", [128, 8, 2048], BF16)
    idx1T = alloc("idx1T", [128, 2048], BF16)
    idx2T = alloc("idx2T", [128, 2048], BF16)
    gT = alloc("gT", [128, 2048], BF16)
    gate2B = alloc("gate2B", [128, D], F32)
    finwB = alloc("finwB", [128, D], F32)
    mark_p3t = M.off
    wqb = alloc("wqb", [128, 8, 2048], BF16)
    k1T = alloc("k1T", [128, 128], BF16)
    k2T = alloc("k2T", [128, 128], BF16)
    xt2 = alloc("xt2", [128, D], F32)
    xn2 = alloc("xn2", [128, D], F32)
    qT = alloc("qT", [128, 16, 128], BF16)
    sc = alloc("sc", [128, 16, 128], F32)
    work = alloc("work", [128, 256], F32)
    vtop = alloc("vtop", [128, 16, 16], F32)
    iu = alloc("iu", [128, 16, 16], U32)
    itf = alloc("itf", [128, 16, 16], F32)
    cand = alloc("cand", [128, 8, 256], F32)
    ts = alloc("ts", [128, 8, 16], F32)
    posu = alloc("posu", [128, 8, 16], U32)
    k1u = alloc("k1u", [128, 8, 16], U32)
    k2u = alloc("k2u", [128, 8, 16], U32)
    k1f = alloc("k1f", [128, 8, 16], F32)
    k2f = alloc("k2f", [128, 8, 16], F32)
    ee = alloc("ee", [128, 8, 16], F32)
    zz = alloc("zz", [128, 8], F32)
    oh = alloc("oh", [128, 128, 16], F32)
    idx_tm = alloc("idx_tm", [128, 3, 128], F32)
    iota16 = alloc("iota16", [128, 16], F32)
    diag = alloc("diag", [128, 128], F32)
    ss2 = alloc("ss2", [128, 4], F32)
    rstd2 = alloc("rstd2", [128, 4], F32)

    S.op("act", lambda e: e.dma_start(out=finwB[:], in_=rowv_d[0:1, :].to_broadcast([128, D])), writes=["finwB"], dma_chan="c_finw")
    wq_v = wq_d.rearrange("(c p) e -> p c e", p=128)
    for ch in range(8):
        S.op("pool", lambda e, ch=ch: e.dma_start(out=wqb[:, ch, :], in_=wq_v[:, ch, :]), writes=["wqb"], dma_chan="c_wqb")
    S.op("pool", lambda e: e.dma_start(out=k1T[:], in_=k1T_d[:, :]), writes=["k1T"], dma_chan="c_k1T")
    S.op("pool", lambda e: e.dma_start(out=k2T[:], in_=k2T_d[:, :]), writes=["k2T"], dma_chan="c_k2T")
    S.op("dve", lambda e: e.tensor_copy(out=iota16[:], in_=iota_f[:, 0:16]), reads=["iota_f"], writes=["iota16"])
    for ch in range(8):
        S.op("dve", lambda e, ch=ch: e.tensor_scalar(out=diag[:], in0=ident_f[:], scalar1=cols[:, 4, ch:ch + 1], scalar2=None, op0=ALU.mult),
             reads=["ident_f", "cols"], writes=["diag"])
        S.op("pe", lambda e: e.matmul(PA[:, 0:128], lhsT=ones_f[:], rhs=diag[:], start=True, stop=True), reads=["ones_f", "diag"], writes=["PA"])
        S.op("act", lambda e, ch=ch: e.activation(out=gate2B[:, ch * 128:(ch + 1) * 128], in_=PA[:, 0:128], func=AF.Copy), reads=["PA"], writes=["gate2B"])

    W01 = [W0, W1]
    xt2b = [xt2, alloc("xt2b", [128, D], F32)]
    xn2b = [xn2, alloc("xn2b", [128, D], F32)]
    qTb = [qT, alloc("qTb", [128, 16, 128], BF16)]
    scb = [sc, alloc("scb", [128, 16, 128], F32)]
    work16 = alloc("work16", [128, 16, 128], F32)
    oh2 = alloc("oh2", [128, 128, 16], F32)
    ohs = [oh, oh2]
    Ireps = [alloc("Irep%d" % i, [128, 16, 128], F32) for i in range(2)]
    NT25 = NT if stage != 3 else NDBG

    def front(ti):
        p = ti % 2
        xt2_, xn2_, qT_, sc_ = xt2b[p], xn2b[p], qTb[p], scb[p]
        kx, kn, kq, ks = "xt2_%d" % p, "xn2_%d" % p, "qT_%d" % p, "sc_%d" % p
        S.op("sp", lambda e: e.dma_start(out=xt2_[:], in_=out_d[ti * 128:(ti + 1) * 128, :]), reads=[("out_d", ti)], writes=[kx], dma_chan="c_xt2_%d" % p)
        S.op("act", lambda e: e.activation(out=xn2_[:], in_=xt2_[:], func=AF.Square, accum_out=ss2[:, p:p + 1]), reads=[kx], writes=[kn, ("ss2", p)])
        S.op("pool", lambda e: e.tensor_scalar(out=ss2[:, 2 + p:3 + p], in0=ss2[:, p:p + 1], scalar1=1.0 / D, scalar2=EPS, op0=ALU.mult, op1=ALU.add),
             reads=[("ss2", p)], writes=[("ss2b", p)])
        S.op("pool", lambda e: e.tensor_tensor(out=rstd2[:, p:p + 1], in0=ss2[:, 2 + p:3 + p], in1=neghalf[:, 0:1], op=ALU.pow), reads=[("ss2b", p), "neghalf"], writes=[("rstd2", p)])
        S.op("act", lambda e: e.activation(out=xn2_[:], in_=xt2_[:], func=AF.Copy, scale=rstd2[:, p:p + 1]), reads=[kx, ("rstd2", p)], writes=[kn])
        for ch in range(8):
            S.op("pe", lambda e, ch=ch: e.transpose(W2[:, ch * 128:(ch + 1) * 128], xn2_[:, ch * 128:(ch + 1) * 128], ident_f[:]),
                 reads=[kn, "ident_f"], writes=kW2)
        for ch in range(8):
            S.op("act", lambda e, ch=ch: e.activation(out=h2T[:, ch, ti * 128:(ti + 1) * 128], in_=W2[:, ch * 128:(ch + 1) * 128], func=AF.Identity,
                                                      scale=cols[:, 2, ch:ch + 1], bias=cols[:, 3, ch:ch + 1]),
                 reads=[kW2[ch // 4], "cols"], writes=[("h2T", ti, ch)])
        for blk in range(16):
            dst = W01[blk // 8][:, (blk % 8) * 128:(blk % 8 + 1) * 128]
            for ch in range(8):
                S.op("pe", lambda e, blk=blk, ch=ch, dst=dst: e.matmul(dst, lhsT=wqb[:, ch, blk * 128:(blk + 1) * 128], rhs=h2T[:, ch, ti * 128:(ti + 1) * 128],
                                                                      start=(ch == 0), stop=(ch == 7)),
                     reads=["wqb", ("h2T", ti, ch)], writes=["W%d" % (blk // 8)])
        qTf = qT_[:].rearrange("p b t -> p (b t)")
        S.op("act", lambda e: e.activation(out=qTf[:, 0:1024], in_=W0, func=AF.Copy), reads=["W0"], writes=[kq])
        S.op("act", lambda e: e.activation(out=qTf[:, 1024:2048], in_=W1, func=AF.Copy), reads=["W1"], writes=[kq])
        for blk in range(16):
            dst = W01[blk // 8][:, (blk % 8) * 128:(blk % 8 + 1) * 128]
            kT_ = k1T if blk % 2 == 0 else k2T
            S.op("pe", lambda e, blk=blk, dst=dst, kT_=kT_: e.matmul(dst, lhsT=qT_[:, blk, :], rhs=kT_[:], start=True, stop=True),
                 reads=[kq, "k1T", "k2T"], writes=["W%d" % (blk // 8)])
        scf = sc_[:].rearrange("p b k -> p (b k)")
        S.op("act", lambda e: e.activation(out=scf[:, 0:1024], in_=W0, func=AF.Copy), reads=["W0"], writes=[ks])
        S.op("act", lambda e: e.activation(out=scf[:, 1024:2048], in_=W1, func=AF.Copy), reads=["W1"], writes=[ks])

    def top16_multi(items):
        for (src, sk, n, vd, idd, wk_, tg) in items:
            S.op("dve", lambda e, src=src, vd=vd: e.max(out=vd[:, 0:8], in_=src), reads=[sk], writes=[("vt", tg)])
        for (src, sk, n, vd, idd, wk_, tg) in items:
            S.op("dve", lambda e, src=src, vd=vd, idd=idd: e.max_index(out=idd[:, 0:8], in_max=vd[:, 0:8], in_values=src), reads=[sk, ("vt", tg)], writes=[("it", tg)])
        for (src, sk, n, vd, idd, wk_, tg) in items:
            S.op("dve", lambda e, src=src, vd=vd, wk_=wk_: e.match_replace(out=wk_, in_to_replace=vd[:, 0:8], in_values=src, imm_value=-1e30),
                 reads=[sk, ("vt", tg)], writes=[("work", tg)])
        for (src, sk, n, vd, idd, wk_, tg) in items:
            S.op("dve", lambda e, vd=vd, wk_=wk_: e.max(out=vd[:, 8:16], in_=wk_), reads=[("work", tg)], writes=[("vt", tg)])
        for (src, sk, n, vd, idd, wk_, tg) in items:
            S.op("dve", lambda e, vd=vd, idd=idd, wk_=wk_: e.max_index(out=idd[:, 8:16], in_max=vd[:, 8:16], in_values=wk_), reads=[("work", tg), ("vt", tg)], writes=[("it", tg)])

    def back(ti, part):
        p = ti % 2
        sc_ = scb[p]
        ks = "sc_%d" % p
        vkeys = [("vt", t) for t in range(16)]
        ikeys_ = [("it", t) for t in range(16)]
        if part == 0:
            top16_multi([(sc_[:, blk, :], ks, 128, vtop[:, blk, :], iu[:, blk, :], work16[:, blk, :], blk) for blk in range(16)])
            S.op("dve", lambda e: e.tensor_copy(out=itf[:], in_=iu[:]), reads=ikeys_, writes=["itf"])
            for which in (0, 1):
                Irep = Ireps[which]
                for j in range(16):
                    S.op("act", lambda e, j=j, which=which, Irep=Irep: e.activation(out=Irep[:, j, :].rearrange("p (h k) -> p h k", h=8),
                                                                                    in_=itf[:, which::2, j:j + 1].to_broadcast([128, 8, 16]), func=AF.Copy),
                         reads=["itf"], writes=[("Irep", which, j)])
            return
        for h in range(8):
            cv = cand[:, h, :].rearrange("p (a b) -> p a b", a=16)
            S.op("dve", lambda e, h=h, cv=cv: e.tensor_tensor(out=cv, in0=vtop[:, 2 * h, :].unsqueeze(2).to_broadcast([128, 16, 16]),
                                                              in1=vtop[:, 2 * h + 1, :].unsqueeze(1).to_broadcast([128, 16, 16]), op=ALU.add),
                 reads=[("vt", 2 * h), ("vt", 2 * h + 1)], writes=[("cand", h)])
        w8 = work16[:].rearrange("p (h a) k -> p h (a k)", h=8)
        top16_multi([(cand[:, h, :], ("cand", h), 256, ts[:, h, :], posu[:, h, :], w8[:, h, :], 100 + h) for h in range(8)])
        tkeys = [("vt", 100 + h) for h in range(8)]
        pkeys = [("it", 100 + h) for h in range(8)]
        S.op("dve", lambda e: e.tensor_tensor(out=ee[:], in0=ts[:], in1=ts[:, :, 0:1].to_broadcast([128, 8, 16]), op=ALU.subtract),
             reads=tkeys, writes=["ee"])
        S.op("act", lambda e: e.activation(out=ee[:], in_=ee[:], func=AF.Exp), reads=["ee"], writes=["ee"])
        S.op("dve", lambda e: e.tensor_single_scalar(out=k1u[:], in_=posu[:], scalar=4, op=ALU.logical_shift_right), reads=pkeys, writes=["k1u"])
        S.op("dve", lambda e: e.tensor_single_scalar(out=k2u[:], in_=posu[:], scalar=15, op=ALU.bitwise_and), reads=pkeys, writes=["k2u"])
        S.op("dve", lambda e: e.tensor_copy(out=k1f[:], in_=k1u[:]), reads=["k1u"], writes=["k1f"])
        S.op("dve", lambda e: e.tensor_copy(out=k2f[:], in_=k2u[:]), reads=["k2u"], writes=["k2f"])
        for which, kf, kfk in ((0, k1f, "k1f"), (1, k2f, "k2f")):
            Irep = Ireps[which]
            pr = ohs[which][:].rearrange("p a b -> p (a b)").rearrange("p (j m) -> p j m", j=16)
            kff = kf[:].rearrange("p h k -> p (h k)")
            for j in range(16):
                S.op("dve", lambda e, j=j, pr=pr, kff=kff, Irep=Irep: e.scalar_tensor_tensor(out=pr[:, j, :], in0=kff, scalar=float(j), in1=Irep[:, j, :],
                                                                                             op0=ALU.is_equal, op1=ALU.mult),
                     reads=[kfk, ("Irep", which, j)], writes=[("pr", which, j)])
            S.op("dve", lambda e, pr=pr: e.tensor_tensor(out=pr[:, 0:8, :], in0=pr[:, 0:8, :], in1=pr[:, 8:16, :], op=ALU.add),
                 reads=[("pr", which, j) for j in range(16)], writes=[("prs", which)])
            S.op("dve", lambda e, pr=pr: e.tensor_tensor(out=pr[:, 0:4, :], in0=pr[:, 0:4, :], in1=pr[:, 4:8, :], op=ALU.add),
                 reads=[("prs", which)], writes=[("prs", which)])
            S.op("dve", lambda e, pr=pr: e.tensor_tensor(out=pr[:, 0:2, :], in0=pr[:, 0:2, :], in1=pr[:, 2:4, :], op=ALU.add),
                 reads=[("prs", which)], writes=[("prs", which)])
            S.op("dve", lambda e, pr=pr, which=which: e.tensor_tensor(out=idx_tm[:, which, :], in0=pr[:, 0, :], in1=pr[:, 1, :], op=ALU.add),
                 reads=[("prs", which)], writes=[("idx_tm", which)])
        S.op("dve", lambda e: e.tensor_reduce(out=zz[:], in_=ee[:], axis=AX.X, op=ALU.add), reads=["ee"], writes=["zz"])
        S.op("dve", lambda e: e.reciprocal(out=zz[:], in_=zz[:]), reads=["zz"], writes=["zz"])
        S.op("dve", lambda e: e.tensor_tensor(out=idx_tm[:, 2, :].rearrange("p (h k) -> p h k", h=8), in0=ee[:],
                                              in1=zz[:].unsqueeze(2).to_broadcast([128, 8, 16]), op=ALU.mult),
             reads=["ee", "zz"], writes=[("idx_tm", 2)])
        for a, dstT in ((0, idx1T), (1, idx2T), (2, gT)):
            S.op("pe", lambda e, a=a: e.transpose(PA[:, a * 128:(a + 1) * 128], idx_tm[:, a, :], ident_f[:]), reads=[("idx_tm", a), "ident_f"], writes=["PA"])
        for a, dstT in ((0, idx1T), (1, idx2T), (2, gT)):
            S.op("act", lambda e, a=a, dstT=dstT: e.activation(out=dstT[:, ti * 128:(ti + 1) * 128], in_=PA[:, a * 128:(a + 1) * 128], func=AF.Copy),
                 reads=["PA"], writes=[("idxT", ti)])

    front(0)
    for ti in range(NT25):
        back(ti, 0)
        if ti + 1 < NT25:
            front(ti + 1)
        back(ti, 1)

    if stage == 3:
        dbg = dram("dbg", [128, 3, 128 * NDBG], kind="ExternalOutput")
        for a, dstT in ((0, idx1T), (1, idx2T), (2, gT)):
            S.op("sp", lambda e, a=a, dstT=dstT: e.dma_start(out=dbg[:, a, :], in_=dstT[:, 0:128 * NDBG]), reads=[("idxT", t) for t in range(NDBG)], writes=["dbg"], dma_chan="c_dbg")
        S.emit(final_waits=["c_out0", "c_out1", "c_dbg"])
        return nc

    S.barrier()
    M.off = mark_p3t
    TT = 256
    NB = 4
    G = alloc("G", [128, 128, TT], BF16)
    NBUF = 3
    dbuf = [alloc("dbuf%d" % i, [128, NB, 8, 128], BF16) for i in range(NBUF)]
    ubuf = [alloc("ubuf%d" % i, [128, NB, D], BF16) for i in range(NBUF)]
    SBT = 8
    p2oh = [alloc("p2oh%d" % i, [128, SBT, 128], BF16) for i in range(2)]
    p1t = [alloc("p1t%d" % i, [128, SBT, 128], BF16) for i in range(2)]
    p1w = [alloc("p1w%d" % i, [128, SBT, 128], BF16) for i in range(2)]
    Ab = [alloc("Ab%d" % i, [128, TT], BF16) for i in range(4)]
    Wb = [alloc("Wb%d" % i, [128, TT], BF16) for i in range(4)]
    x3 = [alloc("x3_%d" % i, [128, D], F32) for i in range(2)]
    y3 = [alloc("y3_%d" % i, [128, D], F32) for i in range(2)]
    ss3 = alloc("ss3", [128, 4], F32)
    rstd3 = alloc("rstd3", [128, 4], F32)
    print("SBUF used (P3):", M.off)
    PAB = [PA, PB]
    nwd = 0
    NT3 = 2048 // TT if stage != 4 else 1

    def g_dve(Tt, sbi):
        ts0_ = Tt * TT + sbi * SBT
        q_ = sbi % 2
        p2o_, p1w2_ = p2oh[q_], p1w[q_]
        ik_ = [("idxT", t) for t in range(2 * Tt, 2 * Tt + 2)]
        for tl in range(SBT):
            tk = ts0_ + tl
            S.op("dve", lambda e, tk=tk, tl=tl, p2o_=p2o_: e.tensor_scalar(out=p2o_[:, tl, :], in0=iota_b[:], scalar1=idx2T[:, tk:tk + 1], scalar2=None, op0=ALU.is_equal),
                 reads=ik_ + ["iota_b"], writes=[("p2oh", q_, tl)])
            S.op("dve", lambda e, tk=tk, tl=tl, p1w2_=p1w2_: e.tensor_scalar(out=p1w2_[:, tl, :], in0=iota_b[:], scalar1=idx1T[:, tk:tk + 1], scalar2=gT[:, tk:tk + 1],
                                                                           op0=ALU.is_equal, op1=ALU.mult),
                 reads=ik_ + ["iota_b"], writes=[("p1w", q_, tl)])

    for T in range(NT3):
        t0 = T * TT
        ikeys = [("idxT", t) for t in range(2 * T, 2 * T + 2)]
        W2h = [W2a, W2b]
        for sbi in range(TT // SBT):
            ts0 = t0 + sbi * SBT
            q = sbi % 2
            p2o, p1t_, p1w_ = p2oh[q], p1t[q], p1w[q]
            if not (T > 0 and sbi < 2):
                g_dve(T, sbi)
            for grp in range(SBT // 4):
                hb = (sbi * (SBT // 4) + grp) % 2
                for tl in range(4):
                    tloc = grp * 4 + tl
                    S.op("pe", lambda e, tl=tl, tloc=tloc, hb=hb, p1w_=p1w_, p2o=p2o: e.matmul(W2h[hb][:, tl * 128:(tl + 1) * 128], lhsT=p1w_[:, tloc, :], rhs=p2o[:, tloc, :], start=True, stop=True),
                         reads=[("p1w", q, tloc), ("p2oh", q, tloc)], writes=[kW2[hb]])
                tg = sbi * SBT + grp * 4
                S.op("act", lambda e, tg=tg, hb=hb: e.activation(out=G[:, :, tg:tg + 4],
                                                                 in_=W2h[hb].rearrange("p (t i) -> p i t", t=4), func=AF.Copy),
                     reads=[kW2[hb]], writes=["G"])
        SK = 2
        NSL = 4
        Aps = [PA[:, 0:TT], PB[:, 0:TT]]
        binfo = {}
        for step in range(128 + SK):
            if step < 128:
                i2 = step
                j = i2 % NB
                if j == 0:
                    b = nwd % NBUF
                    nwd += 1
                    db_, ub_ = dbuf[b], ubuf[b]
                    S.op("sp", lambda e, i2=i2, db_=db_: e.dma_start(out=db_[:], in_=scr_down[:, i2:i2 + NB, :, :]), writes=["dbuf%d" % b], dma_chan="c_dbuf%d" % b)
                    S.op("sp", lambda e, i2=i2, ub_=ub_: e.dma_start(out=ub_[:], in_=scr_up[:, i2:i2 + NB, :]), writes=["ubuf%d" % b], dma_chan="c_ubuf%d" % b)
                binfo[i2] = (b, ub_, j)
                pp = i2 % NSL
                pq = i2 % 2
                pa = Aps[pq]
                for ch in range(8):
                    S.op("pe", lambda e, ch=ch, j=j, db_=db_, pa=pa, t0=t0: e.matmul(pa, lhsT=db_[:, j, ch, :], rhs=h2T[:, ch, t0:t0 + TT], start=(ch == 0), stop=(ch == 7)),
                         reads=["dbuf%d" % b, ("h2T", 2 * T), ("h2T", 2 * T + 1)], writes=["PAB%d" % pq])
                ab, wb_ = Ab[pp], Wb[pp]
                S.op("act", lambda e, ab=ab, pa=pa: e.activation(out=ab[:], in_=pa, func=AF.Gelu), reads=["PAB%d" % pq], writes=["Ab%d" % pp])
                S.op("pool", lambda e, ab=ab, wb_=wb_, i2=i2: e.tensor_tensor(out=wb_[:], in0=ab[:], in1=G[:, i2, :], op=ALU.mult),
                     reads=["Ab%d" % pp, "G"], writes=["Wb%d" % pp])
            if step >= SK:
                i2 = step - SK
                b2, ub2, j2 = binfo[i2]
                pp = i2 % NSL
                wb_ = Wb[pp]
                for tt in range(2):
                    for hf in range(2):
                        S.op("pe", lambda e, tt=tt, hf=hf, wb_=wb_, ub2=ub2, j2=j2, i2=i2: e.matmul(W01[tt][:, hf * 512:(hf + 1) * 512], lhsT=wb_[:, tt * 128:(tt + 1) * 128],
                                                                                                  rhs=ub2[:, j2, hf * 512:(hf + 1) * 512], start=(i2 == 0), stop=(i2 == 127)),
                             reads=["Wb%d" % pp, "ubuf%d" % b2], writes=["W%d" % tt])
        if T + 1 < NT3:
            g_dve(T + 1, 0)
            g_dve(T + 1, 1)
        for tt in range(2):
            r0 = t0 + tt * 128
            okey = ("out_d", r0 // 128)
            xx, yy = x3[tt], y3[tt]
            S.op("sp", lambda e, r0=r0, xx=xx: e.dma_start(out=xx[:], in_=out_d[r0:r0 + 128, :]), reads=[okey], writes=["x3_%d" % tt], dma_chan="c_x3_%d" % tt)
            S.op("dve", lambda e, tt=tt, yy=yy: e.tensor_tensor(out=yy[:], in0=W01[tt], in1=gate2B[:], op=ALU.mult), reads=["W%d" % tt, "gate2B"], writes=["y3_%d" % tt])
            S.op("pool", lambda e, xx=xx, yy=yy: e.tensor_tensor(out=xx[:], in0=xx[:], in1=yy[:], op=ALU.add), reads=["x3_%d" % tt, "y3_%d" % tt], writes=["x3_%d" % tt])
            S.op("act", lambda e, xx=xx, yy=yy, tt=tt: e.activation(out=yy[:], in_=xx[:], func=AF.Square, accum_out=ss3[:, tt:tt + 1]),
                 reads=["x3_%d" % tt], writes=["y3_%d" % tt, "ss3"])
            S.op("dve", lambda e, tt=tt: e.tensor_scalar(out=ss3[:, 2 + tt:3 + tt], in0=ss3[:, tt:tt + 1], scalar1=1.0 / D, scalar2=EPS, op0=ALU.mult, op1=ALU.add),
                 reads=["ss3"], writes=["ss3"])
            S.op("pool", lambda e, tt=tt: e.tensor_tensor(out=rstd3[:, tt:tt + 1], in0=ss3[:, 2 + tt:3 + tt], in1=neghalf[:, 0:1], op=ALU.pow),
                 reads=["ss3", "neghalf"], writes=["rstd3"])
            S.op("dve", lambda e, xx=xx, yy=yy, tt=tt: e.scalar_tensor_tensor(out=yy[:], in0=xx[:], scalar=rstd3[:, tt:tt + 1], in1=finwB[:], op0=ALU.mult, op1=ALU.mult),
                 reads=["x3_%d" % tt, "rstd3", "finwB"], writes=["y3_%d" % tt])
            S.op("sp", lambda e, r0=r0, yy=yy: e.dma_start(out=out_d[r0:r0 + 128, :], in_=yy[:]), reads=["y3_%d" % tt], writes=[okey], dma_chan="c_fin%d" % tt)
    S.emit(final_waits=["c_out0", "c_out1", "c_fin0", "c_fin1"])
    return nc


def host_inputs(inputs):
    x = np.asarray(inputs["x"], np.float32)
    f = lambda k: np.asarray(inputs[k], np.float32)
    c = f("c")
    shared = {
        "w_ada": np.ascontiguousarray(f("w_ada")[0]),
        "b_ada": np.ascontiguousarray(f("b_ada")[0][None, :]),
        "w_in": np.ascontiguousarray(f("w_in")[0]),
        "w_alpha_up": np.ascontiguousarray(f("w_alpha_up")[0]),
        "b_alpha": np.ascontiguousarray(f("b_alpha")[0][None, :]),
        "sgu_wT": np.ascontiguousarray(f("sgu_w")[0].transpose(0, 2, 1)),
        "sgu_bT": np.ascontiguousarray(f("sgu_b")[0].T),
        "w_out": np.ascontiguousarray(f("w_out")[0]),
        "peer_w_q": np.ascontiguousarray(f("peer_w_q")[0]),
        "keys1T": np.ascontiguousarray(f("peer_keys1")[0].T),
        "keys2T": np.ascontiguousarray(f("peer_keys2")[0].T),
        "downT": np.ascontiguousarray(f("peer_down")[0].reshape(128, 128, 8, 128).transpose(3, 1, 2, 0)),
        "up": np.ascontiguousarray(f("peer_up")[0].reshape(128, 128, D)),
        "colv": np.ascontiguousarray(np.stack([f("norm1_w")[0].reshape(8, 128).T, f("norm2_w")[0].reshape(8, 128).T], axis=1)),
        "rowv": np.ascontiguousarray(np.stack([f("final_norm_w"), f("gla_norm_w")[0], f("sgu_ln_w")[0], f("sgu_ln_b")[0]], axis=0)),
    }
    maps = []
    for i in range(8):
        b, j = i // 4, i % 4
        xpre = np.zeros((3, 2048, D), np.float32)
        flags = np.zeros((128, 4), np.float32)
        flags[:, 3] = 1.0
        for s in range(3):
            qidx = s - (3 - j)
            if qidx >= 0:
                xpre[s] = x[b, qidx * 2048:(qidx + 1) * 2048]
                flags[:, s] = 1.0
        m = dict(shared)
        m["xpre"] = xpre
        m["xown"] = np.ascontiguousarray(x[b, j * 2048:(j + 1) * 2048])
        m["flags"] = flags
        m["cT"] = np.ascontiguousarray(c[b].reshape(8, 128).T)
        maps.append(m)
    return maps


_NC_CACHE = {}


def kernel(**inputs):
    maps = host_inputs(inputs)
    if "nc" not in _NC_CACHE:
        _NC_CACHE["nc"] = build()
    nc = _NC_CACHE["nc"]
    res = run_bass_kernel_spmd(nc, maps, core_ids=list(range(8)))
    out = np.zeros((2, 8192, D), np.float32)
    for i in range(8):
        b, j = i // 4, i % 4
        out[b, j * 2048:(j + 1) * 2048] = res.results[i]["out"]
    return out
```
